# Optimizing a Trainium2 kernel written in Bass

```python
import math
import jax, jax.numpy as jnp
from jax import lax
import numpy as np

D_MODEL = 1024
BATCH = 8
SEQ = 2048
DEPTH = 2
DEC_BATCH = 32
DEC_SEQ = 1
PAST_LEN = 16384
PAGE_SIZE = 128

D_PLE = 256
N_CONV_LAYERS = (DEPTH + 1) // 2
N_ATTN_LAYERS = DEPTH // 2
CONV_W = 31
N_HEADS = 16
HEAD_DIM = D_MODEL // N_HEADS
N_KV = 4
HPG = N_HEADS // N_KV
KV_W = N_KV * HEAD_DIM
Q_W = N_HEADS * HEAD_DIM
IN_COLS = Q_W + 6 * KV_W + 3 * N_HEADS
CMP_STRIDE = 16
CMP_LEN = 2 * CMP_STRIDE
CMP_HID = 2 * HEAD_DIM
SLC_BLOCK = 64
N_SELECT = 16
WINDOW = 512
NUM_BUCKETS = 32
MAX_DISTANCE = 128
D_FF = 4 * D_MODEL
EPS = 1e-6
SCALE = HEAD_DIM ** -0.5
WIN_QBLOCK = 128
SEL_QBLOCK = 32

kernel_name = 'hybrid_conformer_nsa_decode_step'


def rmsnorm(x, g):
    xf = x.astype(jnp.float32)
    y = xf * lax.rsqrt(jnp.mean(xf * xf, axis=-1, keepdims=True) + EPS)
    return (y * g.astype(jnp.float32)).astype(x.dtype)


def layernorm(x, g, b):
    xf = x.astype(jnp.float32)
    mu = jnp.mean(xf, axis=-1, keepdims=True)
    var = jnp.mean(jnp.square(xf - mu), axis=-1, keepdims=True)
    y = (xf - mu) * lax.rsqrt(var + EPS)
    return (y * g.astype(jnp.float32) + b.astype(jnp.float32)).astype(x.dtype)


def masked_softmax(s, mask):
    s = jnp.where(mask, s.astype(jnp.float32), -jnp.inf)
    m = jnp.max(s, axis=-1, keepdims=True)
    m = jnp.where(jnp.isfinite(m), m, 0.0)
    e = jnp.where(mask, jnp.exp(s - m), 0.0)
    return e / jnp.maximum(jnp.sum(e, axis=-1, keepdims=True), 1e-30)


def rel_bucket(dist):
    n = jnp.maximum(dist, 0)
    max_exact = NUM_BUCKETS // 2
    nf = jnp.maximum(n, 1).astype(jnp.float32)
    large = max_exact + (jnp.log(nf / max_exact) / math.log(MAX_DISTANCE / max_exact)
                         * (NUM_BUCKETS - max_exact)).astype(jnp.int32)
    large = jnp.minimum(large, NUM_BUCKETS - 1)
    return jnp.where(n < max_exact, n, large)


def head_bias(table, dist):
    b = table[rel_bucket(dist)].astype(jnp.float32)
    return b.reshape(dist.shape + (N_KV, HPG)).transpose(0, 2, 3, 1)


def compress(k, w1, pe, w2):
    b, t, g, hd = k.shape
    c = -(-t // CMP_STRIDE)
    k = jnp.pad(k, ((0, 0), (0, c * CMP_STRIDE - t), (0, 0), (0, 0)))
    kc = k.reshape(b, c, CMP_STRIDE, g, hd).transpose(0, 1, 3, 2, 4).reshape(b, c, g, CMP_STRIDE * hd)
    half = CMP_STRIDE * hd
    h = kc[:, :-1] @ w1[:half] + kc[:, 1:] @ w1[half:] + pe.reshape(-1) @ w1
    return jax.nn.silu(h) @ w2


def cmp_branch(qg, kcmp, vcmp, qpos, table):
    nc = kcmp.shape[1]
    ends = CMP_STRIDE * jnp.arange(nc, dtype=jnp.int32) + (CMP_LEN - 1)
    dist = qpos[:, None] - ends[None, :]
    s = jnp.einsum('btghd,bngd->btghn', qg, kcmp).astype(jnp.float32) * SCALE + head_bias(table, dist)
    mask = (ends[None, :] <= qpos[:, None])[:, None, None, :]
    p = masked_softmax(s, mask)
    o = jnp.einsum('btghn,bngd->btghd', p.astype(vcmp.dtype), vcmp)
    return o, p


def select_blocks(p_cmp, qpos, n_blocks):
    pg = jnp.sum(p_cmp, axis=3)
    nc = pg.shape[-1]
    r = SLC_BLOCK // CMP_STRIDE
    m = CMP_LEN // CMP_STRIDE
    pad = jnp.pad(pg, ((0, 0), (0, 0), (0, 0), (m - 1, r * n_blocks - nc)))
    score = sum(pad[..., a - bb + m - 1: a - bb + m - 1 + r * n_blocks: r]
                for a in range(r) for bb in range(m))
    j = jnp.arange(n_blocks, dtype=jnp.int32)[None, :]
    cur = (qpos // SLC_BLOCK)[:, None]
    valid = j * SLC_BLOCK <= qpos[:, None]
    forced = (j == 0) | (j == cur) | (j == cur - 1)
    score = jnp.where(forced[:, None, :], jnp.inf, jnp.where(valid[:, None, :], score, -jnp.inf))
    _, idx = lax.top_k(score, min(N_SELECT, n_blocks))
    return idx.astype(jnp.int32)


def slc_attend(qg, idx, qpos, ksb, vsb, table_g):
    b, q, g, nsel = idx.shape
    flat = idx.transpose(0, 2, 1, 3).reshape(b, g, q * nsel)
    take = jax.vmap(jax.vmap(lambda blocks, i: blocks[i]))
    kg = take(ksb, flat).reshape(b, g, q, nsel * SLC_BLOCK, HEAD_DIM)
    vg = take(vsb, flat).reshape(b, g, q, nsel * SLC_BLOCK, HEAD_DIM)
    kpos = (idx[..., None] * SLC_BLOCK + jnp.arange(SLC_BLOCK, dtype=jnp.int32)).reshape(b, q, g, nsel * SLC_BLOCK)
    dist = qpos[None, :, None, None] - kpos
    bias = table_g[jnp.arange(g)[None, None, :, None], rel_bucket(dist)].astype(jnp.float32)
    bias = bias.transpose(0, 1, 2, 4, 3)
    s = jnp.einsum('bqghd,bgqmd->bqghm', qg, kg).astype(jnp.float32) * SCALE + bias
    mask = (kpos <= qpos[None, :, None, None])[:, :, :, None, :]
    p = masked_softmax(s, mask)
    return jnp.einsum('bqghm,bgqmd->bqghd', p.astype(vg.dtype), vg)


def win_attend(qg, kw, vw, qpos, kpos, table):
    dist = qpos[:, None] - kpos[None, :]
    s = jnp.einsum('bqghd,bkgd->bqghk', qg, kw).astype(jnp.float32) * SCALE + head_bias(table, dist)
    mask = ((kpos[None, :] <= qpos[:, None]) & (kpos[None, :] >= qpos[:, None] - WINDOW)
            & (kpos[None, :] >= 0))[:, None, None, :]
    p = masked_softmax(s, mask)
    return jnp.einsum('bqghk,bkgd->bqghd', p.astype(vw.dtype), vw)


def win_banded(qg, kw, vw, table):
    b, t = qg.shape[:2]
    if t <= WIN_QBLOCK or t % WIN_QBLOCK:
        pos = jnp.arange(t, dtype=jnp.int32)
        return win_attend(qg, kw, vw, pos, pos, table)
    nb = t // WIN_QBLOCK
    span = WIN_QBLOCK + WINDOW
    kp = jnp.pad(kw, ((0, 0), (WINDOW, 0), (0, 0), (0, 0)))
    vp = jnp.pad(vw, ((0, 0), (WINDOW, 0), (0, 0), (0, 0)))
    qb = qg.reshape((b, nb, WIN_QBLOCK) + qg.shape[2:]).swapaxes(0, 1)

    def one(args):
        i, qi = args
        start = i * WIN_QBLOCK
        ki = lax.dynamic_slice_in_dim(kp, start, span, axis=1)
        vi = lax.dynamic_slice_in_dim(vp, start, span, axis=1)
        qpos = start + jnp.arange(WIN_QBLOCK, dtype=jnp.int32)
        kpos = start - WINDOW + jnp.arange(span, dtype=jnp.int32)
        return win_attend(qi, ki, vi, qpos, kpos, table)

    out = lax.map(one, (jnp.arange(nb, dtype=jnp.int32), qb))
    return out.swapaxes(0, 1).reshape(qg.shape)


def map_query_blocks(fn, block, qg, idx, qpos):
    t = qg.shape[1]
    if t <= block or t % block:
        return fn(qg, idx, qpos)
    n = t // block

    def split(a):
        return a.reshape((a.shape[0], n, block) + a.shape[2:]).swapaxes(0, 1)

    out = lax.map(lambda a: fn(a[0], a[1], a[2]), (split(qg), split(idx), qpos.reshape(n, block)))
    return out.swapaxes(0, 1).reshape(qg.shape)


def nsa_mixer(h, pos0, past, w_in, w_out, ck_w1, ck_pe, ck_w2, cv_w1, cv_pe, cv_w2, table):
    b, t, _ = h.shape
    proj = h @ w_in
    cuts = [Q_W + KV_W * i for i in range(7)]
    q, kc, vc, ks, vs, kw, vw, gates = jnp.split(proj, cuts, axis=-1)
    qg = q.reshape(b, t, N_KV, HPG, HEAD_DIM)
    kc, vc, ks, vs, kw, vw = [a.reshape(b, t, N_KV, HEAD_DIM) for a in (kc, vc, ks, vs, kw, vw)]
    gates = jax.nn.sigmoid(gates.astype(jnp.float32)).reshape(b, t, 3, N_KV, HPG, 1)
    qpos = pos0 + jnp.arange(t, dtype=jnp.int32)
    if past is None:
        kc_all, vc_all, ks_all, vs_all = kc, vc, ks, vs
    else:
        pkc, pvc, pks, pvs, wbk, wbv = past
        kc_all = jnp.concatenate([pkc.astype(kc.dtype), kc], axis=1)
        vc_all = jnp.concatenate([pvc.astype(vc.dtype), vc], axis=1)
        ks_all = jnp.concatenate([pks.astype(ks.dtype), ks], axis=1)
        vs_all = jnp.concatenate([pvs.astype(vs.dtype), vs], axis=1)
    kcmp = compress(kc_all, ck_w1, ck_pe, ck_w2)
    vcmp = compress(vc_all, cv_w1, cv_pe, cv_w2)
    o_cmp, p_cmp = cmp_branch(qg, kcmp, vcmp, qpos, table)
    tk = ks_all.shape[1]
    ns = -(-tk // SLC_BLOCK)
    idx = select_blocks(p_cmp, qpos, ns)

    def to_blocks(a):
        a = jnp.pad(a, ((0, 0), (0, ns * SLC_BLOCK - tk), (0, 0), (0, 0)))
        return a.reshape(b, ns, SLC_BLOCK, N_KV, HEAD_DIM).transpose(0, 3, 1, 2, 4)

    ksb, vsb = to_blocks(ks_all), to_blocks(vs_all)
    table_g = table.reshape(NUM_BUCKETS, N_KV, HPG).transpose(1, 0, 2)
    o_slc = map_query_blocks(lambda qq, ii, pp: slc_attend(qq, ii, pp, ksb, vsb, table_g),
                             SEL_QBLOCK, qg, idx, qpos)
    if past is None:
        o_win = win_banded(qg, kw, vw, table)
        keep = min(WINDOW, t)
        win_k, win_v = kw[:, t - keep:], vw[:, t - keep:]
    else:
        wb = wbk.shape[1]
        kw_all = jnp.concatenate([wbk.astype(kw.dtype), kw], axis=1)
        vw_all = jnp.concatenate([wbv.astype(vw.dtype), vw], axis=1)
        kpos = pos0 - wb + jnp.arange(wb + t, dtype=jnp.int32)
        o_win = win_attend(qg, kw_all, vw_all, qpos, kpos, table)
        win_k, win_v = kw_all[:, t:], vw_all[:, t:]
    o = gates[:, :, 0] * o_cmp + gates[:, :, 1] * o_slc + gates[:, :, 2] * o_win
    out = o.astype(h.dtype).reshape(b, t, Q_W) @ w_out
    return out, (kc, vc, ks, vs, win_k, win_v)


def conv_mixer(h, past, w1, b1, dw, dwb, ln_g, ln_b, w2, b2):
    u = h @ w1 + b1
    a, g = jnp.split(u, 2, axis=-1)
    u = a * jax.nn.sigmoid(g)
    if past is None:
        ctx = jnp.pad(u, ((0, 0), (CONV_W - 1, 0), (0, 0)))
    else:
        ctx = jnp.concatenate([past.astype(u.dtype), u], axis=1)
    y = lax.conv_general_dilated(ctx, dw[:, None, :].astype(ctx.dtype), (1,), 'VALID',
                                 dimension_numbers=('NWC', 'WIO', 'NWC'),
                                 feature_group_count=D_MODEL) + dwb
    y = jax.nn.silu(layernorm(y, ln_g, ln_b))
    return y @ w2 + b2, ctx[:, ctx.shape[1] - (CONV_W - 1):]


def trunk(x, p, pos0, conv_past, attn_past, w):
    conv_states, attn_states = [], []
    for i in range(DEPTH):
        j = i // 2
        h = rmsnorm(x, w['norm_mix'][i])
        if i % 2 == 0:
            mix, st = conv_mixer(h, None if conv_past is None else conv_past(j),
                                 w['conv_w1'][j], w['conv_b1'][j], w['conv_dw'][j], w['conv_dwb'][j],
                                 w['conv_ln_g'][j], w['conv_ln_b'][j], w['conv_w2'][j], w['conv_b2'][j])
            conv_states.append(st)
        else:
            mix, st = nsa_mixer(h, pos0, None if attn_past is None else attn_past(j),
                                w['attn_w_in'][j], w['attn_w_out'][j],
                                w['cmpk_w1'][j], w['cmpk_pe'][j], w['cmpk_w2'][j],
                                w['cmpv_w1'][j], w['cmpv_pe'][j], w['cmpv_w2'][j], w['rel_table'])
            attn_states.append(st)
        x = x + mix
        h = rmsnorm(x, w['norm_ffn'][i])
        x = x + jnp.square(jax.nn.relu(h @ w['mlp_up'][i])) @ w['mlp_down'][i]
        gate = jax.nn.sigmoid(rmsnorm(x, w['norm_ple'][i]) @ w['ple_gate'][i])
        x = x + gate * (p[i] @ w['ple_proj'][i])
    y = rmsnorm(x, w['norm_final'])
    conv_new = jnp.stack(conv_states)
    attn_new = [jnp.stack([s[k] for s in attn_states]) for k in range(6)]
    return y, conv_new, attn_new


def setup_inputs(seed: int = 0) -> dict:
    key = jax.random.key(seed)
    keys = iter(jax.random.split(key, 64))

    def nrm(shape, scale):
        return jax.random.normal(next(keys), shape, jnp.float32) * scale

    n_pages = PAST_LEN // PAGE_SIZE
    n_pool = (5 * DEC_BATCH * n_pages + 3) // 4
    wb = min(WINDOW, PAST_LEN)
    na, nc = N_ATTN_LAYERS, N_CONV_LAYERS
    kv_shape = (na, n_pool, PAGE_SIZE, N_KV, HEAD_DIM)
    x_prompt = nrm((BATCH, SEQ, D_MODEL), 1.0)
    x_sample = nrm((DEC_BATCH, DEC_SEQ, D_MODEL), 1.0)
    cache_cmp_k = nrm(kv_shape, 1.0)
    cache_cmp_v = nrm(kv_shape, 1.0)
    cache_slc_k = nrm(kv_shape, 1.0)
    cache_slc_v = nrm(kv_shape, 1.0)
    state_win_k = nrm((na, DEC_BATCH, wb, N_KV, HEAD_DIM), 1.0)
    state_win_v = nrm((na, DEC_BATCH, wb, N_KV, HEAD_DIM), 1.0)
    state_conv = nrm((nc, DEC_BATCH, CONV_W - 1, D_MODEL), 0.5)
    page_table = jax.random.permutation(next(keys), n_pool)[:DEC_BATCH * n_pages].reshape(
        DEC_BATCH, n_pages).astype(jnp.int32)
    p_prompt = nrm((DEPTH, BATCH, SEQ, D_PLE), 1.0)
    p_sample = nrm((DEPTH, DEC_BATCH, DEC_SEQ, D_PLE), 1.0)
    return {
        'x_prompt': x_prompt, 'x_sample': x_sample,
        'cache_cmp_k': cache_cmp_k, 'cache_cmp_v': cache_cmp_v,
        'cache_slc_k': cache_slc_k, 'cache_slc_v': cache_slc_v,
        'state_win_k': state_win_k, 'state_win_v': state_win_v, 'state_conv': state_conv,
        'page_table': page_table, 'p_prompt': p_prompt, 'p_sample': p_sample,
        'rel_table': nrm((NUM_BUCKETS, N_HEADS), 0.5),
        'norm_mix': 1.0 + nrm((DEPTH, D_MODEL), 0.05),
        'norm_ffn': 1.0 + nrm((DEPTH, D_MODEL), 0.05),
        'norm_ple': 1.0 + nrm((DEPTH, D_MODEL), 0.05),
        'norm_final': 1.0 + nrm((D_MODEL,), 0.05),
        'conv_w1': nrm((nc, D_MODEL, 2 * D_MODEL), D_MODEL ** -0.5),
        'conv_b1': nrm((nc, 2 * D_MODEL), 0.02),
        'conv_dw': nrm((nc, CONV_W, D_MODEL), CONV_W ** -0.5),
        'conv_dwb': nrm((nc, D_MODEL), 0.02),
        'conv_ln_g': 1.0 + nrm((nc, D_MODEL), 0.05),
        'conv_ln_b': nrm((nc, D_MODEL), 0.02),
        'conv_w2': nrm((nc, D_MODEL, D_MODEL), D_MODEL ** -0.5),
        'conv_b2': nrm((nc, D_MODEL), 0.02),
        'attn_w_in': nrm((na, D_MODEL, IN_COLS), D_MODEL ** -0.5),
        'attn_w_out': nrm((na, Q_W, D_MODEL), Q_W ** -0.5),
        'cmpk_w1': nrm((na, CMP_LEN * HEAD_DIM, CMP_HID), (CMP_LEN * HEAD_DIM) ** -0.5),
        'cmpk_pe': nrm((na, CMP_LEN, HEAD_DIM), 0.1),
        'cmpk_w2': nrm((na, CMP_HID, HEAD_DIM), CMP_HID ** -0.5),
        'cmpv_w1': nrm((na, CMP_LEN * HEAD_DIM, CMP_HID), (CMP_LEN * HEAD_DIM) ** -0.5),
        'cmpv_pe': nrm((na, CMP_LEN, HEAD_DIM), 0.1),
        'cmpv_w2': nrm((na, CMP_HID, HEAD_DIM), CMP_HID ** -0.5),
        'mlp_up': nrm((DEPTH, D_MODEL, D_FF), D_MODEL ** -0.5),
        'mlp_down': nrm((DEPTH, D_FF, D_MODEL), D_FF ** -0.5),
        'ple_proj': nrm((DEPTH, D_PLE, D_MODEL), D_PLE ** -0.5),
        'ple_gate': nrm((DEPTH, D_MODEL, D_MODEL), D_MODEL ** -0.5),
    }


def reference(x_prompt, x_sample, cache_cmp_k, cache_cmp_v, cache_slc_k, cache_slc_v,
              state_win_k, state_win_v, state_conv, page_table, p_prompt, p_sample,
              rel_table, norm_mix, norm_ffn, norm_ple, norm_final,
              conv_w1, conv_b1, conv_dw, conv_dwb, conv_ln_g, conv_ln_b, conv_w2, conv_b2,
              attn_w_in, attn_w_out, cmpk_w1, cmpk_pe, cmpk_w2, cmpv_w1, cmpv_pe, cmpv_w2,
              mlp_up, mlp_down, ple_proj, ple_gate):
    w = dict(rel_table=rel_table, norm_mix=norm_mix, norm_ffn=norm_ffn, norm_ple=norm_ple,
             norm_final=norm_final, conv_w1=conv_w1, conv_b1=conv_b1, conv_dw=conv_dw,
             conv_dwb=conv_dwb, conv_ln_g=conv_ln_g, conv_ln_b=conv_ln_b, conv_w2=conv_w2,
             conv_b2=conv_b2, attn_w_in=attn_w_in, attn_w_out=attn_w_out, cmpk_w1=cmpk_w1,
             cmpk_pe=cmpk_pe, cmpk_w2=cmpk_w2, cmpv_w1=cmpv_w1, cmpv_pe=cmpv_pe, cmpv_w2=cmpv_w2,
             mlp_up=mlp_up, mlp_down=mlp_down, ple_proj=ple_proj, ple_gate=ple_gate)
    y_prompt, conv_p, attn_p = trunk(x_prompt, p_prompt, 0, None, None, w)
    db = page_table.shape[0]
    past_len = page_table.shape[1] * cache_cmp_k.shape[2]

    def gather(c, j):
        return c[j, page_table].reshape(db, past_len, N_KV, HEAD_DIM)

    def conv_past(j):
        return state_conv[j]

    def attn_past(j):
        return (gather(cache_cmp_k, j), gather(cache_cmp_v, j), gather(cache_slc_k, j),
                gather(cache_slc_v, j), state_win_k[j], state_win_v[j])

    y_sample, conv_s, attn_s = trunk(x_sample, p_sample, past_len, conv_past, attn_past, w)
    cmp_k_p, cmp_v_p, slc_k_p, slc_v_p, win_k_p, win_v_p = attn_p
    cmp_k_s, cmp_v_s, slc_k_s, slc_v_s, win_k_s, win_v_s = attn_s
    return (y_prompt, y_sample, cmp_k_p, cmp_v_p, slc_k_p, slc_v_p, win_k_p, win_v_p, conv_p,
            cmp_k_s, cmp_v_s, slc_k_s, slc_v_s, win_k_s, win_v_s, conv_s)
```

```python
import math
import numpy as np
from contextlib import ExitStack
import concourse.bass as bass
import concourse.mybir as mybir
from concourse.bass_utils import run_bass_kernel_spmd

F32 = mybir.dt.float32
BF16 = mybir.dt.bfloat16
I32 = mybir.dt.int32
ALU = mybir.AluOpType
AF = mybir.ActivationFunctionType
AX = mybir.AxisListType

NCORES = 8
T = 2048
NS = 4
NT = T + NS
D = 1024
NEG = -30000.0

NM0, NM1, NF0, NF1, NP0, NP1, NFIN, B1, DWB, LNG, LNB, B2, DW, PEK, PEV = \
    0, 8, 16, 24, 32, 40, 48, 56, 72, 80, 88, 96, 104, 352, 368
NVEC = 384


class FW:
    NDMA = 7

    def __init__(self, nc, es):
        self.nc = nc
        self.es = es
        self.eng = {'pe': nc.tensor, 'act': nc.scalar, 'dve': nc.vector, 'pool': nc.gpsimd, 'sp': nc.sync}
        self.n_inst = 0
        self.psl = []
        self.psi = 0
        self.cur_es = None
        self.sfx = ""
        self.nset = 0
        self.cnt = {e: 0 for e in self.eng}
        self.dslot_tok = [None] * self.NDMA
        self.new_sems()

    def new_sems(self):
        if self.nset:
            self.fence()
        i = self.nset
        self.nset += 1
        self.sem = {e: self.es.enter_context(self.nc.semaphore("s%d_%s" % (i, e))) for e in self.eng}
        self.cnt = {e: 0 for e in self.eng}
        self.dsem = [self.es.enter_context(self.nc.semaphore("d%d_%d" % (i, j))) for j in range(self.NDMA)]
        self.dcnt = 0
        self.dslot_tok = [None] * self.NDMA
        self.waited = {e: {} for e in self.eng}
        self.lastw = {}
        self.readers = {}

    def sb(self, name, shape, dt=F32, es=None):
        return (es or self.cur_es or self.es).enter_context(self.nc.sbuf_tensor(name + self.sfx, list(shape), dt))

    def fence(self):
        toks = [(e, self.cnt[e]) for e in self.eng if self.cnt[e]] + [t for t in self.dslot_tok if t is not None]
        for e in self.eng:
            for t in toks:
                if t[0] != e:
                    self._wait(e, t)

    def mkpsum(self):
        for i in range(8):
            self.psl.append((self.es.enter_context(self.nc.psum_tensor("ps%d" % i, [128, 512], F32)), "ps%d" % i))

    def pget(self):
        r = self.psl[self.psi % 6]
        self.psi += 1
        return r

    def pacc(self, i):
        return self.psl[6 + i]

    def _semof(self, tok):
        return self.sem[tok[0]] if tok[0] != 'd' else self.dsem[tok[1]]

    def _wait(self, e, tok):
        key = tok[0] if tok[0] != 'd' else ('d', tok[1])
        val = tok[-1]
        if self.waited[e].get(key, 0) >= val:
            return
        self.waited[e][key] = val
        self.eng[e].wait_ge(self._semof(tok), val)

    def _deps(self, e, reads, writes):
        toks = []
        for k in list(reads) + list(writes):
            t = self.lastw.get(k)
            if t is not None:
                toks.append(t)
        for k in writes:
            toks.extend(self.readers.get(k, ()))
        for t in toks:
            if e == 'pe' and t[0] == 'pe':
                continue
            self._wait(e, t)

    def _record(self, tok, reads, writes):
        for k in reads:
            self.readers.setdefault(k, []).append(tok)
        for k in writes:
            self.lastw[k] = tok
            self.readers[k] = []

    def op(self, e, fn, reads=(), writes=()):
        px = [k for k in reads if isinstance(k, str) and k.startswith("ps")]
        if px:
            writes = list(writes) + px
        self._deps(e, reads, writes)
        ins = fn(self.eng[e])
        self.cnt[e] += 1
        ins.then_inc(self.sem[e], 1)
        self._record((e, self.cnt[e]), reads, writes)
        self.n_inst += 1
        return ins

    def dma(self, out, in_, reads=(), writes=(), q='sp', fn=None, **kw):
        s = self.dcnt % self.NDMA
        v = 16 * (self.dcnt // self.NDMA + 1)
        self.dcnt += 1
        prev = self.dslot_tok[s]
        if prev is not None:
            self._wait(q, prev)
        self._deps(q, reads, writes)
        if fn is not None:
            ins = fn(self.eng[q])
        else:
            ins = self.eng[q].dma_start(out=out, in_=in_, **kw)
        ins.then_inc(self.dsem[s], 16)
        tok = ('d', s, v)
        self.dslot_tok[s] = tok
        self._record(tok, reads, writes)
        self.n_inst += 1
        return tok

    def finish(self):
        for s in range(self.NDMA):
            if self.dslot_tok[s] is not None:
                self._wait('sp', self.dslot_tok[s])
        for e in self.eng:
            if self.cnt[e]:
                self._wait('sp', (e, self.cnt[e]))


def segs(n):
    return [(0, n)] if n <= 512 else [(0, 512), (512, n - 512)]


TILES = [(0, 512), (512, 512), (1024, 512), (1536, 516)]


def xkeys(ti, cs=range(8)):
    ks = [("X", c, ti) for c in cs]
    if ti == 3:
        ks += [("X", c, 4) for c in cs]
    return ks


class K:
    def __init__(self, nc, es, dbg=None, stage=9):
        self.nc = nc
        self.f = FW(nc, es)
        self.dbg = dbg
        self.stage = stage
        self.rr = 0
        f = self.f
        dt = lambda name, shape, d=F32, kind="ExternalInput": nc.dram_tensor(name, list(shape), d, kind=kind).ap()
        NSQ, NSM = 8, 32
        FI = {}
        for name, shape in [("xp", [NSQ * T, D]), ("xs", [NSM, D]), ("pp", [2, NSQ * T, 256]), ("psm", [2, NSM, 256]),
                            ("sconv", [NSM * 30, D]), ("vecs", [NVEC, 128]), ("ident", [128, 128]),
                            ("conv_w1", [D, 2 * D]), ("conv_w2", [D, D]),
                            ("mlp_up", [2, D, 4 * D]), ("mlp_down", [2, 4 * D, D]),
                            ("ple_proj", [2, 256, D]), ("ple_gate", [2, D, D]),
                            ("w_in", [D, 2608]), ("w_out", [D, D]), ("ck_w1", [2048, 128]), ("ck_w2", [128, 64]),
                            ("cv_w1", [2048, 128]), ("cv_w2", [128, 64]), ("rel_table", [32, 16]),
                            ("ohd", [34, 16384]), ("oho", [34, 16384]), ("ohg", [34, 32512]), ("tailm", [128, 128]),
                            ("fvtab", [128, 512]), ("exm", [32, 2048]),
                            ("swk", [NSM * 512, 256]), ("swv", [NSM * 512, 256]),
                            ("cck", [655360, 256]), ("ccv", [655360, 256]), ("csk", [655360, 256]), ("csv", [655360, 256]),
                            ("ohs", [34, 1024]), ("fvs", [1, 256]), ("e2", [33, 128]), ("jm", [48, 4]), ("rm", [48, 12]),
                            ("oh127", [34, 128])]:
            FI[name] = dt(name, shape)
        FI["ptab"] = dt("ptab", [1, NSM * 128], I32)
        FO = {}
        FO["y_p"] = dt("y_p", [NSQ * T, D], kind="ExternalOutput")
        FO["y_s"] = dt("y_s", [NSM, D], kind="ExternalOutput")
        FO["conv_p"] = dt("conv_p", [NSQ * 30, D], kind="ExternalOutput")
        FO["conv_s"] = dt("conv_s", [NSM * 30, D], kind="ExternalOutput")
        for nm in ["cmp_k_p", "cmp_v_p", "slc_k_p", "slc_v_p"]:
            FO[nm] = dt(nm, [NSQ * T, 256], kind="ExternalOutput")
        for nm in ["win_k_p", "win_v_p"]:
            FO[nm] = dt(nm, [NSQ * 512, 256], kind="ExternalOutput")
        for nm in ["cmp_k_s", "cmp_v_s", "slc_k_s", "slc_v_s"]:
            FO[nm] = dt(nm, [NSM, 256], kind="ExternalOutput")
        for nm in ["win_k_s", "win_v_s"]:
            FO[nm] = dt(nm, [NSM * 512, 256], kind="ExternalOutput")
        self.bsc = dt("bsc", [16, 32768], kind="Internal")
        self.FI, self.FO = FI, FO
        self.bind(0)
        self.X = f.sb("X", [128, 8, NT])
        self.xn = f.sb("xn", [128, 8, 516], BF16)
        self.hT = None
        self.VT = f.sb("VT", [128, NVEC])
        self.ids = f.sb("ids", [128, 128])
        self.idb = f.sb("idb", [128, 128], BF16)
        self.ones = f.sb("ones", [128, 128])
        self.wbuf = [f.sb("wb%d" % i, [128, 4096], BF16) for i in range(3)]
        self.wi = 0
        self.stg = [f.sb("stg%d" % i, [128, 1024]) for i in range(2)]
        self.si = 0
        self.sc = [f.sb("sc%d" % i, [128, 516]) for i in range(3)]
        self.sci = 0
        self.rstd = f.sb("rstd", [128, 516])
        f.mkpsum()

    def bind(self, it):
        FI, FO = self.FI, self.FO
        I = dict(FI)
        I["xp"] = FI["xp"][it * T:(it + 1) * T, :]
        I["xs"] = FI["xs"][it * NS:(it + 1) * NS, :]
        I["pp"] = FI["pp"][:, it * T:(it + 1) * T, :]
        I["psm"] = FI["psm"][:, it * NS:(it + 1) * NS, :]
        I["sconv"] = FI["sconv"][it * NS * 30:(it + 1) * NS * 30, :]
        I["swk"] = FI["swk"][it * NS * 512:(it + 1) * NS * 512, :]
        I["swv"] = FI["swv"][it * NS * 512:(it + 1) * NS * 512, :]
        I["ptab"] = FI["ptab"][:, it * NS * 128:(it + 1) * NS * 128]
        O = {}
        O["y_p"] = FO["y_p"][it * T:(it + 1) * T, :]
        O["y_s"] = FO["y_s"][it * NS:(it + 1) * NS, :]
        O["conv_p"] = FO["conv_p"][it * 30:(it + 1) * 30, :]
        O["conv_s"] = FO["conv_s"][it * NS * 30:(it + 1) * NS * 30, :]
        for nm in ["cmp_k_p", "cmp_v_p", "slc_k_p", "slc_v_p"]:
            O[nm] = FO[nm][it * T:(it + 1) * T, :]
        for nm in ["win_k_p", "win_v_p"]:
            O[nm] = FO[nm][it * 512:(it + 1) * 512, :]
        for nm in ["cmp_k_s", "cmp_v_s", "slc_k_s", "slc_v_s"]:
            O[nm] = FO[nm][it * NS:(it + 1) * NS, :]
        for nm in ["win_k_s", "win_v_s"]:
            O[nm] = FO[nm][it * NS * 512:(it + 1) * NS * 512, :]
        self.I, self.O = I, O
        self.f.sfx = "_i%d" % it

    def scope(self):
        k = self

        class _S:
            def __enter__(s2):
                s2.prev = k.f.cur_es
                s2.es = ExitStack()
                s2.es.__enter__()
                k.f.cur_es = s2.es
                return s2

            def __exit__(s2, *a):
                k.f.fence()
                k.f.cur_es = s2.prev
                s2.es.__exit__(None, None, None)
                return False
        return _S()

    def scr(self):
        i = self.sci % 3
        self.sci += 1
        return self.sc[i], "sc%d" % i

    def ev_engine(self):
        self.rr += 1
        return 'act' if self.rr % 2 else 'dve'

    def copy(self, e, out, in_, reads, writes):
        if e == 'act':
            self.f.op('act', lambda en: en.activation(out=out, in_=in_, func=AF.Identity), reads, writes)
        else:
            self.f.op(e, lambda en: en.tensor_copy(out=out, in_=in_), reads, writes)

    def load_rows_T(self, src, nrows, ncols, sink):
        f = self.f
        i = self.si % 2
        self.si += 1
        st, sk = self.stg[i], "stg%d" % i
        f.dma(st[0:nrows, 0:ncols], src, writes=[sk])
        nch = ncols // 128
        per = max(1, 512 // max(nrows, 1))
        per = min(per, 4)
        c = 0
        while c < nch:
            g = min(per, nch - c)
            ps, pk = f.pget()
            for j in range(g):
                f.op('pe', lambda en, j=j, c=c: en.transpose(ps[:, j * nrows:(j + 1) * nrows],
                                                              st[0:nrows, (c + j) * 128:(c + j + 1) * 128],
                                                              self.ids[0:nrows, 0:nrows]),
                     reads=[sk, "ids"], writes=[pk])
            sink(c, g, ps, pk)
            c += g

    def store_T(self, dst, src_fn, ntok, src_keys, ncols=1024, q='sp'):
        f = self.f
        i = self.si % 2
        self.si += 1
        st, sk = self.stg[i], "stg%d" % i
        nch = ncols // 128
        for c0 in range(0, nch, 4):
            ps, pk = f.pget()
            g = min(4, nch - c0)
            for j in range(g):
                f.op('pe', lambda en, j=j, c0=c0: en.transpose(ps[0:ntok, j * 128:(j + 1) * 128], src_fn(c0 + j), self.ids[:, :]),
                     reads=list(src_keys) + ["ids"], writes=[pk])
            self.copy(self.ev_engine(), st[0:ntok, c0 * 128:(c0 + g) * 128], ps[0:ntok, 0:g * 128], [pk], [sk])
        f.dma(dst, st[0:ntok, 0:ncols], reads=[sk], q=q)

    def wload(self, src, kc, width):
        i = self.wi % 3
        self.wi += 1
        wb, wk = self.wbuf[i], "wb%d" % i
        view = wb[:, 0:kc * width].rearrange("p (k m) -> p k m", k=kc)
        self.f.dma(view, src.rearrange("(k p) m -> p k m", p=128), writes=[wk], q='pool')
        return view, wk

    def linear(self, wsrc, kc, nout, xin, xkeys_, n, consume, bw=None):
        f = self.f
        if bw is None:
            bw = 512 if kc <= 8 else 128
        bw = min(bw, nout)
        for b0 in range(0, nout, bw):
            wv, wk = self.wload(wsrc[:, b0:b0 + bw], kc, bw)
            for mm in range(bw // 128):
                outs = []
                for (c0, cn) in segs(n):
                    ps, pk = f.pget()
                    for k in range(kc):
                        f.op('pe', lambda en, k=k, mm=mm, c0=c0, cn=cn, ps=ps: en.matmul(
                            ps[:, 0:cn], lhsT=wv[:, k, mm * 128:(mm + 1) * 128], rhs=xin(k)[:, c0:c0 + cn],
                            start=(k == 0), stop=(k == kc - 1)), reads=[wk] + list(xkeys_), writes=[pk])
                    outs.append((ps, pk, c0, cn))
                consume(b0 // 128 + mm, outs)

    def rmsnorm(self, ti, t0, n, gcol):
        f = self.f
        X = self.X
        for (c0, cn) in segs(n):
            ps, pk = f.pget()
            for c in range(8):
                s, sk = self.scr()
                f.op('act', lambda en, c=c, s=s: en.activation(out=s[:, 0:cn], in_=X[:, c, t0 + c0:t0 + c0 + cn], func=AF.Square),
                     reads=xkeys(ti, [c]), writes=[sk])
                f.op('pe', lambda en, c=c, s=s: en.matmul(ps[:, 0:cn], lhsT=self.ones[:, :], rhs=s[:, 0:cn],
                                                           start=(c == 0), stop=(c == 7)), reads=[sk, "ones"], writes=[pk])
            f.op('act', lambda en: en.activation(out=self.rstd[:, c0:c0 + cn], in_=ps[:, 0:cn], func=AF.Sqrt,
                                                 bias=1e-6, scale=1.0 / D), reads=[pk], writes=["rstd"])
            f.op('dve', lambda en: en.reciprocal(out=self.rstd[:, c0:c0 + cn], in_=self.rstd[:, c0:c0 + cn]),
                 reads=["rstd"], writes=["rstd"])
        for c in range(8):
            f.op('dve', lambda en, c=c: en.scalar_tensor_tensor(
                out=self.xn[:, c, 0:n], in0=X[:, c, t0:t0 + n], scalar=self.VT[:, gcol + c:gcol + c + 1],
                in1=self.rstd[:, 0:n], op0=ALU.mult, op1=ALU.mult),
                reads=xkeys(ti, [c]) + ["rstd", "VT"], writes=[("xn", c)])

    def setup(self, first=True):
        f = self.f
        I = self.I
        if first:
            f.dma(self.ids[:], I["ident"], writes=["ids"])
            f.op('dve', lambda en: en.tensor_copy(out=self.idb[:], in_=self.ids[:]), reads=["ids"], writes=["idb"])
            f.op('pool', lambda en: en.memset(self.ones[:], 1.0), writes=["ones"])
            for r in range(NVEC // 128):
                def sink(c, g, ps, pk, r=r):
                    self.copy('dve', self.VT[:, r * 128:(r + 1) * 128], ps[:, 0:128], [pk], ["VT"])
                self.load_rows_T(I["vecs"][r * 128:(r + 1) * 128, :], 128, 128, sink)
        for rt in range(16):
            ti = rt // 4

            def sink(c, g, ps, pk, rt=rt, ti=ti):
                self.copy(self.ev_engine(), self.X[:, c:c + g, rt * 128:(rt + 1) * 128],
                          ps[:, 0:g * 128].rearrange("p (c t) -> p c t", c=g), [pk], [("X", cc, ti) for cc in range(c, c + g)])
            self.load_rows_T(I["xp"][rt * 128:(rt + 1) * 128, :], 128, D, sink)

        def sink_s(c, g, ps, pk):
            self.copy('dve', self.X[:, c:c + g, T:NT], ps[:, 0:g * NS].rearrange("p (c t) -> p c t", c=g),
                      [pk], [("X", cc, 4) for cc in range(c, c + g)])
        self.load_rows_T(I["xs"][:, :], NS, D, sink_s)

    def conv_mixer(self):
        f = self.f
        I = self.I
        X = self.X
        VT = self.VT
        Ub = f.sb("Ub", [128, 8, 30 + 512], BF16)
        Ulast = f.sb("Ulast", [128, 8, 30])
        CS = f.sb("CS", [128, 8, NS, 31])
        Y = f.sb("Y", [128, 8, 516])
        Dg = [f.sb("Dg%d" % i, [128, 31, 128], BF16) for i in range(2)]
        tmpc = f.sb("tmpc", [128, 8, NS, 31])
        mu = f.sb("mu", [128, 516])
        f.op('pool', lambda en: en.memset(Ub[:, :, 0:30], 0.0), writes=[("Ub", c) for c in range(8)])
        def sink_cs(c, g, ps, pk):
            self.copy('dve', CS[:, c:c + g, :, 0:30], ps[:, 0:g * 120].rearrange("p (c b k) -> p c b k", c=g, b=NS), [pk], ["CS"])
        self.load_rows_T(I["sconv"][:, :], NS * 30, D, sink_cs)
        for b in range(NS):
            f.dma(self.O["conv_s"][b * 30:b * 30 + 29, :], I["sconv"][b * 30 + 1:b * 30 + 30, :])

        for ti, (t0, n) in enumerate(TILES):
            self.rmsnorm(ti, t0, n, NM0)
            for mb in range(2):
                wa, wak = self.wload(I["conv_w1"][:, mb * 512:(mb + 1) * 512], 8, 512)
                wg, wgk = self.wload(I["conv_w1"][:, D + mb * 512:D + (mb + 1) * 512], 8, 512)
                for j in range(4):
                    m = mb * 4 + j
                    for (c0, cn) in segs(n):
                        pa, pak = f.pget()
                        pg, pgk = f.pget()
                        for k in range(8):
                            f.op('pe', lambda en, k=k, j=j, pa=pa: en.matmul(pa[:, 0:cn], lhsT=wa[:, k, j * 128:(j + 1) * 128],
                                                                              rhs=self.xn[:, k, c0:c0 + cn], start=(k == 0), stop=(k == 7)),
                                 reads=[wak] + [("xn", kk) for kk in range(8)], writes=[pak])
                        for k in range(8):
                            f.op('pe', lambda en, k=k, j=j, pg=pg: en.matmul(pg[:, 0:cn], lhsT=wg[:, k, j * 128:(j + 1) * 128],
                                                                              rhs=self.xn[:, k, c0:c0 + cn], start=(k == 0), stop=(k == 7)),
                                 reads=[wgk] + [("xn", kk) for kk in range(8)], writes=[pgk])
                        s, sk = self.scr()
                        f.op('act', lambda en, s=s, pg=pg, m=m: en.activation(out=s[:, 0:cn], in_=pg[:, 0:cn], func=AF.Sigmoid,
                                                                              bias=VT[:, B1 + 8 + m:B1 + 9 + m]),
                             reads=[pgk, "VT"], writes=[sk])
                        u, uk = self.scr()
                        f.op('dve', lambda en, s=s, u=u, pa=pa, m=m: en.scalar_tensor_tensor(
                            out=u[:, 0:cn], in0=pa[:, 0:cn], scalar=VT[:, B1 + m:B1 + m + 1], in1=s[:, 0:cn],
                            op0=ALU.add, op1=ALU.mult), reads=[pak, sk, "VT"], writes=[uk])
                        if c0 == 0:
                            f.op('pool', lambda en, u=u, m=m: en.tensor_copy(out=Ub[:, m, 30:30 + 512], in_=u[:, 0:512]),
                                 reads=[uk], writes=[("Ub", m)])
                            if ti == 3:
                                f.op('pool', lambda en, u=u, m=m: en.tensor_copy(out=Ulast[:, m, :], in_=u[:, 482:512]),
                                     reads=[uk], writes=["Ulast"])
                        else:
                            f.op('pool', lambda en, u=u, m=m: en.tensor_copy(out=CS[:, m, :, 30], in_=u[:, 0:NS]),
                                 reads=[uk], writes=["CS"])
            for c in range(8):
                dg, dgk = Dg[c % 2], "Dg%d" % (c % 2)
                f.op('dve', lambda en, c=c, dg=dg: en.tensor_tensor(
                    out=dg[:, :, :], in0=self.ids[:, None, :].to_broadcast([128, 31, 128]),
                    in1=VT[:, DW + c:DW + c + 248:8][:, :, None].to_broadcast([128, 31, 128]), op=ALU.mult),
                    reads=["ids", "VT"], writes=[dgk])
                ps, pk = f.pget()
                for k in range(31):
                    f.op('pe', lambda en, k=k, c=c, dg=dg, ps=ps: en.matmul(ps[:, 0:512], lhsT=dg[:, k, :], rhs=Ub[:, c, k:k + 512],
                                                                           start=(k == 0), stop=(k == 30)),
                         reads=[dgk, ("Ub", c)], writes=[pk])
                f.op('act', lambda en, c=c, ps=ps: en.activation(out=Y[:, c, 0:512], in_=ps[:, 0:512], func=AF.Identity,
                                                                 bias=VT[:, DWB + c:DWB + c + 1]), reads=[pk, "VT"], writes=[("Y", c)])
                f.op('pool', lambda en, c=c: en.tensor_copy(out=Ub[:, c, 0:30], in_=Ub[:, c, 512:542]),
                     reads=[("Ub", c)], writes=[("Ub", c)])
            if ti == 3:
                dwv = VT[:, DW:DW + 248].rearrange("p (k c) -> p c k", c=8)
                f.op('dve', lambda en: en.tensor_tensor(out=tmpc[:, :, :, :], in0=CS[:, :, :, :],
                                                        in1=dwv[:, :, None, :].to_broadcast([128, 8, NS, 31]), op=ALU.mult),
                     reads=["CS", "VT"], writes=["tmpc"])
                f.op('dve', lambda en: en.tensor_reduce(out=Y[:, :, 512:516], in_=tmpc[:, :, :, :], axis=AX.X, op=ALU.add),
                     reads=["tmpc"], writes=[("Y", c) for c in range(8)])
                f.op('dve', lambda en: en.tensor_tensor(out=Y[:, :, 512:516], in0=Y[:, :, 512:516],
                                                        in1=VT[:, DWB:DWB + 8][:, :, None].to_broadcast([128, 8, NS]), op=ALU.add),
                     reads=[("Y", c) for c in range(8)] + ["VT"], writes=[("Y", c) for c in range(8)])
            for (c0, cn) in segs(n):
                p1, p1k = f.pget()
                p2, p2k = f.pget()
                for c in range(8):
                    f.op('pe', lambda en, c=c: en.matmul(p1[:, 0:cn], lhsT=self.ones[:, :], rhs=Y[:, c, c0:c0 + cn],
                                                         start=(c == 0), stop=(c == 7)), reads=[("Y", c), "ones"], writes=[p1k])
                for c in range(8):
                    s, sk = self.scr()
                    f.op('act', lambda en, c=c, s=s: en.activation(out=s[:, 0:cn], in_=Y[:, c, c0:c0 + cn], func=AF.Square),
                         reads=[("Y", c)], writes=[sk])
                    f.op('pe', lambda en, c=c, s=s: en.matmul(p2[:, 0:cn], lhsT=self.ones[:, :], rhs=s[:, 0:cn],
                                                              start=(c == 0), stop=(c == 7)), reads=[sk, "ones"], writes=[p2k])
                f.op('act', lambda en: en.activation(out=mu[:, c0:c0 + cn], in_=p1[:, 0:cn], func=AF.Identity, scale=1.0 / D),
                     reads=[p1k], writes=["mu"])
                s, sk = self.scr()
                f.op('dve', lambda en, s=s: en.tensor_tensor(out=s[:, 0:cn], in0=mu[:, c0:c0 + cn], in1=mu[:, c0:c0 + cn], op=ALU.mult),
                     reads=["mu"], writes=[sk])
                f.op('dve', lambda en, s=s: en.scalar_tensor_tensor(out=s[:, 0:cn], in0=p2[:, 0:cn], scalar=1.0 / D, in1=s[:, 0:cn],
                                                                    op0=ALU.mult, op1=ALU.subtract), reads=[p2k, sk], writes=[sk])
                f.op('act', lambda en, s=s: en.activation(out=self.rstd[:, c0:c0 + cn], in_=s[:, 0:cn], func=AF.Sqrt, bias=1e-6, scale=1.0),
                     reads=[sk], writes=["rstd"])
                f.op('dve', lambda en: en.reciprocal(out=self.rstd[:, c0:c0 + cn], in_=self.rstd[:, c0:c0 + cn]),
                     reads=["rstd"], writes=["rstd"])
            for c in range(8):
                s, sk = self.scr()
                f.op('dve', lambda en, c=c, s=s: en.tensor_tensor(out=s[:, 0:n], in0=Y[:, c, 0:n], in1=mu[:, 0:n], op=ALU.subtract),
                     reads=[("Y", c), "mu"], writes=[sk])
                f.op('dve', lambda en, c=c, s=s: en.tensor_tensor(out=s[:, 0:n], in0=s[:, 0:n], in1=self.rstd[:, 0:n], op=ALU.mult),
                     reads=[sk, "rstd"], writes=[sk])
                f.op('act', lambda en, c=c, s=s: en.activation(out=self.xn[:, c, 0:n], in_=s[:, 0:n], func=AF.Silu,
                                                               bias=VT[:, LNB + c:LNB + c + 1], scale=VT[:, LNG + c:LNG + c + 1]),
                     reads=[sk, "VT"], writes=[("xn", c)])
            def cons(m, outs, ti=ti, t0=t0):
                for (ps, pk, c0, cn) in outs:
                    f.op('dve', lambda en, ps=ps: en.scalar_tensor_tensor(
                        out=X[:, m, t0 + c0:t0 + c0 + cn], in0=ps[:, 0:cn], scalar=VT[:, B2 + m:B2 + m + 1],
                        in1=X[:, m, t0 + c0:t0 + c0 + cn], op0=ALU.add, op1=ALU.add),
                        reads=[pk, "VT"] + xkeys(ti, [m]), writes=xkeys(ti, [m]))
            self.linear(I["conv_w2"], 8, D, lambda k: self.xn[:, k, :], [("xn", kk) for kk in range(8)], n, cons)
            self.ffn_ple(0, ti, t0, n)
        self.store_T(self.O["conv_p"][:, :], lambda c: Ulast[:, c, :], 30, ["Ulast"])
        Us = f.sb("Us", [128, 8, NS])
        f.op('dve', lambda en: en.tensor_copy(out=Us[:, :, :], in_=CS[:, :, :, 30]), reads=["CS"], writes=["Us"])
        i = self.si % 2
        self.si += 1
        st, sk = self.stg[i], "stg%d" % i
        for c0 in range(0, 8, 4):
            ps, pk = f.pget()
            for j in range(4):
                f.op('pe', lambda en, j=j, c0=c0: en.transpose(ps[0:NS, j * 128:(j + 1) * 128], Us[:, c0 + j, :], self.ids[:, :]),
                     reads=["Us", "ids"], writes=[pk])
            self.copy('dve', st[0:NS, c0 * 128:(c0 + 4) * 128], ps[0:NS, 0:512], [pk], [sk])
        for b in range(NS):
            f.dma(self.O["conv_s"][b * 30 + 29:b * 30 + 30, :], st[b:b + 1, 0:D], reads=[sk])

    def ffn_ple(self, L, ti, t0, n):
        f = self.f
        I = self.I
        X = self.X
        self.rmsnorm(ti, t0, n, NF0 if L == 0 else NF1)

        def cons_up(m, outs):
            for (ps, pk, c0, cn) in outs:
                s, sk = self.scr()
                f.op('act', lambda en, s=s, ps=ps: en.activation(out=s[:, 0:cn], in_=ps[:, 0:cn], func=AF.Relu), reads=[pk], writes=[sk])
                f.op('pool', lambda en, s=s: en.tensor_tensor(out=self.hT[:, m, c0:c0 + cn], in0=s[:, 0:cn], in1=s[:, 0:cn], op=ALU.mult),
                     reads=[sk], writes=[("hT", m)])
        self.linear(I["mlp_up"][L], 8, 4 * D, lambda k: self.xn[:, k, :], [("xn", kk) for kk in range(8)], n, cons_up)

        def cons_dn(m, outs):
            for (ps, pk, c0, cn) in outs:
                f.op('dve', lambda en, ps=ps: en.tensor_tensor(out=X[:, m, t0 + c0:t0 + c0 + cn], in0=ps[:, 0:cn],
                                                               in1=X[:, m, t0 + c0:t0 + c0 + cn], op=ALU.add),
                     reads=[pk] + xkeys(ti, [m]), writes=xkeys(ti, [m]))
        self.linear(I["mlp_down"][L], 32, D, lambda k: self.hT[:, k, :], [("hT", kk) for kk in range(32)], n, cons_dn, bw=128)
        self.rmsnorm(ti, t0, n, NP0 if L == 0 else NP1)
        pT = self.hT
        for r in range(4):
            def sink(c, g, ps, pk, r=r):
                self.copy(self.ev_engine(), pT[:, c:c + g, r * 128:(r + 1) * 128], ps[:, 0:g * 128].rearrange("p (c t) -> p c t", c=g),
                          [pk], [("hT", cc) for cc in range(c, c + g)])
            self.load_rows_T(I["pp"][L, t0 + r * 128:t0 + (r + 1) * 128, :], 128, 256, sink)
        if ti == 3:
            def sink2(c, g, ps, pk):
                self.copy('dve', pT[:, c:c + g, 512:516], ps[:, 0:g * NS].rearrange("p (c t) -> p c t", c=g),
                          [pk], [("hT", cc) for cc in range(c, c + g)])
            self.load_rows_T(I["psm"][L, :, :], NS, 256, sink2)
        wp, wpk = None, None
        for mb in range(2):
            wgt, wgk = self.wload(I["ple_gate"][L][:, mb * 512:(mb + 1) * 512], 8, 512)
            wp, wpk = self.wload(I["ple_proj"][L][:, mb * 512:(mb + 1) * 512], 2, 512)
            for j in range(4):
                m = mb * 4 + j
                for (c0, cn) in segs(n):
                    pa, pak = f.pget()
                    pb, pbk = f.pget()
                    for k in range(8):
                        f.op('pe', lambda en, k=k, j=j, pa=pa: en.matmul(pa[:, 0:cn], lhsT=wgt[:, k, j * 128:(j + 1) * 128],
                                                                          rhs=self.xn[:, k, c0:c0 + cn], start=(k == 0), stop=(k == 7)),
                             reads=[wgk] + [("xn", kk) for kk in range(8)], writes=[pak])
                    for k in range(2):
                        f.op('pe', lambda en, k=k, j=j, pb=pb: en.matmul(pb[:, 0:cn], lhsT=wp[:, k, j * 128:(j + 1) * 128],
                                                                          rhs=pT[:, k, c0:c0 + cn], start=(k == 0), stop=(k == 1)),
                             reads=[wpk, ("hT", 0), ("hT", 1)], writes=[pbk])
                    s, sk = self.scr()
                    f.op('act', lambda en, s=s, pa=pa: en.activation(out=s[:, 0:cn], in_=pa[:, 0:cn], func=AF.Sigmoid), reads=[pak], writes=[sk])
                    f.op('dve', lambda en, s=s, pb=pb: en.tensor_tensor(out=s[:, 0:cn], in0=pb[:, 0:cn], in1=s[:, 0:cn], op=ALU.mult),
                         reads=[pbk, sk], writes=[sk])
                    f.op('dve', lambda en, s=s, m=m: en.tensor_tensor(out=X[:, m, t0 + c0:t0 + c0 + cn], in0=s[:, 0:cn],
                                                                      in1=X[:, m, t0 + c0:t0 + c0 + cn], op=ALU.add),
                         reads=[sk] + xkeys(ti, [m]), writes=xkeys(ti, [m]))


    def gen_table(self, ohsrc, ncols, TabAug, dst_view_fn):
        f = self.f
        for h0 in range(0, ncols, 1024):
            hw = min(1024, ncols - h0)
            i = self.si % 2
            self.si += 1
            st, sk = self.stg[i], "stg%d" % i
            f.dma(st[0:34, 0:hw], ohsrc[:, h0:h0 + hw], writes=[sk])
            for c0 in range(0, hw, 512):
                cw = min(512, hw - c0)
                ps, pk = f.pget()
                f.op('pe', lambda en, c0=c0, cw=cw, ps=ps, st=st: en.matmul(ps[0:16, 0:cw], lhsT=TabAug[0:34, 0:16], rhs=st[0:34, c0:c0 + cw],
                                                                            start=True, stop=True), reads=[sk, "TabAug"], writes=[pk])
                o, ok = self.scr()
                self.copy('dve', o[0:16, 0:cw], ps[0:16, 0:cw], [pk], [ok])
                f.dma(self.bsc[:, h0 + c0:h0 + c0 + cw], o[0:16, 0:cw], reads=[ok], writes=["bsc"])

    def nsa(self):
        f = self.f
        I = self.I
        X = self.X
        VT = self.VT
        win = I["w_in"]
        with self.scope():
            KT = f.sb("KT", [128, 2, 2, T], BF16)
            VA = f.sb("VA", [128, 16, 2, 4, 65], BF16)
            KCT = f.sb("KCT", [128, 2, 128], BF16)
            VCM = f.sb("VCM", [128, 4, 64], BF16)
            f.op('pool', lambda en: en.memset(VA[:, :, :, :, :], 1.0), writes=["VA"])
            with self.scope():
                CT = f.sb("CT", [128, 2, 2, T], BF16)
                W1r = f.sb("W1r", [128, 2, 32, 128], BF16)
                W2k = f.sb("W2k", [128, 128], BF16)
                W2v = f.sb("W2v", [128, 64], BF16)
                VTb = f.sb("VTb", [128, 32], BF16)
                hb = f.sb("hb", [128, 2])
                shb = f.sb("shb", [128, 128], BF16)
                for ci, nm in enumerate(["ck_w1", "cv_w1"]):
                    src = I[nm].rearrange("(t h) m -> h t m", h=64)
                    f.dma(W1r[0:64, ci], src, writes=["W1r"], q='pool')
                    f.dma(W1r[64:128, ci], src, writes=["W1r"], q='pool')
                f.dma(W2k[:, 0:64], I["ck_w2"], writes=["W2k"], q='pool')
                f.dma(W2k[:, 64:128], I["ck_w2"], writes=["W2k"], q='pool')
                f.dma(W2v[:, :], I["cv_w2"], writes=["W2v"], q='pool')
                f.op('dve', lambda en: en.tensor_copy(out=VTb[:, :], in_=VT[:, PEK:PEK + 32]), reads=["VT"], writes=["VTb"])
                kvn = ["cmp_k_p", "cmp_v_p", "slc_k_p", "slc_v_p", "win_k_p", "win_v_p"]
                for ti, (t0, n) in enumerate(TILES):
                    self.rmsnorm(ti, t0, n, NM1)
                    xk = [("xn", kk) for kk in range(8)]
                    for cb in range(3):
                        wv, wk = self.wload(win[:, 1024 + cb * 512:1024 + (cb + 1) * 512], 8, 512)
                        for r in range(4):
                            rt = ti * 4 + r
                            ps, pk = f.pget()
                            for k in range(8):
                                f.op('pe', lambda en, k=k, r=r, ps=ps: en.matmul(ps[:, 0:512], lhsT=self.xn[:, k, r * 128:(r + 1) * 128],
                                                                                  rhs=wv[:, k, :], start=(k == 0), stop=(k == 7)),
                                     reads=[wk] + xk, writes=[pk])
                            if cb < 2 or ti == 3:
                                s_, sk_ = self.scr()
                                self.copy('act', s_[:, 0:512], ps[:, 0:512], [pk], [sk_])
                                for hh in range(2):
                                    nm = kvn[2 * cb + hh]
                                    row0 = (t0 + r * 128) if cb < 2 else r * 128
                                    f.dma(self.O[nm][row0:row0 + 128, :], s_[:, hh * 256:(hh + 1) * 256], reads=[sk_])
                            if ti == 3 and r == 0:
                                ps4, pk4 = f.pget()
                                for k in range(8):
                                    f.op('pe', lambda en, k=k, ps4=ps4: en.matmul(ps4[0:NS, 0:512], lhsT=self.xn[:, k, 512:516],
                                                                                  rhs=wv[:, k, :], start=(k == 0), stop=(k == 7)),
                                         reads=[wk] + xk, writes=[pk4])
                                s4, sk4 = self.scr()
                                self.copy('dve', s4[0:NS, 0:512], ps4[0:NS, 0:512], [pk4], [sk4])
                                if cb < 2:
                                    for hh in range(2):
                                        f.dma(self.O[["cmp_k_s", "cmp_v_s", "slc_k_s", "slc_v_s"][2 * cb + hh]][:, :],
                                              s4[0:NS, hh * 256:(hh + 1) * 256], reads=[sk4])
                                else:
                                    for hh, (onm, inm) in enumerate([("win_k_s", "swk"), ("win_v_s", "swv")]):
                                        for b in range(NS):
                                            f.dma(self.O[onm][b * 512 + 511:b * 512 + 512, :], s4[b:b + 1, hh * 256:(hh + 1) * 256], reads=[sk4])
                                            f.dma(self.O[onm][b * 512:b * 512 + 511, :], I[inm][b * 512 + 1:b * 512 + 512, :])
                            if cb >= 1:
                                f.op('dve', lambda en, ps=ps, rt=rt, cb=cb: en.tensor_copy(
                                    out=VA[:, rt, cb - 1, :, 0:64], in_=ps[:, 256:512].rearrange("p (g d) -> p g d", g=4)),
                                    reads=[pk], writes=["VA"])
                    for ty, base in enumerate([1024, 1280, 1536, 2048]):
                        if self.stage < 2.2:
                            break
                        def cons(m, outs, ty=ty, t0=t0):
                            for (ps, pk, c0, cn) in outs:
                                if c0 != 0:
                                    continue
                                dst = CT[:, ty, m, t0:t0 + 512] if ty < 2 else KT[:, ty - 2, m, t0:t0 + 512]
                                key = ("CT", ty) if ty < 2 else ("KT", ty - 2)
                                self.copy(self.ev_engine(), dst, ps[:, 0:512], [pk], [key])
                        self.linear(win[:, base:base + 256], 8, 256, lambda k: self.xn[:, k, :], xk, 512, cons, bw=256)
                for ci in range(2):
                    if self.stage < 2.3:
                        break
                    pss = [f.pget(), f.pget()]
                    for hf in range(2):
                        ps, pk = pss[hf]
                        for r in range(16):
                            t = 2 * r + hf
                            f.op('pe', lambda en, t=t, r=r, hf=hf, ps=ps, ci=ci: en.matmul(
                                ps[:, 0:1], lhsT=W1r[hf * 64:(hf + 1) * 64, ci, t, :], rhs=VTb[hf * 64:(hf + 1) * 64, ci * 16 + r:ci * 16 + r + 1],
                                start=(r == 0), stop=(r == 15)), reads=["W1r", "VTb"], writes=[pk])
                    self.copy('dve', hb[:, ci:ci + 1], pss[0][0][:, 0:1], [pss[0][1]], ["hb"])
                    f.op('dve', lambda en, ci=ci: en.tensor_tensor(out=hb[:, ci:ci + 1], in0=pss[1][0][:, 0:1], in1=hb[:, ci:ci + 1], op=ALU.add),
                         reads=[pss[1][1], "hb"], writes=["hb"])
                for ci in range(2):
                    if self.stage < 2.4:
                        break
                    for g in range(4):
                        hf, sl = g % 2, g // 2
                        P0 = hf * 64
                        ps, pk = f.pget()
                        for t in range(32):
                            f.op('pe', lambda en, t=t, ps=ps, ci=ci, P0=P0, sl=sl: en.matmul(
                                ps[:, 0:127], lhsT=W1r[P0:P0 + 64, ci, t, :], rhs=CT[P0:P0 + 64, ci, sl, t:t + 16 * 126 + 1:16],
                                start=(t == 0), stop=(t == 31)), reads=["W1r", ("CT", ci)], writes=[pk])
                        f.op('act', lambda en, ps=ps, ci=ci: en.activation(out=shb[:, 0:127], in_=ps[:, 0:127], func=AF.Silu, bias=hb[:, ci:ci + 1]),
                             reads=[pk, "hb"], writes=["shb"])
                        ps2, pk2 = f.pget()
                        if ci == 0:
                            f.op('pe', lambda en, ps2=ps2: en.matmul(ps2[:, 0:127], lhsT=W2k[:, :], rhs=shb[:, 0:127], start=True, stop=True),
                                 reads=["W2k", "shb"], writes=[pk2])
                            self.copy('dve', KCT[P0:P0 + 64, sl, 0:127], ps2[P0:P0 + 64, 0:127], [pk2], ["KCT"])
                        else:
                            f.op('pe', lambda en, ps2=ps2: en.matmul(ps2[0:127, 0:64], lhsT=shb[:, 0:127], rhs=W2v[:, :], start=True, stop=True),
                                 reads=["W2v", "shb"], writes=[pk2])
                            self.copy('dve', VCM[0:127, g, :], ps2[0:127, 0:64], [pk2], ["VCM"])
            if self.stage < 3:
                return
            with self.scope():
                TabAug = f.sb("TabAug", [34, 16])
                BTd = f.sb("BTd", [128, 16, 128], BF16)
                BTo = f.sb("BTo", [128, 16, 128], BF16)
                BTt = f.sb("BTt", [128, 128], BF16)
                Gtab = f.sb("Gtab", [128, 16, 254], BF16)
                FV = f.sb("FV", [128, 16, 32])
                Ex = f.sb("Ex", [32, 16, 128], BF16)
                t31 = f.sb("t31", [1, 16])
                C31 = f.sb("C31", [1, 16, 128], BF16)
                onesb = f.sb("onesb", [1, 128], BF16)
                qT = f.sb("qT", [128, 16, 512], BF16)
                G = f.sb("G", [128, 4, 48])
                Oall = f.sb("Oall", [128, 16, 64])
                scm = f.sb("scm", [128, 4, 127])
                ee = f.sb("ee", [128, 4, 127])
                sm = f.sb("sm", [128, 64])
                PG = f.sb("PG", [128, 132])
                sco = f.sb("sco", [128, 96])
                NST = f.sb("NST", [32, 128], BF16)
                pT = f.sb("pT", [128, 512], BF16)
                PT = [f.sb("PT%d" % i, [128, 512], BF16) for i in range(2)]
                tmpo = f.sb("tmpo", [128, 4, 64])
                f.dma(TabAug[0:32, :], I["rel_table"], writes=["TabAug"])
                f.op('pool', lambda en: en.memset(TabAug[32:33, :], NEG), writes=["TabAug"])
                f.dma(TabAug[33:34, :], I["rel_table"][31:32, :], writes=["TabAug"])
                f.dma(t31[0:1, :], I["rel_table"][31:32, :], writes=["t31"])
                f.op('dve', lambda en: en.tensor_copy(out=C31[0:1, :, :], in_=t31[0:1, :][:, :, None].to_broadcast([1, 16, 128])),
                     reads=["t31"], writes=["C31"])
                f.op('pool', lambda en: en.memset(onesb[0:1, :], 1.0), writes=["onesb"])
                f.op('pool', lambda en: en.memset(PG[:, :], 0.0), writes=["PG"])
                f.dma(FV[:, :, :], I["fvtab"].rearrange("p (a b) -> p a b", a=16), writes=["FV"])
                f.dma(Ex[:, :, :], I["exm"].rearrange("p (a b) -> p a b", a=16), writes=["Ex"], q='pool')
                f.dma(BTt[:, :], I["tailm"], writes=["BTt"], q='pool')
                self.gen_table(I["ohd"], 16384, TabAug, None)
                f.dma(BTd[:, :, :], self.bsc[:, 0:16384].rearrange("h (k q) -> k h q", k=128), reads=["bsc"], writes=["BTd"], q='pool')
                self.gen_table(I["oho"], 16384, TabAug, None)
                f.dma(BTo[:, :, :], self.bsc[:, 0:16384].rearrange("h (k q) -> k h q", k=128), reads=["bsc"], writes=["BTo"], q='pool')
                self.gen_table(I["ohg"], 32512, TabAug, None)
                f.dma(Gtab[:, :, :], self.bsc[:, 0:32512].rearrange("h (q m) -> q h m", q=128), reads=["bsc"], writes=["Gtab"], q='pool')
                f.op('pool', lambda en: en.memset(qT[:, :, :], 0.0), writes=["qT"])

                for ti, (t0, n) in enumerate(TILES):
                    if self.stage < 4:
                        break
                    self.rmsnorm(ti, t0, n, NM1)
                    xk = [("xn", kk) for kk in range(8)]
                    for half in range(2):
                        wi_ = self.wi % 3
                        self.wi += 1
                        wb_, wqk = self.wbuf[wi_], "wb%d" % wi_
                        wq5 = wb_[:, 0:4096].rearrange("p (k s u i) -> p k s u i", k=8, s=4, u=2)
                        src5 = win[:, half * 512:(half + 1) * 512].rearrange("(k p) (u s i) -> p k u s i", p=128, u=2, s=4)
                        for u in range(2):
                            for s4 in range(4):
                                f.dma(wq5[:, :, s4, u, :], src5[:, :, u, s4, :], writes=[wqk], q='pool')
                        wq = wb_[:, 0:4096].rearrange("p (k m) -> p k m", k=8)
                        for sl4 in range(4):
                            s_ = half * 4 + sl4
                            A = sl4
                            ps, pk = f.pget()
                            for k in range(8):
                                f.op('pe', lambda en, k=k, A=A, ps=ps: en.matmul(
                                    ps[:, 0:512],
                                    lhsT=wq[:, k, 128 * A:128 * A + 128],
                                    rhs=self.xn[:, k, 0:512], start=(k == 0), stop=(k == 7)), reads=[wqk] + xk, writes=[pk])
                            gA = 2 * half
                            f.op('act', lambda en, ps=ps, gA=gA, A=A: en.activation(out=qT[0:64, 4 * gA + A, :], in_=ps[0:64, 0:512], func=AF.Identity, scale=0.125),
                                 reads=[pk], writes=["qT"])
                            f.op('act', lambda en, ps=ps, gA=gA, A=A: en.activation(out=qT[64:128, 4 * (gA + 1) + A, :], in_=ps[64:128, 0:512], func=AF.Identity, scale=0.125),
                                 reads=[pk], writes=["qT"])
                    wg, wgk = self.wload(win[:, 2560:2608], 8, 48)
                    for r in range(4):
                        ps, pk = f.pget()
                        for k in range(8):
                            f.op('pe', lambda en, k=k, r=r, ps=ps: en.matmul(ps[:, 0:48], lhsT=self.xn[:, k, r * 128:(r + 1) * 128], rhs=wg[:, k, :],
                                                                              start=(k == 0), stop=(k == 7)), reads=[wgk] + xk, writes=[pk])
                        f.op('act', lambda en, ps=ps, r=r: en.activation(out=G[:, r, :], in_=ps[:, 0:48], func=AF.Sigmoid), reads=[pk], writes=["G"])
                    OT = self.xn
                    for r in range(4):
                        qt = ti * 4 + r
                        qs = slice(r * 128, (r + 1) * 128)
                        for g in range(4):
                            hf, sl = g % 2, g // 2
                            P0 = hf * 64
                            s0 = 4 * sl
                            ps, pk = f.pget()
                            for j in range(4):
                                f.op('pe', lambda en, j=j, ps=ps: en.matmul(ps[:, j * 127:(j + 1) * 127], lhsT=qT[:, 4 * g + j, qs],
                                                                            rhs=KCT[:, sl, 0:127], start=True, stop=True),
                                     reads=["qT", "KCT"], writes=[pk])
                            f.op('dve', lambda en, ps=ps: en.tensor_tensor(out=scm[:, :, :], in0=ps[:, 0:508].rearrange("p (h j) -> p h j", h=4),
                                                                           in1=Gtab[:, 4 * g:4 * g + 4, 127 - 8 * qt:254 - 8 * qt], op=ALU.add),
                                 reads=[pk, "Gtab"], writes=["scm"])
                            f.op('dve', lambda en: en.tensor_reduce(out=sm[:, 0:4], in_=scm[:, :, :], axis=AX.X, op=ALU.max), reads=["scm"], writes=["sm"])
                            f.op('dve', lambda en: en.tensor_scalar(out=sm[:, 4:8], in0=sm[:, 0:4], scalar1=-10000.0, scalar2=-1.0, op0=ALU.max, op1=ALU.mult),
                                 reads=["sm"], writes=["sm"])
                            f.op('dve', lambda en: en.memset(sm[:, 8:12], 0.0), reads=["sm"], writes=["sm"])
                            for j in range(4):
                                f.op('act', lambda en, j=j: en.activation(out=ee[:, j, :], in_=scm[:, j, :], func=AF.Exp, bias=sm[:, 4 + j:5 + j],
                                                                         accum_out=sm[:, 8 + j:9 + j]), reads=["scm", "sm"], writes=["ee", "sm"])
                            f.op('dve', lambda en: en.tensor_scalar(out=sm[:, 12:16], in0=sm[:, 8:12], scalar1=1e-30, scalar2=None, op0=ALU.max),
                                 reads=["sm"], writes=["sm"])
                            f.op('dve', lambda en: en.reciprocal(out=sm[:, 12:16], in_=sm[:, 12:16]), reads=["sm"], writes=["sm"])
                            f.op('dve', lambda en: en.tensor_tensor(out=ee[:, :, :], in0=ee[:, :, :], in1=sm[:, 12:16][:, :, None].to_broadcast([128, 4, 127]),
                                                                    op=ALU.mult), reads=["ee", "sm"], writes=["ee"])
                            f.op('dve', lambda en: en.tensor_reduce(out=PG[:, 1:128], in_=ee[:, :, :].rearrange("p h j -> p j h"), axis=AX.X, op=ALU.add),
                                 reads=["ee"], writes=["PG"])
                            f.op('dve', lambda en: en.tensor_tensor(out=sco[:, 0:32], in0=PG[:, 1:129:4], in1=PG[:, 2:130:4], op=ALU.add), reads=["PG"], writes=["sco"])
                            f.op('dve', lambda en: en.tensor_tensor(out=sco[:, 0:32], in0=sco[:, 0:32], in1=PG[:, 3:131:4], op=ALU.add), reads=["PG", "sco"], writes=["sco"])
                            f.op('dve', lambda en: en.scalar_tensor_tensor(out=sco[:, 0:32], in0=sco[:, 0:32], scalar=2.0, in1=PG[:, 0:128:4],
                                                                           op0=ALU.mult, op1=ALU.add), reads=["PG", "sco"], writes=["sco"])
                            f.op('dve', lambda en: en.tensor_tensor(out=sco[:, 0:32], in0=sco[:, 0:32], in1=PG[:, 4:132:4], op=ALU.add), reads=["PG", "sco"], writes=["sco"])
                            f.op('dve', lambda en: en.tensor_tensor(out=sco[:, 0:32], in0=sco[:, 0:32], in1=FV[:, qt, :], op=ALU.add), reads=["FV", "sco"], writes=["sco"])
                            f.op('dve', lambda en: en.max(out=sm[:, 16:24], in_=sco[:, 0:32]), reads=["sco"], writes=["sm"])
                            f.op('dve', lambda en: en.match_replace(out=sco[:, 32:64], in_to_replace=sm[:, 16:24], in_values=sco[:, 0:32], imm_value=-1e30),
                                 reads=["sco", "sm"], writes=["sco"])
                            f.op('dve', lambda en: en.max(out=sm[:, 24:32], in_=sco[:, 32:64]), reads=["sco"], writes=["sm"])
                            f.op('dve', lambda en: en.tensor_scalar(out=sco[:, 64:96], in0=sco[:, 0:32], scalar1=sm[:, 31:32], scalar2=None, op0=ALU.is_ge),
                                 reads=["sco", "sm"], writes=["sco"])
                            f.op('dve', lambda en: en.tensor_scalar(out=sco[:, 64:96], in0=sco[:, 64:96], scalar1=-1.0, scalar2=-NEG, op0=ALU.add, op1=ALU.mult),
                                 reads=["sco"], writes=["sco"])
                            ps, pk = f.pget()
                            f.op('pe', lambda en, ps=ps: en.transpose(ps[0:32, 0:128], sco[:, 64:96], self.ids[:, :]), reads=["sco", "ids"], writes=[pk])
                            self.copy('act', NST[0:32, :], ps[0:32, 0:128], [pk], ["NST"])
                            ps, pk = f.pget()
                            for j in range(4):
                                f.op('pe', lambda en, j=j, ps=ps: en.transpose(ps[0:127, j * 128:(j + 1) * 128], ee[:, j, :], self.ids[:, :]),
                                     reads=["ee", "ids"], writes=[pk])
                            self.copy('act', pT[0:127, :], ps[0:127, 0:512], [pk], ["pT"])
                            ps, pk = f.pget()
                            for j in range(4):
                                f.op('pe', lambda en, j=j, ps=ps: en.matmul(ps[:, j * 64:(j + 1) * 64], lhsT=pT[0:127, j * 128:(j + 1) * 128],
                                                                            rhs=VCM[0:127, g, :], start=True, stop=True), reads=["pT", "VCM"], writes=[pk])
                            f.op('dve', lambda en, ps=ps: en.tensor_tensor(out=Oall[:, 4 * g:4 * g + 4, :], in0=ps[:, 0:256].rearrange("p (h d) -> p h d", h=4),
                                                                           in1=G[:, r, 4 * g:4 * g + 4][:, :, None].to_broadcast([128, 4, 64]), op=ALU.mult),
                                 reads=[pk, "G"], writes=[("Oall", g)])
                            for br in range(2):
                                if self.stage < 5 + br:
                                    break
                                pa, pak = f.pacc(br)
                                kts = list(range(0, qt + 1)) if br == 0 else list(range(max(0, qt - 4), qt + 1))
                                for kt in kts:
                                    ps, pk = f.pget()
                                    special = (kt == qt) or (kt == qt - 1) or (br == 1 and kt == qt - 4)
                                    rd = [("KT", br), "qT", "C31", "onesb"]
                                    f.op('pe', lambda en, ps=ps, kt=kt, br=br: en.matmul(ps[:, 0:512], lhsT=KT[:, br, sl, kt * 128:(kt + 1) * 128],
                                                                                       rhs=qT[:, 4 * g:4 * g + 4, qs], start=True, stop=False),
                                         reads=rd, writes=[pk])
                                    if br == 0:
                                        f.op('pe', lambda en, ps=ps, kt=kt: en.matmul(ps[:, 0:512], lhsT=Ex[0:32, kt, :],
                                                                                    rhs=NST[0:32, :][:, None, :].to_broadcast([32, 4, 128]), start=False, stop=False),
                                             reads=["Ex", "NST"], writes=[pk])
                                    f.op('pe', lambda en, ps=ps, sp_=special: en.matmul(ps[:, 0:512], lhsT=onesb[0:1, :], rhs=C31[0:1, 4 * g:4 * g + 4, :],
                                                                                       start=False, stop=(not sp_)), reads=rd, writes=[pk])
                                    if kt == qt:
                                        f.op('pe', lambda en, ps=ps: en.matmul(ps[:, 0:512], lhsT=self.idb[:, :], rhs=BTd[:, 4 * g:4 * g + 4, :], start=False, stop=True),
                                             reads=["idb", "BTd"], writes=[pk])
                                    elif kt == qt - 1:
                                        f.op('pe', lambda en, ps=ps: en.matmul(ps[:, 0:512], lhsT=self.idb[:, :], rhs=BTo[:, 4 * g:4 * g + 4, :], start=False, stop=True),
                                             reads=["idb", "BTo"], writes=[pk])
                                    elif br == 1 and kt == qt - 4:
                                        f.op('pe', lambda en, ps=ps: en.matmul(ps[:, 0:512], lhsT=self.idb[:, :],
                                                                               rhs=BTt[:, :][:, None, :].to_broadcast([128, 4, 128]), start=False, stop=True),
                                             reads=["idb", "BTt"], writes=[pk])
                                    pt_, ptk = PT[kt % 2], "PT%d" % (kt % 2)
                                    f.op('act', lambda en, ps=ps, pt_=pt_: en.activation(out=pt_[:, :], in_=ps[:, 0:512], func=AF.Exp), reads=[pk], writes=[ptk])
                                    for j in range(4):
                                        f.op('pe', lambda en, j=j, pt_=pt_, kt=kt, br=br, kts=kts: en.matmul(
                                            pa[:, j * 65:(j + 1) * 65], lhsT=pt_[:, j * 128:(j + 1) * 128], rhs=VA[:, kt, br, g, :],
                                            start=(kt == kts[0] and j == 0), stop=(kt == kts[-1]), skip_group_check=True), reads=[ptk, "VA"], writes=[pak])
                                pav = pa[:, 0:260].rearrange("p (h d) -> p h d", h=4)
                                f.op('dve', lambda en, pav=pav: en.tensor_scalar(out=sm[:, 32:36], in0=pav[:, :, 64], scalar1=1e-30, scalar2=None, op0=ALU.max),
                                     reads=[pak], writes=["sm"])
                                f.op('dve', lambda en: en.reciprocal(out=sm[:, 32:36], in_=sm[:, 32:36]), reads=["sm"], writes=["sm"])
                                f.op('dve', lambda en, br=br: en.tensor_tensor(out=sm[:, 32:36], in0=sm[:, 32:36], in1=G[:, r, 16 * (br + 1) + 4 * g:16 * (br + 1) + 4 * g + 4],
                                                                               op=ALU.mult), reads=["sm", "G"], writes=["sm"])
                                f.op('dve', lambda en, pav=pav: en.tensor_tensor(out=tmpo[:, :, :], in0=pav[:, :, 0:64],
                                                                                 in1=sm[:, 32:36][:, :, None].to_broadcast([128, 4, 64]), op=ALU.mult),
                                     reads=[pak, "sm"], writes=["tmpo"])
                                f.op('pool', lambda en: en.tensor_tensor(out=Oall[:, 4 * g:4 * g + 4, :], in0=Oall[:, 4 * g:4 * g + 4, :], in1=tmpo[:, :, :], op=ALU.add),
                                     reads=["tmpo", ("Oall", g)], writes=[("Oall", g)])
                        O2 = Oall[:, :, :].rearrange("p h d -> p (h d)")
                        for c0 in range(0, 8, 4):
                            ps, pk = f.pget()
                            for j in range(4):
                                f.op('pe', lambda en, j=j, c0=c0, ps=ps: en.transpose(ps[:, j * 128:(j + 1) * 128], O2[:, (c0 + j) * 128:(c0 + j + 1) * 128], self.ids[:, :]),
                                     reads=[("Oall", gg) for gg in range(4)] + ["ids"], writes=[pk])
                            self.copy(self.ev_engine(), OT[:, c0:c0 + 4, qs], ps[:, 0:512].rearrange("p (c t) -> p c t", c=4), [pk],
                                      [("xn", cc) for cc in range(c0, c0 + 4)])
                    def cons_o(m, outs, ti=ti, t0=t0):
                        for (ps, pk, c0, cn) in outs:
                            f.op('dve', lambda en, ps=ps: en.tensor_tensor(out=X[:, m, t0 + c0:t0 + c0 + cn], in0=ps[:, 0:cn],
                                                                           in1=X[:, m, t0 + c0:t0 + c0 + cn], op=ALU.add),
                                 reads=[pk, ("X", m, ti)], writes=[("X", m, ti)])
                    self.linear(I["w_out"], 8, D, lambda k: OT[:, k, :], [("xn", kk) for kk in range(8)], 512, cons_o)


    def gather_KT_V(self, pool, IDX, b, page0, npages, CT, ci, VAs=None, plain=None):
        f = self.f
        for r0 in range(0, npages, 4):
            i = self.si % 2
            self.si += 1
            st, sk = self.stg[i], "stg%d" % i
            for j in range(4):
                pg_ = page0 + r0 + j
                if plain is not None:
                    f.dma(st[:, j * 256:(j + 1) * 256], plain[(r0 + j) * 128:(r0 + j + 1) * 128, :], writes=[sk])
                else:
                    f.dma(None, None, reads=["IDX"], writes=[sk], q='pool',
                          fn=lambda en, j=j, pg_=pg_, st=st: en.indirect_dma_start(
                              out=st[:, j * 256:(j + 1) * 256], out_offset=None, in_=pool[:, :],
                              in_offset=bass.IndirectOffsetOnAxis(ap=IDX[:, b, pg_:pg_ + 1], axis=0)))
            if VAs is not None:
                f.op('act', lambda en, st=st, r0=r0: en.activation(
                    out=VAs[:, r0:r0 + 4, :, 0:64], in_=st[:, 0:1024].rearrange("p (r g d) -> p r g d", r=4, g=4), func=AF.Identity),
                    reads=[sk], writes=["VAs"])
                continue
            for gp in range(2):
                ps, pk = f.pget()
                for j in range(4):
                    f.op('pe', lambda en, j=j, gp=gp, ps=ps, st=st: en.transpose(
                        ps[:, j * 128:(j + 1) * 128], st[:, j * 256 + gp * 128:j * 256 + (gp + 1) * 128], self.ids[:, :]),
                        reads=[sk, "ids"], writes=[pk])
                self.copy(self.ev_engine(), CT[:, ci, gp, r0 * 128:(r0 + 4) * 128], ps[:, 0:512], [pk], [("CT", ci)])

    def nsa_samples(self):
        f = self.f
        I = self.I
        X = self.X
        VT = self.VT
        win = I["w_in"]
        with self.scope():
            CT = f.sb("CTs", [128, 2, 2, T], BF16)
            CTx = f.sb("CTx", [128, 2, 2, 7, 32], BF16)
            W1r = f.sb("W1rs", [128, 2, 32, 128], BF16)
            W2k = f.sb("W2ks", [128, 128], BF16)
            W2v = f.sb("W2vs", [128, 64], BF16)
            VTb = f.sb("VTbs", [128, 32], BF16)
            hb = f.sb("hbs", [128, 2])
            shb = f.sb("shbs", [128, 128], BF16)
            KCTs = f.sb("KCTs", [128, 2, 1024], BF16)
            VCMs = f.sb("VCMs", [128, 8, 4, 64], BF16)
            VAs = f.sb("VAs", [128, 16, 4, 65], BF16)
            PTi = f.sb("PTi", [128, NS, 128], I32)
            IDX = f.sb("IDX", [128, NS, 128], I32)
            iop = f.sb("iop", [128, 1], I32)
            iopf = f.sb("iopf", [128, 1])
            IDXf = self.sc[0][:, 0:NS * 128] if False else None
            QZS = f.sb("QZS", [128, NS, 16], BF16)
            KTn = f.sb("KTn", [128, 2, 2, NS], BF16)
            VN = f.sb("VN", [1, NS, 2, 4, 65], BF16)
            gT = f.sb("gT", [48, NS])
            Jm = f.sb("Jm", [48, 4])
            Rm = f.sb("Rm", [48, 12])
            Lg = f.sb("Lg", [48, 4])
            G4 = f.sb("G4", [4, 12])
            TabAug = f.sb("TabAugs", [34, 16])
            OHs = f.sb("OHs", [34, 1024])
            OH127 = f.sb("OH127", [34, 128])
            B127 = f.sb("B127", [128, 16], BF16)
            trow = f.sb("trow", [1, 32])
            C31s = f.sb("C31s", [1, 16], BF16)
            T0s = f.sb("T0s", [1, 16], BF16)
            onesb = f.sb("onesbs", [1, 128], BF16)
            E2 = f.sb("E2", [33, 128], BF16)
            NS2 = f.sb("NS2", [33, 4, 128], BF16)
            FVs = f.sb("FVs", [1, 256])
            scs = f.sb("scs", [4, 1024])
            ees = f.sb("ees", [4, 1024])
            bss = f.sb("bss", [4, 1024])
            sms = f.sb("sms", [4, 16])
            PGs = f.sb("PGs", [1, 1032])
            sco = f.sb("scos", [1, 784])
            NSr = f.sb("NSr", [1, 384], BF16)
            pTs = f.sb("pTs", [128, 8, 4], BF16)
            Ps = f.sb("Ps", [128, 64], BF16)
            Pn = f.sb("Pn", [1, 4], BF16)
            Ob = f.sb("Ob", [4, 3, 4, 64])
            Osum = f.sb("Osum", [4, 4, 128])
            tmpo = f.sb("tmpos", [4, 4, 64])
            shx = f.sb("shx", [128, 8], BF16)
            vx = f.sb("vx", [8, 64], BF16)
            OTs = f.sb("OTs", [128, 8, NS], BF16)
            for ci, nm in enumerate(["ck_w1", "cv_w1"]):
                src = I[nm].rearrange("(t h) m -> h t m", h=64)
                f.dma(W1r[0:64, ci], src, writes=["W1r"], q='pool')
                f.dma(W1r[64:128, ci], src, writes=["W1r"], q='pool')
            f.dma(W2k[:, 0:64], I["ck_w2"], writes=["W2k"], q='pool')
            f.dma(W2k[:, 64:128], I["ck_w2"], writes=["W2k"], q='pool')
            f.dma(W2v[:, :], I["cv_w2"], writes=["W2v"], q='pool')
            f.op('dve', lambda en: en.tensor_copy(out=VTb[:, :], in_=VT[:, PEK:PEK + 32]), reads=["VT"], writes=["VTb"])
            for ci in range(2):
                pss = [f.pget(), f.pget()]
                for hf in range(2):
                    ps, pk = pss[hf]
                    for r in range(16):
                        t = 2 * r + hf
                        f.op('pe', lambda en, t=t, r=r, hf=hf, ps=ps, ci=ci: en.matmul(
                            ps[:, 0:1], lhsT=W1r[hf * 64:(hf + 1) * 64, ci, t, :], rhs=VTb[hf * 64:(hf + 1) * 64, ci * 16 + r:ci * 16 + r + 1],
                            start=(r == 0), stop=(r == 15)), reads=["W1r", "VTb"], writes=[pk])
                self.copy('dve', hb[:, ci:ci + 1], pss[0][0][:, 0:1], [pss[0][1]], ["hb"])
                f.op('dve', lambda en, ci=ci: en.tensor_tensor(out=hb[:, ci:ci + 1], in0=pss[1][0][:, 0:1], in1=hb[:, ci:ci + 1], op=ALU.add),
                     reads=[pss[1][1], "hb"], writes=["hb"])
            f.dma(TabAug[0:32, :], I["rel_table"], writes=["TabAug"])
            f.op('pool', lambda en: en.memset(TabAug[32:33, :], NEG), writes=["TabAug"])
            f.dma(TabAug[33:34, :], I["rel_table"][31:32, :], writes=["TabAug"])
            f.dma(OHs[:, :], I["ohs"], writes=["OHs"])
            f.dma(OH127[:, :], I["oh127"], writes=["OH127"])
            f.dma(trow[0:1, 0:16], I["rel_table"][31:32, :], writes=["trow"])
            f.dma(trow[0:1, 16:32], I["rel_table"][0:1, :], writes=["trow"])
            f.op('dve', lambda en: en.tensor_copy(out=C31s[0:1, :], in_=trow[0:1, 0:16]), reads=["trow"], writes=["C31s"])
            f.op('dve', lambda en: en.tensor_copy(out=T0s[0:1, :], in_=trow[0:1, 16:32]), reads=["trow"], writes=["T0s"])
            f.op('pool', lambda en: en.memset(onesb[0:1, :], 1.0), writes=["onesb"])
            f.dma(E2[:, :], I["e2"], writes=["E2"], q='pool')
            f.dma(FVs[:, :], I["fvs"], writes=["FVs"])
            f.dma(Jm[:, :], I["jm"], writes=["Jm"])
            f.dma(Rm[:, :], I["rm"], writes=["Rm"])
            f.op('pool', lambda en: en.memset(NS2[:, :, :], 0.0), writes=["NS2"])
            f.op('pool', lambda en: en.memset(PGs[:, :], 0.0), writes=["PGs"])
            f.op('pool', lambda en: en.memset(VAs[:, :, :, :], 1.0), writes=["VAs"])
            f.op('pool', lambda en: en.memset(VN[:, :, :, :, :], 1.0), writes=["VN"])
            f.op('pool', lambda en: en.memset(QZS[:, :, :], 0.0), writes=["QZS"])
            f.op('pool', lambda en: en.memset(Osum[:, :, :], 0.0), writes=["Osum"])
            ps, pk = f.pget()
            f.op('pe', lambda en, ps=ps: en.matmul(ps[:, 0:16], lhsT=OH127[0:34, :], rhs=TabAug[0:34, :], start=True, stop=True),
                 reads=["OH127", "TabAug"], writes=[pk])
            self.copy('dve', B127[:, :], ps[:, 0:16], [pk], ["B127"])
            f.dma(PTi[:, :, :].rearrange("p a b -> p (a b)"), I["ptab"].partition_broadcast(128), writes=["PTi"])
            f.op('pool', lambda en: en.iota(iop[:, 0:1], pattern=[[0, 1]], base=0, channel_multiplier=1), writes=["iop"])
            IDXf, _ = self.scr()
            IDXf = IDXf[:, 0:NS * 128]
            f.op('dve', lambda en: en.tensor_copy(out=iopf[:, 0:1], in_=iop[:, 0:1]), reads=["iop"], writes=["iopf"])
            f.op('dve', lambda en: en.tensor_copy(out=IDXf[:, :], in_=PTi[:, :, :].rearrange("p a b -> p (a b)")), reads=["PTi"], writes=["sc0", "sc1", "sc2"])
            f.op('dve', lambda en: en.tensor_scalar(out=IDXf[:, :], in0=IDXf[:, :], scalar1=128.0, scalar2=iopf[:, 0:1], op0=ALU.mult, op1=ALU.add),
                 reads=["iopf", "sc0", "sc1", "sc2"], writes=["sc0", "sc1", "sc2"])
            f.op('dve', lambda en: en.tensor_copy(out=IDX[:, :, :].rearrange("p a b -> p (a b)"), in_=IDXf[:, :]), reads=["sc0", "sc1", "sc2"], writes=["IDX"])
            self.rmsnorm(3, 1536, 516, NM1)
            xk = [("xn", kk) for kk in range(8)]
            xs_ = lambda k: self.xn[:, k, 512:516]
            for half in range(2):
                wi_ = self.wi % 3
                self.wi += 1
                wb_, wqk = self.wbuf[wi_], "wb%d" % wi_
                wq5 = wb_[:, 0:4096].rearrange("p (k s u i) -> p k s u i", k=8, s=4, u=2)
                src5 = win[:, half * 512:(half + 1) * 512].rearrange("(k p) (u s i) -> p k u s i", p=128, u=2, s=4)
                for u in range(2):
                    for s4 in range(4):
                        f.dma(wq5[:, :, s4, u, :], src5[:, :, u, s4, :], writes=[wqk], q='pool')
                wq = wb_[:, 0:4096].rearrange("p (k m) -> p k m", k=8)
                for A in range(4):
                    ps, pk = f.pget()
                    for k in range(8):
                        f.op('pe', lambda en, k=k, A=A, ps=ps: en.matmul(ps[:, 0:NS], lhsT=wq[:, k, 128 * A:128 * A + 128], rhs=xs_(k),
                                                                         start=(k == 0), stop=(k == 7)), reads=[wqk] + xk, writes=[pk])
                    gA = 2 * half
                    f.op('act', lambda en, ps=ps, gA=gA, A=A: en.activation(out=QZS[0:64, :, 4 * gA + A], in_=ps[0:64, 0:NS], func=AF.Identity, scale=0.125),
                         reads=[pk], writes=["QZS"])
                    f.op('act', lambda en, ps=ps, gA=gA, A=A: en.activation(out=QZS[64:128, :, 4 * (gA + 1) + A], in_=ps[64:128, 0:NS], func=AF.Identity, scale=0.125),
                         reads=[pk], writes=["QZS"])
            wg, wgk = self.wload(win[:, 2560:2608], 8, 48)
            ps, pk = f.pget()
            for k in range(8):
                f.op('pe', lambda en, k=k, ps=ps: en.matmul(ps[0:48, 0:NS], lhsT=wg[:, k, 0:48], rhs=xs_(k), start=(k == 0), stop=(k == 7)),
                     reads=[wgk] + xk, writes=[pk])
            f.op('act', lambda en, ps=ps: en.activation(out=gT[:, :], in_=ps[0:48, 0:NS], func=AF.Sigmoid), reads=[pk], writes=["gT"])
            for ty, base in enumerate([1536, 2048]):
                wv, wk = self.wload(win[:, base:base + 512], 8, 512)
                for m in range(2):
                    ps, pk = f.pget()
                    for k in range(8):
                        f.op('pe', lambda en, k=k, m=m, ps=ps: en.matmul(ps[:, 0:NS], lhsT=wv[:, k, m * 128:(m + 1) * 128], rhs=xs_(k),
                                                                         start=(k == 0), stop=(k == 7)), reads=[wk] + xk, writes=[pk])
                    self.copy('dve', KTn[:, ty, m, :], ps[:, 0:NS], [pk], ["KTn"])
                for b in range(NS):
                    ps, pk = f.pget()
                    for k in range(8):
                        f.op('pe', lambda en, k=k, b=b, ps=ps: en.matmul(ps[0:1, 0:256], lhsT=self.xn[:, k, 512 + b:513 + b], rhs=wv[:, k, 256:512],
                                                                         start=(k == 0), stop=(k == 7)), reads=[wk] + xk, writes=[pk])
                    self.copy('dve', VN[0:1, b, ty, :, 0:64], ps[0:1, 0:256].rearrange("p (g d) -> p g d", g=4), [pk], ["VN"])

            def compress(ci, nblk, rhs_fn, kdst_fn, vdst_fn, sh):
                for g in range(4):
                    hf, sl = g % 2, g // 2
                    P0 = hf * 64
                    ps, pk = f.pget()
                    for t in range(32):
                        f.op('pe', lambda en, t=t, ps=ps, P0=P0, sl=sl: en.matmul(
                            ps[:, 0:nblk], lhsT=W1r[P0:P0 + 64, ci, t, :], rhs=rhs_fn(P0, sl, t), start=(t == 0), stop=(t == 31)),
                            reads=["W1r", ("CT", ci), "CTx"], writes=[pk])
                    f.op('act', lambda en, ps=ps: en.activation(out=sh[:, 0:nblk], in_=ps[:, 0:nblk], func=AF.Silu, bias=hb[:, ci:ci + 1]),
                         reads=[pk, "hb"], writes=["sh"])
                    ps2, pk2 = f.pget()
                    if ci == 0:
                        f.op('pe', lambda en, ps2=ps2: en.matmul(ps2[:, 0:nblk], lhsT=W2k[:, :], rhs=sh[:, 0:nblk], start=True, stop=True),
                             reads=["W2k", "sh"], writes=[pk2])
                        kdst_fn(g, P0, sl, ps2, pk2)
                    else:
                        f.op('pe', lambda en, ps2=ps2: en.matmul(ps2[0:nblk, 0:64], lhsT=sh[:, 0:nblk], rhs=W2v[:, :], start=True, stop=True),
                             reads=["W2v", "sh"], writes=[pk2])
                        vdst_fn(g, ps2, pk2)

            for b in range(NS):
                f.op('pool', lambda en: en.memset(KCTs[:, :, :], 0.0), writes=["KCTs"])
                f.op('pool', lambda en: en.memset(VCMs[:, :, :, :], 0.0), writes=["VCMs"])
                for s_ in range(8):
                    for ci, pool in enumerate([I["cck"], I["ccv"]]):
                        self.gather_KT_V(pool, IDX, b, 16 * s_, 16, CT, ci)
                        for sl in range(2):
                            if s_ < 7:
                                f.op('pool', lambda en, ci=ci, sl=sl, s_=s_: en.tensor_copy(out=CTx[:, ci, sl, s_, 0:16], in_=CT[:, ci, sl, 2032:2048]),
                                     reads=[("CT", ci)], writes=["CTx"])
                            if s_ > 0:
                                f.op('pool', lambda en, ci=ci, sl=sl, s_=s_: en.tensor_copy(out=CTx[:, ci, sl, s_ - 1, 16:32], in_=CT[:, ci, sl, 0:16]),
                                     reads=[("CT", ci)], writes=["CTx"])

                        def kd(g, P0, sl, ps2, pk2, s_=s_):
                            self.copy('dve', KCTs[P0:P0 + 64, sl, s_ * 128:s_ * 128 + 127], ps2[P0:P0 + 64, 0:127], [pk2], ["KCTs"])

                        def vd(g, ps2, pk2, s_=s_):
                            self.copy('dve', VCMs[0:127, s_, g, :], ps2[0:127, 0:64], [pk2], ["VCMs"])
                        compress(ci, 127, lambda P0, sl, t, ci=ci: CT[P0:P0 + 64, ci, sl, t:t + 16 * 126 + 1:16], kd, vd, shb)
                for ci in range(2):
                    def kdx(g, P0, sl, ps2, pk2):
                        self.copy('dve', KCTs[P0:P0 + 64, sl, 127:127 + 128 * 6 + 1:128], ps2[P0:P0 + 64, 0:7], [pk2], ["KCTs"])

                    def vdx(g, ps2, pk2):
                        self.copy('dve', vx[0:7, :], ps2[0:7, 0:64], [pk2], ["vx"])
                        for s_ in range(7):
                            f.dma(VCMs[127:128, s_, g, :], vx[s_:s_ + 1, :], reads=["vx"], writes=["VCMs"])
                    compress(ci, 7, lambda P0, sl, t, ci=ci: CTx[P0:P0 + 64, ci, sl, 0:7, t], kdx, vdx, shx)
                for g in range(4):
                    sl = g // 2
                    for c0 in (0, 512):
                        ps, pk = f.pget()
                        f.op('pe', lambda en, ps=ps, c0=c0: en.matmul(ps[0:4, 0:512], lhsT=QZS[:, b, 4 * g:4 * g + 4], rhs=KCTs[:, sl, c0:c0 + 512],
                                                                      start=True, stop=True), reads=["QZS", "KCTs"], writes=[pk])
                        pb, pbk = f.pget()
                        f.op('pe', lambda en, pb=pb, c0=c0: en.matmul(pb[0:4, 0:512], lhsT=TabAug[0:34, 4 * g:4 * g + 4], rhs=OHs[0:34, c0:c0 + 512],
                                                                      start=True, stop=True), reads=["TabAug", "OHs"], writes=[pbk])
                        self.copy('act', bss[0:4, c0:c0 + 512], pb[0:4, 0:512], [pbk], ["bss"])
                        f.op('dve', lambda en, ps=ps, c0=c0: en.tensor_tensor(out=scs[0:4, c0:c0 + 512], in0=ps[0:4, 0:512], in1=bss[0:4, c0:c0 + 512], op=ALU.add),
                             reads=[pk, "bss"], writes=["scs"])
                    f.op('dve', lambda en: en.tensor_reduce(out=sms[0:4, 0:1], in_=scs[0:4, :], axis=AX.X, op=ALU.max), reads=["scs"], writes=["sms"])
                    f.op('dve', lambda en: en.tensor_scalar(out=sms[0:4, 1:2], in0=sms[0:4, 0:1], scalar1=-10000.0, scalar2=-1.0, op0=ALU.max, op1=ALU.mult),
                         reads=["sms"], writes=["sms"])
                    f.op('dve', lambda en: en.memset(sms[0:4, 2:3], 0.0), reads=["sms"], writes=["sms"])
                    f.op('act', lambda en: en.activation(out=ees[0:4, :], in_=scs[0:4, :], func=AF.Exp, bias=sms[0:4, 1:2], accum_out=sms[0:4, 2:3]),
                         reads=["scs", "sms"], writes=["ees", "sms"])
                    f.op('dve', lambda en: en.tensor_scalar(out=sms[0:4, 3:4], in0=sms[0:4, 2:3], scalar1=1e-30, scalar2=None, op0=ALU.max), reads=["sms"], writes=["sms"])
                    f.op('dve', lambda en: en.reciprocal(out=sms[0:4, 3:4], in_=sms[0:4, 3:4]), reads=["sms"], writes=["sms"])
                    f.op('dve', lambda en: en.tensor_scalar(out=ees[0:4, :], in0=ees[0:4, :], scalar1=sms[0:4, 3:4], scalar2=None, op0=ALU.mult),
                         reads=["ees", "sms"], writes=["ees"])
                    for c0 in (0, 512):
                        ps, pk = f.pget()
                        f.op('pe', lambda en, ps=ps, c0=c0: en.matmul(ps[0:1, 0:512], lhsT=self.ones[0:4, 0:1], rhs=ees[0:4, c0:c0 + 512], start=True, stop=True),
                             reads=["ones", "ees"], writes=[pk])
                        self.copy('dve', PGs[0:1, 1 + c0:1 + c0 + 512], ps[0:1, 0:512], [pk], ["PGs"])
                    S = sco
                    f.op('dve', lambda en: en.tensor_tensor(out=S[0:1, 0:256], in0=PGs[0:1, 1:1022:4], in1=PGs[0:1, 2:1023:4], op=ALU.add), reads=["PGs"], writes=["sco"])
                    f.op('dve', lambda en: en.tensor_tensor(out=S[0:1, 0:256], in0=S[0:1, 0:256], in1=PGs[0:1, 3:1024:4], op=ALU.add), reads=["PGs", "sco"], writes=["sco"])
                    f.op('dve', lambda en: en.scalar_tensor_tensor(out=S[0:1, 0:256], in0=S[0:1, 0:256], scalar=2.0, in1=PGs[0:1, 0:1021:4], op0=ALU.mult, op1=ALU.add),
                         reads=["PGs", "sco"], writes=["sco"])
                    f.op('dve', lambda en: en.tensor_tensor(out=S[0:1, 0:256], in0=S[0:1, 0:256], in1=PGs[0:1, 4:1025:4], op=ALU.add), reads=["PGs", "sco"], writes=["sco"])
                    f.op('dve', lambda en: en.tensor_tensor(out=S[0:1, 0:256], in0=S[0:1, 0:256], in1=FVs[0:1, :], op=ALU.add), reads=["FVs", "sco"], writes=["sco"])
                    f.op('dve', lambda en: en.max(out=S[0:1, 768:776], in_=S[0:1, 0:256]), reads=["sco"], writes=["sco"])
                    f.op('dve', lambda en: en.match_replace(out=S[0:1, 256:512], in_to_replace=S[0:1, 768:776], in_values=S[0:1, 0:256], imm_value=-1e30),
                         reads=["sco"], writes=["sco"])
                    f.op('dve', lambda en: en.max(out=S[0:1, 776:784], in_=S[0:1, 256:512]), reads=["sco"], writes=["sco"])
                    f.op('dve', lambda en: en.tensor_scalar(out=S[0:1, 512:768], in0=S[0:1, 0:256], scalar1=S[0:1, 782:783], scalar2=None, op0=ALU.is_ge),
                         reads=["sco"], writes=["sco"])
                    f.op('dve', lambda en: en.tensor_scalar(out=NSr[0:1, 0:256], in0=S[0:1, 512:768], scalar1=-1.0, scalar2=-NEG, op0=ALU.add, op1=ALU.mult),
                         reads=["sco"], writes=["NSr"])
                    f.op('dve', lambda en, g=g: en.tensor_copy(out=NS2[0:1, g, :], in_=NSr[0:1, 0:256:2]), reads=["NSr"], writes=["NS2"])
                    f.op('dve', lambda en: en.tensor_copy(out=NSr[0:1, 256:384], in_=NSr[0:1, 1:256:2]), reads=["NSr"], writes=["NSr"])
                    f.dma(NS2[32:33, g, :], NSr[0:1, 256:384], reads=["NSr"], writes=["NS2"])
                    ps, pk = f.pget()
                    for tl in range(8):
                        f.op('pe', lambda en, tl=tl, ps=ps: en.transpose(ps[:, tl * 4:(tl + 1) * 4], ees[0:4, tl * 128:(tl + 1) * 128], self.ids[0:4, 0:4]),
                             reads=["ees", "ids"], writes=[pk])
                    self.copy('act', pTs[:, :, :], ps[:, 0:32].rearrange("p (t h) -> p t h", t=8), [pk], ["pTs"])
                    ps, pk = f.pget()
                    for tl in range(8):
                        f.op('pe', lambda en, tl=tl, ps=ps: en.matmul(ps[0:4, 0:64], lhsT=pTs[:, tl, :], rhs=VCMs[:, tl, g, :], start=(tl == 0), stop=(tl == 7)),
                             reads=["pTs", "VCMs"], writes=[pk])
                    self.copy('dve', Ob[0:4, 0, g, :], ps[0:4, 0:64], [pk], ["Ob"])
                for br in range(2):
                    pa, pak = f.pacc(br)
                    first = [True]
                    nseg = 8 if br == 0 else 1
                    for s_ in range(nseg):
                        ntile = 16 if br == 0 else 4
                        if br == 0:
                            self.gather_KT_V(I["csk"], IDX, b, 16 * s_, 16, CT, 0)
                            self.gather_KT_V(I["csv"], IDX, b, 16 * s_, 16, CT, 0, VAs=VAs)
                        else:
                            self.gather_KT_V(None, None, b, 0, 4, CT, 0, plain=I["swk"][b * 512:(b + 1) * 512, :])
                            self.gather_KT_V(None, None, b, 0, 4, CT, 0, VAs=VAs, plain=I["swv"][b * 512:(b + 1) * 512, :])
                        for g in range(4):
                            sl = g // 2
                            ps, pk = f.pget()
                            for rt in range(ntile):
                                last = (s_ == nseg - 1 and rt == ntile - 1)
                                f.op('pe', lambda en, rt=rt, ps=ps: en.matmul(ps[:, rt * 4:(rt + 1) * 4], lhsT=CT[:, 0, sl, rt * 128:(rt + 1) * 128],
                                                                              rhs=QZS[:, b, 4 * g:4 * g + 4], start=True, stop=False),
                                     reads=[("CT", 0), "QZS"], writes=[pk])
                                if br == 0:
                                    f.op('pe', lambda en, rt=rt, ps=ps, s_=s_: en.matmul(
                                        ps[:, rt * 4:(rt + 1) * 4], lhsT=E2[0:33, :],
                                        rhs=NS2[0:33, g, 16 * s_ + rt:16 * s_ + rt + 1].to_broadcast([33, 4]), start=False, stop=False),
                                        reads=["E2", "NS2"], writes=[pk])
                                f.op('pe', lambda en, rt=rt, ps=ps, last=last: en.matmul(ps[:, rt * 4:(rt + 1) * 4], lhsT=onesb[0:1, :], rhs=C31s[0:1, 4 * g:4 * g + 4],
                                                                                       start=False, stop=(not last)), reads=["onesb", "C31s"], writes=[pk])
                                if last:
                                    f.op('pe', lambda en, rt=rt, ps=ps: en.matmul(ps[:, rt * 4:(rt + 1) * 4], lhsT=self.idb[:, :], rhs=B127[:, 4 * g:4 * g + 4],
                                                                                  start=False, stop=True), reads=["idb", "B127"], writes=[pk])
                            f.op('act', lambda en, ps=ps, ntile=ntile: en.activation(out=Ps[:, 0:ntile * 4], in_=ps[:, 0:ntile * 4], func=AF.Exp), reads=[pk], writes=["Ps"])
                            for rt in range(ntile):
                                f.op('pe', lambda en, rt=rt, fs=first[0]: en.matmul(pa[0:4, g * 65:(g + 1) * 65], lhsT=Ps[:, rt * 4:(rt + 1) * 4], rhs=VAs[:, rt, g, :],
                                                                                  start=fs, stop=False, skip_group_check=True), reads=["Ps", "VAs"], writes=[pak])
                                first[0] = False
                    for g in range(4):
                        sl = g // 2
                        ps, pk = f.pget()
                        f.op('pe', lambda en, ps=ps: en.matmul(ps[0:1, 0:4], lhsT=KTn[:, br, sl, b:b + 1], rhs=QZS[:, b, 4 * g:4 * g + 4], start=True, stop=False),
                             reads=["KTn", "QZS"], writes=[pk])
                        f.op('pe', lambda en, ps=ps: en.matmul(ps[0:1, 0:4], lhsT=onesb[0:1, 0:1], rhs=T0s[0:1, 4 * g:4 * g + 4], start=False, stop=True),
                             reads=["onesb", "T0s"], writes=[pk])
                        f.op('act', lambda en, ps=ps: en.activation(out=Pn[0:1, :], in_=ps[0:1, 0:4], func=AF.Exp), reads=[pk], writes=["Pn"])
                        f.op('pe', lambda en: en.matmul(pa[0:4, g * 65:(g + 1) * 65], lhsT=Pn[0:1, 0:4], rhs=VN[0:1, b, br, g, :], start=False, stop=True,
                                                        skip_group_check=True), reads=["Pn", "VN"], writes=[pak])
                    pav = pa[0:4, 0:260].rearrange("p (g d) -> p g d", g=4)
                    f.op('dve', lambda en, pav=pav: en.tensor_scalar(out=sms[0:4, 8:12], in0=pav[:, :, 64], scalar1=1e-30, scalar2=None, op0=ALU.max),
                         reads=[pak], writes=["sms"])
                    f.op('dve', lambda en: en.reciprocal(out=sms[0:4, 8:12], in_=sms[0:4, 8:12]), reads=["sms"], writes=["sms"])
                    f.op('dve', lambda en, pav=pav, br=br: en.tensor_tensor(out=Ob[0:4, 1 + br, :, :], in0=pav[:, :, 0:64],
                                                                            in1=sms[0:4, 8:12][:, :, None].to_broadcast([4, 4, 64]), op=ALU.mult),
                         reads=[pak, "sms"], writes=["Ob"])
                f.op('dve', lambda en: en.tensor_scalar(out=Lg[:, :], in0=Jm[:, :], scalar1=gT[:, b:b + 1], scalar2=None, op0=ALU.mult),
                     reads=["Jm", "gT"], writes=["Lg"])
                ps, pk = f.pget()
                f.op('pe', lambda en, ps=ps: en.matmul(ps[0:4, 0:12], lhsT=Lg[:, :], rhs=Rm[:, :], start=True, stop=True), reads=["Lg", "Rm"], writes=[pk])
                self.copy('dve', G4[:, :], ps[0:4, 0:12], [pk], ["G4"])
                for br in range(3):
                    f.op('dve', lambda en, br=br: en.tensor_tensor(out=tmpo[:, :, :], in0=Ob[0:4, br, :, :],
                                                                   in1=G4[:, 4 * br:4 * br + 4][:, :, None].to_broadcast([4, 4, 64]), op=ALU.mult),
                         reads=["Ob", "G4"], writes=["tmpo"])
                    if br == 0:
                        f.op('dve', lambda en: en.tensor_copy(out=Osum[:, :, 0:64], in_=tmpo[:, :, :]), reads=["tmpo"], writes=["Osum"])
                    else:
                        f.op('dve', lambda en: en.tensor_tensor(out=Osum[:, :, 0:64], in0=Osum[:, :, 0:64], in1=tmpo[:, :, :], op=ALU.add),
                             reads=["tmpo", "Osum"], writes=["Osum"])
                f.op('dve', lambda en: en.tensor_copy(out=Osum[:, :, 64:128], in_=Osum[:, :, 0:64]), reads=["Osum"], writes=["Osum"])
                for g in range(4):
                    ps, pk = f.pget()
                    f.op('pe', lambda en, ps=ps: en.transpose(ps[:, 0:4], Osum[0:4, g, :], self.ids[0:4, 0:4]), reads=["Osum", "ids"], writes=[pk])
                    self.copy('dve', OTs[0:64, 2 * g:2 * g + 2, b], ps[0:64, 0:4:2], [pk], ["OTs"])
                    self.copy('dve', OTs[64:128, 2 * g:2 * g + 2, b], ps[64:128, 1:4:2], [pk], ["OTs"])
            def cons_o(m, outs):
                for (ps, pk, c0, cn) in outs:
                    f.op('dve', lambda en, ps=ps: en.tensor_tensor(out=X[:, m, T:NT], in0=ps[:, 0:NS], in1=X[:, m, T:NT], op=ALU.add),
                         reads=[pk, ("X", m, 4)], writes=[("X", m, 4)])
            self.linear(I["w_out"], 8, D, lambda k: OTs[:, k, :], ["OTs"], NS, cons_o)

    def final(self, raw=False):
        f = self.f
        Yo = self.hT
        for ti, (t0, n) in enumerate(TILES):
            if not raw:
                for (c0, cn) in segs(n):
                    ps, pk = f.pget()
                    for c in range(8):
                        s, sk = self.scr()
                        f.op('act', lambda en, c=c, s=s: en.activation(out=s[:, 0:cn], in_=self.X[:, c, t0 + c0:t0 + c0 + cn], func=AF.Square),
                             reads=xkeys(ti, [c]), writes=[sk])
                        f.op('pe', lambda en, c=c, s=s: en.matmul(ps[:, 0:cn], lhsT=self.ones[:, :], rhs=s[:, 0:cn],
                                                                  start=(c == 0), stop=(c == 7)), reads=[sk, "ones"], writes=[pk])
                    f.op('act', lambda en: en.activation(out=self.rstd[:, c0:c0 + cn], in_=ps[:, 0:cn], func=AF.Sqrt, bias=1e-6, scale=1.0 / D),
                         reads=[pk], writes=["rstd"])
                    f.op('dve', lambda en: en.reciprocal(out=self.rstd[:, c0:c0 + cn], in_=self.rstd[:, c0:c0 + cn]),
                         reads=["rstd"], writes=["rstd"])
                for c in range(8):
                    f.op('dve', lambda en, c=c: en.scalar_tensor_tensor(
                        out=self.X[:, c, t0:t0 + n], in0=self.X[:, c, t0:t0 + n], scalar=self.VT[:, NFIN + c:NFIN + c + 1],
                        in1=self.rstd[:, 0:n], op0=ALU.mult, op1=ALU.mult),
                        reads=xkeys(ti, [c]) + ["rstd", "VT"], writes=xkeys(ti, [c]))
            for r in range(4):
                self.store_T(self.O["y_p"][t0 + r * 128:t0 + (r + 1) * 128, :],
                             lambda c, r=r, t0=t0: self.X[:, c, t0 + r * 128:t0 + (r + 1) * 128], 128, xkeys(ti))
        self.store_T(self.O["y_s"][:, :], lambda c: self.X[:, c, T:NT], NS, xkeys(3))


def build(stage=10, dbg=None, nit=8):
    nc = bass.Bass("TRN2", target_bir_lowering=False)
    es = ExitStack()
    with es:
        k = K(nc, es, stage=stage, dbg=dbg)
        for it in range(nit):
            if it:
                k.f.new_sems()
            k.bind(it)
            k.setup(first=(it == 0))
            with k.scope():
                k.hT = k.f.sb("hT", [128, 32, 516], BF16)
                k.conv_mixer()
            k.nsa()
            k.nsa_samples()
            with k.scope():
                k.hT = k.f.sb("hT1", [128, 32, 516], BF16)
                for ti, (t0, n) in enumerate(TILES):
                    k.ffn_ple(1, ti, t0, n)
            k.final()
            k.f.fence()
        k.f.finish()
        print("instructions:", k.f.n_inst, {e: k.f.cnt[e] for e in k.f.cnt}, "dmas", k.f.dcnt)
    return nc


def host_vecs(inp):
    rows = []
    for name in ["norm_mix", "norm_ffn", "norm_ple"]:
        rows.append(np.asarray(inp[name]).reshape(16, 128))
    rows.append(np.asarray(inp["norm_final"]).reshape(8, 128))
    rows.append(np.asarray(inp["conv_b1"]).reshape(16, 128))
    for name in ["conv_dwb", "conv_ln_g", "conv_ln_b", "conv_b2"]:
        rows.append(np.asarray(inp[name]).reshape(8, 128))
    rows.append(np.asarray(inp["conv_dw"]).reshape(31 * 8, 128))
    rows.append(np.asarray(inp["cmpk_pe"]).reshape(16, 128))
    rows.append(np.asarray(inp["cmpv_pe"]).reshape(16, 128))
    v = np.ascontiguousarray(np.concatenate(rows, axis=0).astype(np.float32))
    assert v.shape == (NVEC, 128)
    return v


def rel_bucket_np(dist):
    n = np.maximum(dist, 0)
    nf = np.maximum(n, 1).astype(np.float32)
    large = 16 + (np.log(nf / np.float32(16)) / np.float32(math.log(8.0)) * np.float32(16)).astype(np.int32)
    large = np.minimum(large, 31)
    return np.where(n < 16, n, large)


def onehot_table(dist, sub31):
    d = dist.reshape(-1)
    oh = np.zeros((34, d.size), np.float32)
    ok = d >= 0
    b = rel_bucket_np(d)
    idx = np.nonzero(ok)[0]
    oh[b[idx], idx] = 1.0
    oh[32, ~ok] = 1.0
    if sub31:
        oh[33, ok] = -1.0
    return oh


_CONST = {}


def host_consts():
    if _CONST:
        return _CONST
    key = np.arange(128)[:, None]
    q = np.arange(128)[None, :]
    _CONST["ohd"] = onehot_table(q - key, True)
    _CONST["oho"] = onehot_table(128 + q - key, True)
    ql = np.arange(128)[:, None]
    m = np.arange(254)[None, :]
    _CONST["ohg"] = onehot_table(ql - 16 * (m - 127) - 31, False)
    _CONST["tailm"] = np.where(q <= key, 0.0, NEG).astype(np.float32)
    qpos = (128 * np.arange(16)[None, :, None] + np.arange(128)[:, None, None])
    j = np.arange(32)[None, None, :]
    cur = qpos // 64
    valid = j * 64 <= qpos
    forced = (j == 0) | (j == cur) | (j == cur - 1)
    fv = np.where(forced, 1000.0, np.where(valid, 0.0, -1000.0)).astype(np.float32)
    _CONST["fvtab"] = np.ascontiguousarray(fv.reshape(128, 512))
    b = np.arange(32)[:, None, None]
    kt = np.arange(16)[None, :, None]
    k = np.arange(128)[None, None, :]
    _CONST["exm"] = np.ascontiguousarray((b == 2 * kt + k // 64).astype(np.float32).reshape(32, 2048))
    _CONST["ident"] = np.eye(128, dtype=np.float32)
    c = np.arange(1024)
    dist = 16384 - 16 * c - 31
    dist[1023] = -1
    _CONST["ohs"] = onehot_table(dist, False)
    fvs = np.zeros((1, 256), np.float32)
    fvs[0, 0] = 1000.0
    fvs[0, 255] = 1000.0
    _CONST["fvs"] = fvs
    e2 = np.zeros((33, 128), np.float32)
    e2[0, 0:64] = 1.0
    e2[32, 64:128] = 1.0
    _CONST["e2"] = e2
    row = np.arange(48)
    _CONST["jm"] = (row[:, None] % 4 == np.arange(4)[None, :]).astype(np.float32)
    _CONST["rm"] = (row[:, None] // 4 == np.arange(12)[None, :]).astype(np.float32)
    _CONST["oh127"] = onehot_table(128 - np.arange(128), True)
    return _CONST


def full_inputs(inp):
    g = lambda n: np.asarray(inp[n])
    c = host_consts()
    m = {
        "xp": np.ascontiguousarray(g("x_prompt").reshape(8 * T, D)),
        "xs": np.ascontiguousarray(g("x_sample").reshape(32, D)),
        "pp": np.ascontiguousarray(g("p_prompt").reshape(2, 8 * T, 256)),
        "psm": np.ascontiguousarray(g("p_sample").reshape(2, 32, 256)),
        "sconv": np.ascontiguousarray(g("state_conv").reshape(32 * 30, D)),
        "vecs": host_vecs(inp),
        "conv_w1": g("conv_w1")[0], "conv_w2": g("conv_w2")[0],
        "mlp_up": g("mlp_up"), "mlp_down": g("mlp_down"),
        "ple_proj": g("ple_proj"), "ple_gate": g("ple_gate"),
        "w_in": g("attn_w_in")[0], "w_out": g("attn_w_out")[0],
        "ck_w1": g("cmpk_w1")[0], "ck_w2": g("cmpk_w2")[0], "cv_w1": g("cmpv_w1")[0], "cv_w2": g("cmpv_w2")[0],
        "rel_table": g("rel_table"),
        "swk": np.ascontiguousarray(g("state_win_k").reshape(32 * 512, 256)),
        "swv": np.ascontiguousarray(g("state_win_v").reshape(32 * 512, 256)),
        "cck": g("cache_cmp_k").reshape(655360, 256), "ccv": g("cache_cmp_v").reshape(655360, 256),
        "csk": g("cache_slc_k").reshape(655360, 256), "csv": g("cache_slc_v").reshape(655360, 256),
        "ptab": np.ascontiguousarray(g("page_table").reshape(1, 32 * 128).astype(np.int32)),
    }
    for k in ["ident", "ohd", "oho", "ohg", "tailm", "fvtab", "exm", "ohs", "fvs", "e2", "jm", "rm", "oh127"]:
        m[k] = c[k]
    return {k: np.ascontiguousarray(v) for k, v in m.items()}


_NC_CACHE = {}


def kernel(**inp):
    m = full_inputs(inp)
    if "nc" not in _NC_CACHE:
        _NC_CACHE["nc"] = build(stage=10, nit=8)
    nc = _NC_CACHE["nc"]
    res = run_bass_kernel_spmd(nc, [m], core_ids=[0])
    R = res.results[0]
    g = lambda nm: np.asarray(R[nm]).astype(np.float32)
    outs = [g("y_p").reshape(8, T, D), g("y_s").reshape(32, 1, D)]
    for nm in ["cmp_k_p", "cmp_v_p", "slc_k_p", "slc_v_p"]:
        outs.append(g(nm).reshape(1, 8, T, 4, 64))
    for nm in ["win_k_p", "win_v_p"]:
        outs.append(g(nm).reshape(1, 8, 512, 4, 64))
    outs.append(g("conv_p").reshape(1, 8, 30, D))
    for nm in ["cmp_k_s", "cmp_v_s", "slc_k_s", "slc_v_s"]:
        outs.append(g(nm).reshape(1, 32, 1, 4, 64))
    for nm in ["win_k_s", "win_v_s"]:
        outs.append(g(nm).reshape(1, 32, 512, 4, 64))
    outs.append(g("conv_s").reshape(1, 32, 30, D))
    return tuple(outs)
```

```python
import math
import numpy as np
from contextlib import ExitStack
import concourse.bass as bass
import concourse.mybir as mybir
from concourse.bass_utils import run_bass_kernel_spmd

F32 = mybir.dt.float32
BF16 = mybir.dt.bfloat16
I32 = mybir.dt.int32
ALU = mybir.AluOpType
AF = mybir.ActivationFunctionType
AX = mybir.AxisListType

NCORES = 8
T = 2048
NS = 4
NT = T + NS
D = 1024
NEG = -30000.0

NM0, NM1, NF0, NF1, NP0, NP1, NFIN, B1, DWB, LNG, LNB, B2, DW, PEK, PEV = \
    0, 8, 16, 24, 32, 40, 48, 56, 72, 80, 88, 96, 104, 352, 368
NVEC = 384


class FW:
    NDMA = 40

    def __init__(self, nc, es):
        self.nc = nc
        self.es = es
        self.eng = {'pe': nc.tensor, 'act': nc.scalar, 'dve': nc.vector, 'pool': nc.gpsimd, 'sp': nc.sync}
        self.n_inst = 0
        self.psl = []
        self.psi = 0
        self.cur_es = None
        self.sfx = ""
        self.nset = 0
        self.cnt = {e: 0 for e in self.eng}
        self.dsem = [es.enter_context(nc.semaphore("d%d" % j)) for j in range(self.NDMA)]
        self.dcnt = 0
        self.dslot_tok = [None] * self.NDMA
        self.waited = {e: {} for e in self.eng}
        self.new_sems()

    def new_sems(self):
        if self.nset:
            self.fence()
        i = self.nset
        self.nset += 1
        self.sem = {e: self.es.enter_context(self.nc.semaphore("s%d_%s" % (i, e))) for e in self.eng}
        self.cnt = {e: 0 for e in self.eng}
        self.waited = {e: {k: v for k, v in self.waited[e].items() if isinstance(k, tuple)} for e in self.eng}
        self.lastw = {}
        self.readers = {}

    def sb(self, name, shape, dt=F32, es=None):
        return (es or self.cur_es or self.es).enter_context(self.nc.sbuf_tensor(name + self.sfx, list(shape), dt))

    def fence(self):
        toks = [(e, self.cnt[e]) for e in self.eng if self.cnt[e]] + [t for t in self.dslot_tok if t is not None]
        for e in self.eng:
            for t in toks:
                if t[0] != e:
                    self._wait(e, t)

    def mkpsum(self):
        for i in range(8):
            self.psl.append((self.es.enter_context(self.nc.psum_tensor("ps%d" % i, [128, 512], F32)), "ps%d" % i))

    def pget(self):
        r = self.psl[self.psi % 6]
        self.psi += 1
        return r

    def pacc(self, i):
        return self.psl[6 + i]

    def _semof(self, tok):
        return self.sem[tok[0]] if tok[0] != 'd' else self.dsem[tok[1]]

    def _wait(self, e, tok):
        key = tok[0] if tok[0] != 'd' else ('d', tok[1])
        val = tok[-1]
        if self.waited[e].get(key, 0) >= val:
            return
        self.waited[e][key] = val
        self.eng[e].wait_ge(self._semof(tok), val)

    def _deps(self, e, reads, writes):
        toks = []
        for k in list(reads) + list(writes):
            t = self.lastw.get(k)
            if t is not None:
                toks.append(t)
        for k in writes:
            toks.extend(self.readers.get(k, ()))
        for t in toks:
            if e == 'pe' and t[0] == 'pe':
                continue
            self._wait(e, t)

    def _record(self, tok, reads, writes):
        for k in reads:
            self.readers.setdefault(k, []).append(tok)
        for k in writes:
            self.lastw[k] = tok
            self.readers[k] = []

    def op(self, e, fn, reads=(), writes=()):
        px = [k for k in reads if isinstance(k, str) and k.startswith("ps")]
        if px:
            writes = list(writes) + px
        self._deps(e, reads, writes)
        ins = fn(self.eng[e])
        self.cnt[e] += 1
        ins.then_inc(self.sem[e], 1)
        self._record((e, self.cnt[e]), reads, writes)
        self.n_inst += 1
        return ins

    def dma(self, out, in_, reads=(), writes=(), q='sp', fn=None, **kw):
        s = self.dcnt % self.NDMA
        v = 16 * (self.dcnt // self.NDMA + 1)
        self.dcnt += 1
        prev = self.dslot_tok[s]
        if prev is not None:
            self._wait(q, prev)
        self._deps(q, reads, writes)
        if fn is not None:
            ins = fn(self.eng[q])
        else:
            ins = self.eng[q].dma_start(out=out, in_=in_, **kw)
        ins.then_inc(self.dsem[s], 16)
        tok = ('d', s, v)
        self.dslot_tok[s] = tok
        self._record(tok, reads, writes)
        self.n_inst += 1
        return tok

    def finish(self):
        for s in range(self.NDMA):
            if self.dslot_tok[s] is not None:
                self._wait('sp', self.dslot_tok[s])
        for e in self.eng:
            if self.cnt[e]:
                self._wait('sp', (e, self.cnt[e]))


def segs(n):
    return [(0, n)] if n <= 512 else [(0, 512), (512, n - 512)]


TILES = [(0, 512), (512, 512), (1024, 512), (1536, 516)]


def xkeys(ti, cs=range(8)):
    ks = [("X", c, ti) for c in cs]
    if ti == 3:
        ks += [("X", c, 4) for c in cs]
    return ks


class K:
    def __init__(self, nc, es, dbg=None, stage=9):
        self.nc = nc
        self.f = FW(nc, es)
        self.dbg = dbg
        self.stage = stage
        self.rr = 0
        f = self.f
        dt = lambda name, shape, d=F32, kind="ExternalInput": nc.dram_tensor(name, list(shape), d, kind=kind).ap()
        NSQ, NSM = 8, 32
        FI = {}
        for name, shape in [("xp", [NSQ * T, D]), ("xs", [NSM, D]), ("pp", [2, NSQ * T, 256]), ("psm", [2, NSM, 256]),
                            ("sconv", [NSM * 30, D]), ("vecs", [NVEC, 128]), ("ident", [128, 128]),
                            ("conv_w1", [D, 2 * D]), ("conv_w2", [D, D]),
                            ("mlp_up", [2, D, 4 * D]), ("mlp_down", [2, 4 * D, D]),
                            ("ple_proj", [2, 256, D]), ("ple_gate", [2, D, D]),
                            ("w_in", [D, 2608]), ("w_out", [D, D]), ("ck_w1", [2048, 128]), ("ck_w2", [128, 64]),
                            ("cv_w1", [2048, 128]), ("cv_w2", [128, 64]), ("rel_table", [32, 16]),
                            ("ohd", [34, 16384]), ("oho", [34, 16384]), ("ohg", [34, 32512]), ("tailm", [128, 128]),
                            ("fvtab", [128, 512]), ("exm", [32, 2048]),
                            ("swk", [NSM * 512, 256]), ("swv", [NSM * 512, 256]),
                            ("cck", [655360, 256]), ("ccv", [655360, 256]), ("csk", [655360, 256]), ("csv", [655360, 256]),
                            ("ohs", [34, 1024]), ("fvs", [1, 256]), ("e2", [33, 128]), ("jm", [48, 4]), ("rm", [48, 12]),
                            ("oh127", [34, 128])]:
            FI[name] = dt(name, shape)
        FI["ptab"] = dt("ptab", [1, NSM * 128], I32)
        FO = {}
        FO["y_p"] = dt("y_p", [NSQ * T, D], kind="ExternalOutput")
        FO["y_s"] = dt("y_s", [NSM, D], kind="ExternalOutput")
        FO["conv_p"] = dt("conv_p", [NSQ * 30, D], kind="ExternalOutput")
        FO["conv_s"] = dt("conv_s", [NSM * 30, D], kind="ExternalOutput")
        for nm in ["cmp_k_p", "cmp_v_p", "slc_k_p", "slc_v_p"]:
            FO[nm] = dt(nm, [NSQ * T, 256], kind="ExternalOutput")
        for nm in ["win_k_p", "win_v_p"]:
            FO[nm] = dt(nm, [NSQ * 512, 256], kind="ExternalOutput")
        for nm in ["cmp_k_s", "cmp_v_s", "slc_k_s", "slc_v_s"]:
            FO[nm] = dt(nm, [NSM, 256], kind="ExternalOutput")
        for nm in ["win_k_s", "win_v_s"]:
            FO[nm] = dt(nm, [NSM * 512, 256], kind="ExternalOutput")
        self.bsc = dt("bsc", [16, 32768], kind="Internal")
        self.FI, self.FO = FI, FO
        self.bind(0)
        self.X = f.sb("X", [128, 8, NT])
        self.xn = f.sb("xn", [128, 8, 516], BF16)
        self.hT = None
        self.VT = f.sb("VT", [128, NVEC])
        self.ids = f.sb("ids", [128, 128])
        self.idb = f.sb("idb", [128, 128], BF16)
        self.ones = f.sb("ones", [128, 128])
        self.wbuf = [f.sb("wb%d" % i, [128, 4096], BF16) for i in range(3)]
        self.wi = 0
        self.stg = [f.sb("stg%d" % i, [128, 1024]) for i in range(2)]
        self.si = 0
        self.sc = [f.sb("sc%d" % i, [128, 516]) for i in range(3)]
        self.sci = 0
        self.rstd = f.sb("rstd", [128, 516])
        f.mkpsum()

    def bind(self, it):
        FI, FO = self.FI, self.FO
        I = dict(FI)
        I["xp"] = FI["xp"][it * T:(it + 1) * T, :]
        I["xs"] = FI["xs"][it * NS:(it + 1) * NS, :]
        I["pp"] = FI["pp"][:, it * T:(it + 1) * T, :]
        I["psm"] = FI["psm"][:, it * NS:(it + 1) * NS, :]
        I["sconv"] = FI["sconv"][it * NS * 30:(it + 1) * NS * 30, :]
        I["swk"] = FI["swk"][it * NS * 512:(it + 1) * NS * 512, :]
        I["swv"] = FI["swv"][it * NS * 512:(it + 1) * NS * 512, :]
        I["ptab"] = FI["ptab"][:, it * NS * 128:(it + 1) * NS * 128]
        O = {}
        O["y_p"] = FO["y_p"][it * T:(it + 1) * T, :]
        O["y_s"] = FO["y_s"][it * NS:(it + 1) * NS, :]
        O["conv_p"] = FO["conv_p"][it * 30:(it + 1) * 30, :]
        O["conv_s"] = FO["conv_s"][it * NS * 30:(it + 1) * NS * 30, :]
        for nm in ["cmp_k_p", "cmp_v_p", "slc_k_p", "slc_v_p"]:
            O[nm] = FO[nm][it * T:(it + 1) * T, :]
        for nm in ["win_k_p", "win_v_p"]:
            O[nm] = FO[nm][it * 512:(it + 1) * 512, :]
        for nm in ["cmp_k_s", "cmp_v_s", "slc_k_s", "slc_v_s"]:
            O[nm] = FO[nm][it * NS:(it + 1) * NS, :]
        for nm in ["win_k_s", "win_v_s"]:
            O[nm] = FO[nm][it * NS * 512:(it + 1) * NS * 512, :]
        self.I, self.O = I, O
        self.f.sfx = "_i%d" % it

    def scope(self):
        k = self

        class _S:
            def __enter__(s2):
                s2.prev = k.f.cur_es
                s2.es = ExitStack()
                s2.es.__enter__()
                k.f.cur_es = s2.es
                return s2

            def __exit__(s2, *a):
                k.f.fence()
                k.f.cur_es = s2.prev
                s2.es.__exit__(None, None, None)
                return False
        return _S()

    def scr(self):
        i = self.sci % 3
        self.sci += 1
        return self.sc[i], "sc%d" % i

    def ev_engine(self):
        self.rr += 1
        return 'act' if self.rr % 2 else 'dve'

    def copy(self, e, out, in_, reads, writes):
        if e == 'act':
            self.f.op('act', lambda en: en.activation(out=out, in_=in_, func=AF.Identity), reads, writes)
        else:
            self.f.op(e, lambda en: en.tensor_copy(out=out, in_=in_), reads, writes)

    def load_rows_T(self, src, nrows, ncols, sink):
        f = self.f
        i = self.si % 2
        self.si += 1
        st, sk = self.stg[i], "stg%d" % i
        f.dma(st[0:nrows, 0:ncols], src, writes=[sk])
        nch = ncols // 128
        per = max(1, 512 // max(nrows, 1))
        per = min(per, 4)
        c = 0
        while c < nch:
            g = min(per, nch - c)
            ps, pk = f.pget()
            for j in range(g):
                f.op('pe', lambda en, j=j, c=c: en.transpose(ps[:, j * nrows:(j + 1) * nrows],
                                                              st[0:nrows, (c + j) * 128:(c + j + 1) * 128],
                                                              self.ids[0:nrows, 0:nrows]),
                     reads=[sk, "ids"], writes=[pk])
            sink(c, g, ps, pk)
            c += g

    def store_T(self, dst, src_fn, ntok, src_keys, ncols=1024, q='sp'):
        f = self.f
        i = self.si % 2
        self.si += 1
        st, sk = self.stg[i], "stg%d" % i
        nch = ncols // 128
        for c0 in range(0, nch, 4):
            ps, pk = f.pget()
            g = min(4, nch - c0)
            for j in range(g):
                f.op('pe', lambda en, j=j, c0=c0: en.transpose(ps[0:ntok, j * 128:(j + 1) * 128], src_fn(c0 + j), self.ids[:, :]),
                     reads=list(src_keys) + ["ids"], writes=[pk])
            self.copy(self.ev_engine(), st[0:ntok, c0 * 128:(c0 + g) * 128], ps[0:ntok, 0:g * 128], [pk], [sk])
        f.dma(dst, st[0:ntok, 0:ncols], reads=[sk], q=q)

    def wload(self, src, kc, width):
        i = self.wi % 3
        self.wi += 1
        wb, wk = self.wbuf[i], "wb%d" % i
        view = wb[:, 0:kc * width].rearrange("p (k m) -> p k m", k=kc)
        self.f.dma(view, src.rearrange("(k p) m -> p k m", p=128), writes=[wk], q='pool')
        return view, wk

    def linear(self, wsrc, kc, nout, xin, xkeys_, n, consume, bw=None):
        f = self.f
        if bw is None:
            bw = 512 if kc <= 8 else 128
        bw = min(bw, nout)
        blocks = list(range(0, nout, bw))
        nxt = self.wload(wsrc[:, blocks[0]:blocks[0] + bw], kc, bw)
        for bi, b0 in enumerate(blocks):
            wv, wk = nxt
            if bi + 1 < len(blocks):
                nb = blocks[bi + 1]
                nxt = self.wload(wsrc[:, nb:nb + bw], kc, bw)
            for mm in range(bw // 128):
                outs = []
                for (c0, cn) in segs(n):
                    ps, pk = f.pget()
                    for k in range(kc):
                        f.op('pe', lambda en, k=k, mm=mm, c0=c0, cn=cn, ps=ps, wv=wv: en.matmul(
                            ps[:, 0:cn], lhsT=wv[:, k, mm * 128:(mm + 1) * 128], rhs=xin(k)[:, c0:c0 + cn],
                            start=(k == 0), stop=(k == kc - 1)), reads=[wk] + list(xkeys_), writes=[pk])
                    outs.append((ps, pk, c0, cn))
                consume(b0 // 128 + mm, outs)

    def rmsnorm(self, ti, t0, n, gcol):
        f = self.f
        X = self.X
        for (c0, cn) in segs(n):
            ps, pk = f.pget()
            for c in range(8):
                s, sk = self.scr()
                f.op('act', lambda en, c=c, s=s: en.activation(out=s[:, 0:cn], in_=X[:, c, t0 + c0:t0 + c0 + cn], func=AF.Square),
                     reads=xkeys(ti, [c]), writes=[sk])
                f.op('pe', lambda en, c=c, s=s: en.matmul(ps[:, 0:cn], lhsT=self.ones[:, :], rhs=s[:, 0:cn],
                                                           start=(c == 0), stop=(c == 7)), reads=[sk, "ones"], writes=[pk])
            f.op('act', lambda en: en.activation(out=self.rstd[:, c0:c0 + cn], in_=ps[:, 0:cn], func=AF.Sqrt,
                                                 bias=1e-6, scale=1.0 / D), reads=[pk], writes=["rstd"])
            f.op('dve', lambda en: en.reciprocal(out=self.rstd[:, c0:c0 + cn], in_=self.rstd[:, c0:c0 + cn]),
                 reads=["rstd"], writes=["rstd"])
        for c in range(8):
            f.op('dve', lambda en, c=c: en.scalar_tensor_tensor(
                out=self.xn[:, c, 0:n], in0=X[:, c, t0:t0 + n], scalar=self.VT[:, gcol + c:gcol + c + 1],
                in1=self.rstd[:, 0:n], op0=ALU.mult, op1=ALU.mult),
                reads=xkeys(ti, [c]) + ["rstd", "VT"], writes=[("xn", c)])

    def setup(self, first=True):
        f = self.f
        I = self.I
        if first:
            f.dma(self.ids[:], I["ident"], writes=["ids"])
            f.op('dve', lambda en: en.tensor_copy(out=self.idb[:], in_=self.ids[:]), reads=["ids"], writes=["idb"])
            f.op('pool', lambda en: en.memset(self.ones[:], 1.0), writes=["ones"])
            for r in range(NVEC // 128):
                def sink(c, g, ps, pk, r=r):
                    self.copy('dve', self.VT[:, r * 128:(r + 1) * 128], ps[:, 0:128], [pk], ["VT"])
                self.load_rows_T(I["vecs"][r * 128:(r + 1) * 128, :], 128, 128, sink)
        for rt in range(16):
            ti = rt // 4

            def sink(c, g, ps, pk, rt=rt, ti=ti):
                self.copy(self.ev_engine(), self.X[:, c:c + g, rt * 128:(rt + 1) * 128],
                          ps[:, 0:g * 128].rearrange("p (c t) -> p c t", c=g), [pk], [("X", cc, ti) for cc in range(c, c + g)])
            self.load_rows_T(I["xp"][rt * 128:(rt + 1) * 128, :], 128, D, sink)

        def sink_s(c, g, ps, pk):
            self.copy('dve', self.X[:, c:c + g, T:NT], ps[:, 0:g * NS].rearrange("p (c t) -> p c t", c=g),
                      [pk], [("X", cc, 4) for cc in range(c, c + g)])
        self.load_rows_T(I["xs"][:, :], NS, D, sink_s)

    def conv_mixer(self):
        f = self.f
        I = self.I
        X = self.X
        VT = self.VT
        Ub = f.sb("Ub", [128, 8, 30 + 512], BF16)
        Ulast = f.sb("Ulast", [128, 8, 30])
        CS = f.sb("CS", [128, 8, NS, 31])
        Y = f.sb("Y", [128, 8, 516])
        Dg = [f.sb("Dg%d" % i, [128, 31, 128], BF16) for i in range(2)]
        tmpc = f.sb("tmpc", [128, 8, NS, 31])
        mu = f.sb("mu", [128, 516])
        f.op('pool', lambda en: en.memset(Ub[:, :, 0:30], 0.0), writes=[("Ub", c) for c in range(8)])
        def sink_cs(c, g, ps, pk):
            self.copy('dve', CS[:, c:c + g, :, 0:30], ps[:, 0:g * 120].rearrange("p (c b k) -> p c b k", c=g, b=NS), [pk], ["CS"])
        self.load_rows_T(I["sconv"][:, :], NS * 30, D, sink_cs)
        for b in range(NS):
            f.dma(self.O["conv_s"][b * 30:b * 30 + 29, :], I["sconv"][b * 30 + 1:b * 30 + 30, :])

        for ti, (t0, n) in enumerate(TILES):
            self.rmsnorm(ti, t0, n, NM0)
            for mb in range(2):
                wa, wak = self.wload(I["conv_w1"][:, mb * 512:(mb + 1) * 512], 8, 512)
                wg, wgk = self.wload(I["conv_w1"][:, D + mb * 512:D + (mb + 1) * 512], 8, 512)
                for j in range(4):
                    m = mb * 4 + j
                    for (c0, cn) in segs(n):
                        pa, pak = f.pget()
                        pg, pgk = f.pget()
                        for k in range(8):
                            f.op('pe', lambda en, k=k, j=j, pa=pa: en.matmul(pa[:, 0:cn], lhsT=wa[:, k, j * 128:(j + 1) * 128],
                                                                              rhs=self.xn[:, k, c0:c0 + cn], start=(k == 0), stop=(k == 7)),
                                 reads=[wak] + [("xn", kk) for kk in range(8)], writes=[pak])
                        for k in range(8):
                            f.op('pe', lambda en, k=k, j=j, pg=pg: en.matmul(pg[:, 0:cn], lhsT=wg[:, k, j * 128:(j + 1) * 128],
                                                                              rhs=self.xn[:, k, c0:c0 + cn], start=(k == 0), stop=(k == 7)),
                                 reads=[wgk] + [("xn", kk) for kk in range(8)], writes=[pgk])
                        s, sk = self.scr()
                        f.op('act', lambda en, s=s, pg=pg, m=m: en.activation(out=s[:, 0:cn], in_=pg[:, 0:cn], func=AF.Sigmoid,
                                                                              bias=VT[:, B1 + 8 + m:B1 + 9 + m]),
                             reads=[pgk, "VT"], writes=[sk])
                        u, uk = self.scr()
                        f.op('dve', lambda en, s=s, u=u, pa=pa, m=m: en.scalar_tensor_tensor(
                            out=u[:, 0:cn], in0=pa[:, 0:cn], scalar=VT[:, B1 + m:B1 + m + 1], in1=s[:, 0:cn],
                            op0=ALU.add, op1=ALU.mult), reads=[pak, sk, "VT"], writes=[uk])
                        if c0 == 0:
                            f.op('act', lambda en, u=u, m=m: en.activation(out=Ub[:, m, 30:30 + 512], in_=u[:, 0:512], func=AF.Identity),
                                 reads=[uk], writes=[("Ub", m)])
                            if ti == 3:
                                f.op('pool', lambda en, u=u, m=m: en.tensor_copy(out=Ulast[:, m, :], in_=u[:, 482:512]),
                                     reads=[uk], writes=["Ulast"])
                        else:
                            f.op('pool', lambda en, u=u, m=m: en.tensor_copy(out=CS[:, m, :, 30], in_=u[:, 0:NS]),
                                 reads=[uk], writes=["CS"])
            for c in range(8):
                dg, dgk = Dg[c % 2], "Dg%d" % (c % 2)
                f.op('dve', lambda en, c=c, dg=dg: en.tensor_tensor(
                    out=dg[:, :, :], in0=self.ids[:, None, :].to_broadcast([128, 31, 128]),
                    in1=VT[:, DW + c:DW + c + 248:8][:, :, None].to_broadcast([128, 31, 128]), op=ALU.mult),
                    reads=["ids", "VT"], writes=[dgk])
                ps, pk = f.pget()
                for k in range(31):
                    f.op('pe', lambda en, k=k, c=c, dg=dg, ps=ps: en.matmul(ps[:, 0:512], lhsT=dg[:, k, :], rhs=Ub[:, c, k:k + 512],
                                                                           start=(k == 0), stop=(k == 30)),
                         reads=[dgk, ("Ub", c)], writes=[pk])
                f.op('act', lambda en, c=c, ps=ps: en.activation(out=Y[:, c, 0:512], in_=ps[:, 0:512], func=AF.Identity,
                                                                 bias=VT[:, DWB + c:DWB + c + 1]), reads=[pk, "VT"], writes=[("Y", c)])
                f.op('pool', lambda en, c=c: en.tensor_copy(out=Ub[:, c, 0:30], in_=Ub[:, c, 512:542]),
                     reads=[("Ub", c)], writes=[("Ub", c)])
            if ti == 3:
                dwv = VT[:, DW:DW + 248].rearrange("p (k c) -> p c k", c=8)
                f.op('dve', lambda en: en.tensor_tensor(out=tmpc[:, :, :, :], in0=CS[:, :, :, :],
                                                        in1=dwv[:, :, None, :].to_broadcast([128, 8, NS, 31]), op=ALU.mult),
                     reads=["CS", "VT"], writes=["tmpc"])
                f.op('dve', lambda en: en.tensor_reduce(out=Y[:, :, 512:516], in_=tmpc[:, :, :, :], axis=AX.X, op=ALU.add),
                     reads=["tmpc"], writes=[("Y", c) for c in range(8)])
                f.op('dve', lambda en: en.tensor_tensor(out=Y[:, :, 512:516], in0=Y[:, :, 512:516],
                                                        in1=VT[:, DWB:DWB + 8][:, :, None].to_broadcast([128, 8, NS]), op=ALU.add),
                     reads=[("Y", c) for c in range(8)] + ["VT"], writes=[("Y", c) for c in range(8)])
            for (c0, cn) in segs(n):
                p1, p1k = f.pget()
                p2, p2k = f.pget()
                for c in range(8):
                    f.op('pe', lambda en, c=c: en.matmul(p1[:, 0:cn], lhsT=self.ones[:, :], rhs=Y[:, c, c0:c0 + cn],
                                                         start=(c == 0), stop=(c == 7)), reads=[("Y", c), "ones"], writes=[p1k])
                for c in range(8):
                    s, sk = self.scr()
                    f.op('act', lambda en, c=c, s=s: en.activation(out=s[:, 0:cn], in_=Y[:, c, c0:c0 + cn], func=AF.Square),
                         reads=[("Y", c)], writes=[sk])
                    f.op('pe', lambda en, c=c, s=s: en.matmul(p2[:, 0:cn], lhsT=self.ones[:, :], rhs=s[:, 0:cn],
                                                              start=(c == 0), stop=(c == 7)), reads=[sk, "ones"], writes=[p2k])
                f.op('act', lambda en: en.activation(out=mu[:, c0:c0 + cn], in_=p1[:, 0:cn], func=AF.Identity, scale=1.0 / D),
                     reads=[p1k], writes=["mu"])
                s, sk = self.scr()
                f.op('dve', lambda en, s=s: en.tensor_tensor(out=s[:, 0:cn], in0=mu[:, c0:c0 + cn], in1=mu[:, c0:c0 + cn], op=ALU.mult),
                     reads=["mu"], writes=[sk])
                f.op('dve', lambda en, s=s: en.scalar_tensor_tensor(out=s[:, 0:cn], in0=p2[:, 0:cn], scalar=1.0 / D, in1=s[:, 0:cn],
                                                                    op0=ALU.mult, op1=ALU.subtract), reads=[p2k, sk], writes=[sk])
                f.op('act', lambda en, s=s: en.activation(out=self.rstd[:, c0:c0 + cn], in_=s[:, 0:cn], func=AF.Sqrt, bias=1e-6, scale=1.0),
                     reads=[sk], writes=["rstd"])
                f.op('dve', lambda en: en.reciprocal(out=self.rstd[:, c0:c0 + cn], in_=self.rstd[:, c0:c0 + cn]),
                     reads=["rstd"], writes=["rstd"])
            for c in range(8):
                s, sk = self.scr()
                f.op('dve', lambda en, c=c, s=s: en.tensor_tensor(out=s[:, 0:n], in0=Y[:, c, 0:n], in1=mu[:, 0:n], op=ALU.subtract),
                     reads=[("Y", c), "mu"], writes=[sk])
                f.op('dve', lambda en, c=c, s=s: en.tensor_tensor(out=s[:, 0:n], in0=s[:, 0:n], in1=self.rstd[:, 0:n], op=ALU.mult),
                     reads=[sk, "rstd"], writes=[sk])
                f.op('act', lambda en, c=c, s=s: en.activation(out=self.xn[:, c, 0:n], in_=s[:, 0:n], func=AF.Silu,
                                                               bias=VT[:, LNB + c:LNB + c + 1], scale=VT[:, LNG + c:LNG + c + 1]),
                     reads=[sk, "VT"], writes=[("xn", c)])
            def cons(m, outs, ti=ti, t0=t0):
                for (ps, pk, c0, cn) in outs:
                    f.op('dve', lambda en, ps=ps: en.scalar_tensor_tensor(
                        out=X[:, m, t0 + c0:t0 + c0 + cn], in0=ps[:, 0:cn], scalar=VT[:, B2 + m:B2 + m + 1],
                        in1=X[:, m, t0 + c0:t0 + c0 + cn], op0=ALU.add, op1=ALU.add),
                        reads=[pk, "VT"] + xkeys(ti, [m]), writes=xkeys(ti, [m]))
            self.linear(I["conv_w2"], 8, D, lambda k: self.xn[:, k, :], [("xn", kk) for kk in range(8)], n, cons)
            self.ffn_ple(0, ti, t0, n)
        self.store_T(self.O["conv_p"][:, :], lambda c: Ulast[:, c, :], 30, ["Ulast"])
        Us = f.sb("Us", [128, 8, NS])
        f.op('dve', lambda en: en.tensor_copy(out=Us[:, :, :], in_=CS[:, :, :, 30]), reads=["CS"], writes=["Us"])
        i = self.si % 2
        self.si += 1
        st, sk = self.stg[i], "stg%d" % i
        for c0 in range(0, 8, 4):
            ps, pk = f.pget()
            for j in range(4):
                f.op('pe', lambda en, j=j, c0=c0: en.transpose(ps[0:NS, j * 128:(j + 1) * 128], Us[:, c0 + j, :], self.ids[:, :]),
                     reads=["Us", "ids"], writes=[pk])
            self.copy('dve', st[0:NS, c0 * 128:(c0 + 4) * 128], ps[0:NS, 0:512], [pk], [sk])
        for b in range(NS):
            f.dma(self.O["conv_s"][b * 30 + 29:b * 30 + 30, :], st[b:b + 1, 0:D], reads=[sk])

    def ffn_ple(self, L, ti, t0, n):
        f = self.f
        I = self.I
        X = self.X
        self.rmsnorm(ti, t0, n, NF0 if L == 0 else NF1)

        def cons_up(m, outs):
            for (ps, pk, c0, cn) in outs:
                s, sk = self.scr()
                f.op('act', lambda en, s=s, ps=ps: en.activation(out=s[:, 0:cn], in_=ps[:, 0:cn], func=AF.Relu), reads=[pk], writes=[sk])
                f.op('dve', lambda en, s=s: en.tensor_tensor(out=self.hT[:, m, c0:c0 + cn], in0=s[:, 0:cn], in1=s[:, 0:cn], op=ALU.mult),
                     reads=[sk], writes=[("hT", m)])
        self.linear(I["mlp_up"][L], 8, 4 * D, lambda k: self.xn[:, k, :], [("xn", kk) for kk in range(8)], n, cons_up)

        def cons_dn(m, outs):
            for (ps, pk, c0, cn) in outs:
                f.op('dve', lambda en, ps=ps: en.tensor_tensor(out=X[:, m, t0 + c0:t0 + c0 + cn], in0=ps[:, 0:cn],
                                                               in1=X[:, m, t0 + c0:t0 + c0 + cn], op=ALU.add),
                     reads=[pk] + xkeys(ti, [m]), writes=xkeys(ti, [m]))
        self.linear(I["mlp_down"][L], 32, D, lambda k: self.hT[:, k, :], [("hT", kk) for kk in range(32)], n, cons_dn, bw=128)
        self.rmsnorm(ti, t0, n, NP0 if L == 0 else NP1)
        pT = self.hT
        for r in range(4):
            def sink(c, g, ps, pk, r=r):
                self.copy(self.ev_engine(), pT[:, c:c + g, r * 128:(r + 1) * 128], ps[:, 0:g * 128].rearrange("p (c t) -> p c t", c=g),
                          [pk], [("hT", cc) for cc in range(c, c + g)])
            self.load_rows_T(I["pp"][L, t0 + r * 128:t0 + (r + 1) * 128, :], 128, 256, sink)
        if ti == 3:
            def sink2(c, g, ps, pk):
                self.copy('dve', pT[:, c:c + g, 512:516], ps[:, 0:g * NS].rearrange("p (c t) -> p c t", c=g),
                          [pk], [("hT", cc) for cc in range(c, c + g)])
            self.load_rows_T(I["psm"][L, :, :], NS, 256, sink2)
        wp, wpk = None, None
        for mb in range(2):
            wgt, wgk = self.wload(I["ple_gate"][L][:, mb * 512:(mb + 1) * 512], 8, 512)
            wp, wpk = self.wload(I["ple_proj"][L][:, mb * 512:(mb + 1) * 512], 2, 512)
            for j in range(4):
                m = mb * 4 + j
                for (c0, cn) in segs(n):
                    pa, pak = f.pget()
                    pb, pbk = f.pget()
                    for k in range(8):
                        f.op('pe', lambda en, k=k, j=j, pa=pa: en.matmul(pa[:, 0:cn], lhsT=wgt[:, k, j * 128:(j + 1) * 128],
                                                                          rhs=self.xn[:, k, c0:c0 + cn], start=(k == 0), stop=(k == 7)),
                             reads=[wgk] + [("xn", kk) for kk in range(8)], writes=[pak])
                    for k in range(2):
                        f.op('pe', lambda en, k=k, j=j, pb=pb: en.matmul(pb[:, 0:cn], lhsT=wp[:, k, j * 128:(j + 1) * 128],
                                                                          rhs=pT[:, k, c0:c0 + cn], start=(k == 0), stop=(k == 1)),
                             reads=[wpk, ("hT", 0), ("hT", 1)], writes=[pbk])
                    s, sk = self.scr()
                    f.op('act', lambda en, s=s, pa=pa: en.activation(out=s[:, 0:cn], in_=pa[:, 0:cn], func=AF.Sigmoid), reads=[pak], writes=[sk])
                    f.op('dve', lambda en, s=s, pb=pb: en.tensor_tensor(out=s[:, 0:cn], in0=pb[:, 0:cn], in1=s[:, 0:cn], op=ALU.mult),
                         reads=[pbk, sk], writes=[sk])
                    f.op('dve', lambda en, s=s, m=m: en.tensor_tensor(out=X[:, m, t0 + c0:t0 + c0 + cn], in0=s[:, 0:cn],
                                                                      in1=X[:, m, t0 + c0:t0 + c0 + cn], op=ALU.add),
                         reads=[sk] + xkeys(ti, [m]), writes=xkeys(ti, [m]))


    def gen_table(self, ohsrc, ncols, TabAug, dst_view_fn):
        f = self.f
        for h0 in range(0, ncols, 1024):
            hw = min(1024, ncols - h0)
            i = self.si % 2
            self.si += 1
            st, sk = self.stg[i], "stg%d" % i
            f.dma(st[0:34, 0:hw], ohsrc[:, h0:h0 + hw], writes=[sk])
            for c0 in range(0, hw, 512):
                cw = min(512, hw - c0)
                ps, pk = f.pget()
                f.op('pe', lambda en, c0=c0, cw=cw, ps=ps, st=st: en.matmul(ps[0:16, 0:cw], lhsT=TabAug[0:34, 0:16], rhs=st[0:34, c0:c0 + cw],
                                                                            start=True, stop=True), reads=[sk, "TabAug"], writes=[pk])
                o, ok = self.scr()
                self.copy('dve', o[0:16, 0:cw], ps[0:16, 0:cw], [pk], [ok])
                f.dma(self.bsc[:, h0 + c0:h0 + c0 + cw], o[0:16, 0:cw], reads=[ok], writes=["bsc"])

    def nsa(self):
        f = self.f
        I = self.I
        X = self.X
        VT = self.VT
        win = I["w_in"]
        with self.scope():
            KT = f.sb("KT", [128, 2, 2, T], BF16)
            VA = f.sb("VA", [128, 16, 2, 4, 65], BF16)
            KCT = f.sb("KCT", [128, 2, 128], BF16)
            VCM = f.sb("VCM", [128, 4, 64], BF16)
            f.op('pool', lambda en: en.memset(VA[:, :, :, :, :], 1.0), writes=["VA"])
            with self.scope():
                CT = f.sb("CT", [128, 2, 2, T], BF16)
                W1r = f.sb("W1r", [128, 2, 32, 128], BF16)
                W2k = f.sb("W2k", [128, 128], BF16)
                W2v = f.sb("W2v", [128, 64], BF16)
                VTb = f.sb("VTb", [128, 32], BF16)
                hb = f.sb("hb", [128, 2])
                shb = f.sb("shb", [128, 128], BF16)
                for ci, nm in enumerate(["ck_w1", "cv_w1"]):
                    src = I[nm].rearrange("(t h) m -> h t m", h=64)
                    f.dma(W1r[0:64, ci], src, writes=["W1r"], q='pool')
                    f.dma(W1r[64:128, ci], src, writes=["W1r"], q='pool')
                f.dma(W2k[:, 0:64], I["ck_w2"], writes=["W2k"], q='pool')
                f.dma(W2k[:, 64:128], I["ck_w2"], writes=["W2k"], q='pool')
                f.dma(W2v[:, :], I["cv_w2"], writes=["W2v"], q='pool')
                f.op('dve', lambda en: en.tensor_copy(out=VTb[:, :], in_=VT[:, PEK:PEK + 32]), reads=["VT"], writes=["VTb"])
                kvn = ["cmp_k_p", "cmp_v_p", "slc_k_p", "slc_v_p", "win_k_p", "win_v_p"]
                for ti, (t0, n) in enumerate(TILES):
                    self.rmsnorm(ti, t0, n, NM1)
                    xk = [("xn", kk) for kk in range(8)]
                    for cb in range(3):
                        wv, wk = self.wload(win[:, 1024 + cb * 512:1024 + (cb + 1) * 512], 8, 512)
                        for r in range(4):
                            rt = ti * 4 + r
                            ps, pk = f.pget()
                            for k in range(8):
                                f.op('pe', lambda en, k=k, r=r, ps=ps: en.matmul(ps[:, 0:512], lhsT=self.xn[:, k, r * 128:(r + 1) * 128],
                                                                                  rhs=wv[:, k, :], start=(k == 0), stop=(k == 7)),
                                     reads=[wk] + xk, writes=[pk])
                            if cb < 2 or ti == 3:
                                s_, sk_ = self.scr()
                                self.copy('act', s_[:, 0:512], ps[:, 0:512], [pk], [sk_])
                                for hh in range(2):
                                    nm = kvn[2 * cb + hh]
                                    row0 = (t0 + r * 128) if cb < 2 else r * 128
                                    f.dma(self.O[nm][row0:row0 + 128, :], s_[:, hh * 256:(hh + 1) * 256], reads=[sk_])
                            if ti == 3 and r == 0:
                                ps4, pk4 = f.pget()
                                for k in range(8):
                                    f.op('pe', lambda en, k=k, ps4=ps4: en.matmul(ps4[0:NS, 0:512], lhsT=self.xn[:, k, 512:516],
                                                                                  rhs=wv[:, k, :], start=(k == 0), stop=(k == 7)),
                                         reads=[wk] + xk, writes=[pk4])
                                s4, sk4 = self.scr()
                                self.copy('dve', s4[0:NS, 0:512], ps4[0:NS, 0:512], [pk4], [sk4])
                                if cb < 2:
                                    for hh in range(2):
                                        f.dma(self.O[["cmp_k_s", "cmp_v_s", "slc_k_s", "slc_v_s"][2 * cb + hh]][:, :],
                                              s4[0:NS, hh * 256:(hh + 1) * 256], reads=[sk4])
                                else:
                                    for hh, (onm, inm) in enumerate([("win_k_s", "swk"), ("win_v_s", "swv")]):
                                        for b in range(NS):
                                            f.dma(self.O[onm][b * 512 + 511:b * 512 + 512, :], s4[b:b + 1, hh * 256:(hh + 1) * 256], reads=[sk4])
                                            f.dma(self.O[onm][b * 512:b * 512 + 511, :], I[inm][b * 512 + 1:b * 512 + 512, :])
                            if cb >= 1:
                                f.op('dve', lambda en, ps=ps, rt=rt, cb=cb: en.tensor_copy(
                                    out=VA[:, rt, cb - 1, :, 0:64], in_=ps[:, 256:512].rearrange("p (g d) -> p g d", g=4)),
                                    reads=[pk], writes=["VA"])
                    for ty, base in enumerate([1024, 1280, 1536, 2048]):
                        if self.stage < 2.2:
                            break
                        def cons(m, outs, ty=ty, t0=t0):
                            for (ps, pk, c0, cn) in outs:
                                if c0 != 0:
                                    continue
                                dst = CT[:, ty, m, t0:t0 + 512] if ty < 2 else KT[:, ty - 2, m, t0:t0 + 512]
                                key = ("CT", ty) if ty < 2 else ("KT", ty - 2)
                                self.copy(self.ev_engine(), dst, ps[:, 0:512], [pk], [key])
                        self.linear(win[:, base:base + 256], 8, 256, lambda k: self.xn[:, k, :], xk, 512, cons, bw=256)
                for ci in range(2):
                    if self.stage < 2.3:
                        break
                    pss = [f.pget(), f.pget()]
                    for hf in range(2):
                        ps, pk = pss[hf]
                        for r in range(16):
                            t = 2 * r + hf
                            f.op('pe', lambda en, t=t, r=r, hf=hf, ps=ps, ci=ci: en.matmul(
                                ps[:, 0:1], lhsT=W1r[hf * 64:(hf + 1) * 64, ci, t, :], rhs=VTb[hf * 64:(hf + 1) * 64, ci * 16 + r:ci * 16 + r + 1],
                                start=(r == 0), stop=(r == 15)), reads=["W1r", "VTb"], writes=[pk])
                    self.copy('dve', hb[:, ci:ci + 1], pss[0][0][:, 0:1], [pss[0][1]], ["hb"])
                    f.op('dve', lambda en, ci=ci: en.tensor_tensor(out=hb[:, ci:ci + 1], in0=pss[1][0][:, 0:1], in1=hb[:, ci:ci + 1], op=ALU.add),
                         reads=[pss[1][1], "hb"], writes=["hb"])
                for ci in range(2):
                    if self.stage < 2.4:
                        break
                    for g in range(4):
                        hf, sl = g % 2, g // 2
                        P0 = hf * 64
                        ps, pk = f.pget()
                        for t in range(32):
                            f.op('pe', lambda en, t=t, ps=ps, ci=ci, P0=P0, sl=sl: en.matmul(
                                ps[:, 0:127], lhsT=W1r[P0:P0 + 64, ci, t, :], rhs=CT[P0:P0 + 64, ci, sl, t:t + 16 * 126 + 1:16],
                                start=(t == 0), stop=(t == 31)), reads=["W1r", ("CT", ci)], writes=[pk])
                        f.op('act', lambda en, ps=ps, ci=ci: en.activation(out=shb[:, 0:127], in_=ps[:, 0:127], func=AF.Silu, bias=hb[:, ci:ci + 1]),
                             reads=[pk, "hb"], writes=["shb"])
                        ps2, pk2 = f.pget()
                        if ci == 0:
                            f.op('pe', lambda en, ps2=ps2: en.matmul(ps2[:, 0:127], lhsT=W2k[:, :], rhs=shb[:, 0:127], start=True, stop=True),
                                 reads=["W2k", "shb"], writes=[pk2])
                            self.copy('dve', KCT[P0:P0 + 64, sl, 0:127], ps2[P0:P0 + 64, 0:127], [pk2], ["KCT"])
                        else:
                            f.op('pe', lambda en, ps2=ps2: en.matmul(ps2[0:127, 0:64], lhsT=shb[:, 0:127], rhs=W2v[:, :], start=True, stop=True),
                                 reads=["W2v", "shb"], writes=[pk2])
                            self.copy('dve', VCM[0:127, g, :], ps2[0:127, 0:64], [pk2], ["VCM"])
            if self.stage < 3:
                return
            with self.scope():
                TabAug = f.sb("TabAug", [34, 16])
                BTd = f.sb("BTd", [128, 16, 128], BF16)
                BTo = f.sb("BTo", [128, 16, 128], BF16)
                BTt = f.sb("BTt", [128, 128], BF16)
                Gtab = f.sb("Gtab", [128, 16, 254], BF16)
                FV = f.sb("FV", [128, 16, 32])
                Ex = f.sb("Ex", [32, 16, 128], BF16)
                t31 = f.sb("t31", [1, 16])
                C31 = f.sb("C31", [1, 16, 128], BF16)
                onesb = f.sb("onesb", [1, 128], BF16)
                qT = f.sb("qT", [128, 16, 512], BF16)
                G = f.sb("G", [128, 4, 48])
                Oall = f.sb("Oall", [128, 16, 64])
                scm = f.sb("scm", [128, 4, 127])
                ee = f.sb("ee", [128, 4, 127])
                sm = f.sb("sm", [128, 64])
                PG = f.sb("PG", [128, 132])
                sco = f.sb("sco", [128, 96])
                NST = f.sb("NST", [32, 128], BF16)
                pT = f.sb("pT", [128, 512], BF16)
                PT = [f.sb("PT%d" % i, [128, 512], BF16) for i in range(2)]
                tmpo = f.sb("tmpo", [128, 4, 64])
                f.dma(TabAug[0:32, :], I["rel_table"], writes=["TabAug"])
                f.op('pool', lambda en: en.memset(TabAug[32:33, :], NEG), writes=["TabAug"])
                f.dma(TabAug[33:34, :], I["rel_table"][31:32, :], writes=["TabAug"])
                f.dma(t31[0:1, :], I["rel_table"][31:32, :], writes=["t31"])
                f.op('dve', lambda en: en.tensor_copy(out=C31[0:1, :, :], in_=t31[0:1, :][:, :, None].to_broadcast([1, 16, 128])),
                     reads=["t31"], writes=["C31"])
                f.op('pool', lambda en: en.memset(onesb[0:1, :], 1.0), writes=["onesb"])
                f.op('pool', lambda en: en.memset(PG[:, :], 0.0), writes=["PG"])
                f.dma(FV[:, :, :], I["fvtab"].rearrange("p (a b) -> p a b", a=16), writes=["FV"])
                f.dma(Ex[:, :, :], I["exm"].rearrange("p (a b) -> p a b", a=16), writes=["Ex"], q='pool')
                f.dma(BTt[:, :], I["tailm"], writes=["BTt"], q='pool')
                self.gen_table(I["ohd"], 16384, TabAug, None)
                f.dma(BTd[:, :, :], self.bsc[:, 0:16384].rearrange("h (k q) -> k h q", k=128), reads=["bsc"], writes=["BTd"], q='pool')
                self.gen_table(I["oho"], 16384, TabAug, None)
                f.dma(BTo[:, :, :], self.bsc[:, 0:16384].rearrange("h (k q) -> k h q", k=128), reads=["bsc"], writes=["BTo"], q='pool')
                self.gen_table(I["ohg"], 32512, TabAug, None)
                f.dma(Gtab[:, :, :], self.bsc[:, 0:32512].rearrange("h (q m) -> q h m", q=128), reads=["bsc"], writes=["Gtab"], q='pool')
                f.op('pool', lambda en: en.memset(qT[:, :, :], 0.0), writes=["qT"])

                for ti, (t0, n) in enumerate(TILES):
                    if self.stage < 4:
                        break
                    self.rmsnorm(ti, t0, n, NM1)
                    xk = [("xn", kk) for kk in range(8)]
                    for half in range(2):
                        wi_ = self.wi % 3
                        self.wi += 1
                        wb_, wqk = self.wbuf[wi_], "wb%d" % wi_
                        wq5 = wb_[:, 0:4096].rearrange("p (k s u i) -> p k s u i", k=8, s=4, u=2)
                        src5 = win[:, half * 512:(half + 1) * 512].rearrange("(k p) (u s i) -> p k u s i", p=128, u=2, s=4)
                        for u in range(2):
                            for s4 in range(4):
                                f.dma(wq5[:, :, s4, u, :], src5[:, :, u, s4, :], writes=[wqk], q='pool')
                        wq = wb_[:, 0:4096].rearrange("p (k m) -> p k m", k=8)
                        for sl4 in range(4):
                            s_ = half * 4 + sl4
                            A = sl4
                            ps, pk = f.pget()
                            for k in range(8):
                                f.op('pe', lambda en, k=k, A=A, ps=ps: en.matmul(
                                    ps[:, 0:512],
                                    lhsT=wq[:, k, 128 * A:128 * A + 128],
                                    rhs=self.xn[:, k, 0:512], start=(k == 0), stop=(k == 7)), reads=[wqk] + xk, writes=[pk])
                            gA = 2 * half
                            f.op('act', lambda en, ps=ps, gA=gA, A=A: en.activation(out=qT[0:64, 4 * gA + A, :], in_=ps[0:64, 0:512], func=AF.Identity, scale=0.125),
                                 reads=[pk], writes=["qT"])
                            f.op('act', lambda en, ps=ps, gA=gA, A=A: en.activation(out=qT[64:128, 4 * (gA + 1) + A, :], in_=ps[64:128, 0:512], func=AF.Identity, scale=0.125),
                                 reads=[pk], writes=["qT"])
                    wg, wgk = self.wload(win[:, 2560:2608], 8, 48)
                    for r in range(4):
                        ps, pk = f.pget()
                        for k in range(8):
                            f.op('pe', lambda en, k=k, r=r, ps=ps: en.matmul(ps[:, 0:48], lhsT=self.xn[:, k, r * 128:(r + 1) * 128], rhs=wg[:, k, :],
                                                                              start=(k == 0), stop=(k == 7)), reads=[wgk] + xk, writes=[pk])
                        f.op('act', lambda en, ps=ps, r=r: en.activation(out=G[:, r, :], in_=ps[:, 0:48], func=AF.Sigmoid), reads=[pk], writes=["G"])
                    OT = self.xn
                    for r in range(4):
                        qt = ti * 4 + r
                        qs = slice(r * 128, (r + 1) * 128)
                        for g in range(4):
                            hf, sl = g % 2, g // 2
                            P0 = hf * 64
                            s0 = 4 * sl
                            ps, pk = f.pget()
                            for j in range(4):
                                f.op('pe', lambda en, j=j, ps=ps: en.matmul(ps[:, j * 127:(j + 1) * 127], lhsT=qT[:, 4 * g + j, qs],
                                                                            rhs=KCT[:, sl, 0:127], start=True, stop=True),
                                     reads=["qT", "KCT"], writes=[pk])
                            f.op('dve', lambda en, ps=ps: en.tensor_tensor(out=scm[:, :, :], in0=ps[:, 0:508].rearrange("p (h j) -> p h j", h=4),
                                                                           in1=Gtab[:, 4 * g:4 * g + 4, 127 - 8 * qt:254 - 8 * qt], op=ALU.add),
                                 reads=[pk, "Gtab"], writes=["scm"])
                            f.op('dve', lambda en: en.tensor_reduce(out=sm[:, 0:4], in_=scm[:, :, :], axis=AX.X, op=ALU.max), reads=["scm"], writes=["sm"])
                            f.op('dve', lambda en: en.tensor_scalar(out=sm[:, 4:8], in0=sm[:, 0:4], scalar1=-10000.0, scalar2=-1.0, op0=ALU.max, op1=ALU.mult),
                                 reads=["sm"], writes=["sm"])
                            f.op('dve', lambda en: en.memset(sm[:, 8:12], 0.0), reads=["sm"], writes=["sm"])
                            for j in range(4):
                                f.op('act', lambda en, j=j: en.activation(out=ee[:, j, :], in_=scm[:, j, :], func=AF.Exp, bias=sm[:, 4 + j:5 + j],
                                                                         accum_out=sm[:, 8 + j:9 + j]), reads=["scm", "sm"], writes=["ee", "sm"])
                            f.op('dve', lambda en: en.tensor_scalar(out=sm[:, 12:16], in0=sm[:, 8:12], scalar1=1e-30, scalar2=None, op0=ALU.max),
                                 reads=["sm"], writes=["sm"])
                            f.op('dve', lambda en: en.reciprocal(out=sm[:, 12:16], in_=sm[:, 12:16]), reads=["sm"], writes=["sm"])
                            f.op('dve', lambda en: en.tensor_tensor(out=ee[:, :, :], in0=ee[:, :, :], in1=sm[:, 12:16][:, :, None].to_broadcast([128, 4, 127]),
                                                                    op=ALU.mult), reads=["ee", "sm"], writes=["ee"])
                            f.op('dve', lambda en: en.tensor_reduce(out=PG[:, 1:128], in_=ee[:, :, :].rearrange("p h j -> p j h"), axis=AX.X, op=ALU.add),
                                 reads=["ee"], writes=["PG"])
                            f.op('dve', lambda en: en.tensor_tensor(out=sco[:, 0:32], in0=PG[:, 1:129:4], in1=PG[:, 2:130:4], op=ALU.add), reads=["PG"], writes=["sco"])
                            f.op('dve', lambda en: en.tensor_tensor(out=sco[:, 0:32], in0=sco[:, 0:32], in1=PG[:, 3:131:4], op=ALU.add), reads=["PG", "sco"], writes=["sco"])
                            f.op('dve', lambda en: en.scalar_tensor_tensor(out=sco[:, 0:32], in0=sco[:, 0:32], scalar=2.0, in1=PG[:, 0:128:4],
                                                                           op0=ALU.mult, op1=ALU.add), reads=["PG", "sco"], writes=["sco"])
                            f.op('dve', lambda en: en.tensor_tensor(out=sco[:, 0:32], in0=sco[:, 0:32], in1=PG[:, 4:132:4], op=ALU.add), reads=["PG", "sco"], writes=["sco"])
                            f.op('dve', lambda en: en.tensor_tensor(out=sco[:, 0:32], in0=sco[:, 0:32], in1=FV[:, qt, :], op=ALU.add), reads=["FV", "sco"], writes=["sco"])
                            f.op('dve', lambda en: en.max(out=sm[:, 16:24], in_=sco[:, 0:32]), reads=["sco"], writes=["sm"])
                            f.op('dve', lambda en: en.match_replace(out=sco[:, 32:64], in_to_replace=sm[:, 16:24], in_values=sco[:, 0:32], imm_value=-1e30),
                                 reads=["sco", "sm"], writes=["sco"])
                            f.op('dve', lambda en: en.max(out=sm[:, 24:32], in_=sco[:, 32:64]), reads=["sco"], writes=["sm"])
                            f.op('dve', lambda en: en.tensor_scalar(out=sco[:, 64:96], in0=sco[:, 0:32], scalar1=sm[:, 31:32], scalar2=None, op0=ALU.is_ge),
                                 reads=["sco", "sm"], writes=["sco"])
                            f.op('dve', lambda en: en.tensor_scalar(out=sco[:, 64:96], in0=sco[:, 64:96], scalar1=-1.0, scalar2=-NEG, op0=ALU.add, op1=ALU.mult),
                                 reads=["sco"], writes=["sco"])
                            ps, pk = f.pget()
                            f.op('pe', lambda en, ps=ps: en.transpose(ps[0:32, 0:128], sco[:, 64:96], self.ids[:, :]), reads=["sco", "ids"], writes=[pk])
                            self.copy('act', NST[0:32, :], ps[0:32, 0:128], [pk], ["NST"])
                            ps, pk = f.pget()
                            for j in range(4):
                                f.op('pe', lambda en, j=j, ps=ps: en.transpose(ps[0:127, j * 128:(j + 1) * 128], ee[:, j, :], self.ids[:, :]),
                                     reads=["ee", "ids"], writes=[pk])
                            self.copy('act', pT[0:127, :], ps[0:127, 0:512], [pk], ["pT"])
                            ps, pk = f.pget()
                            for j in range(4):
                                f.op('pe', lambda en, j=j, ps=ps: en.matmul(ps[:, j * 64:(j + 1) * 64], lhsT=pT[0:127, j * 128:(j + 1) * 128],
                                                                            rhs=VCM[0:127, g, :], start=True, stop=True), reads=["pT", "VCM"], writes=[pk])
                            f.op('dve', lambda en, ps=ps: en.tensor_tensor(out=Oall[:, 4 * g:4 * g + 4, :], in0=ps[:, 0:256].rearrange("p (h d) -> p h d", h=4),
                                                                           in1=G[:, r, 4 * g:4 * g + 4][:, :, None].to_broadcast([128, 4, 64]), op=ALU.mult),
                                 reads=[pk, "G"], writes=[("Oall", g)])
                            for br in range(2):
                                if self.stage < 5 + br:
                                    break
                                pa, pak = f.pacc(br)
                                kts = list(range(0, qt + 1)) if br == 0 else list(range(max(0, qt - 4), qt + 1))
                                for kt in kts:
                                    ps, pk = f.pget()
                                    special = (kt == qt) or (kt == qt - 1) or (br == 1 and kt == qt - 4)
                                    rd = [("KT", br), "qT", "C31", "onesb"]
                                    f.op('pe', lambda en, ps=ps, kt=kt, br=br: en.matmul(ps[:, 0:512], lhsT=KT[:, br, sl, kt * 128:(kt + 1) * 128],
                                                                                       rhs=qT[:, 4 * g:4 * g + 4, qs], start=True, stop=False),
                                         reads=rd, writes=[pk])
                                    if br == 0:
                                        f.op('pe', lambda en, ps=ps, kt=kt: en.matmul(ps[:, 0:512], lhsT=Ex[0:32, kt, :],
                                                                                    rhs=NST[0:32, :][:, None, :].to_broadcast([32, 4, 128]), start=False, stop=False),
                                             reads=["Ex", "NST"], writes=[pk])
                                    f.op('pe', lambda en, ps=ps, sp_=special: en.matmul(ps[:, 0:512], lhsT=onesb[0:1, :], rhs=C31[0:1, 4 * g:4 * g + 4, :],
                                                                                       start=False, stop=(not sp_)), reads=rd, writes=[pk])
                                    if kt == qt:
                                        f.op('pe', lambda en, ps=ps: en.matmul(ps[:, 0:512], lhsT=self.idb[:, :], rhs=BTd[:, 4 * g:4 * g + 4, :], start=False, stop=True),
                                             reads=["idb", "BTd"], writes=[pk])
                                    elif kt == qt - 1:
                                        f.op('pe', lambda en, ps=ps: en.matmul(ps[:, 0:512], lhsT=self.idb[:, :], rhs=BTo[:, 4 * g:4 * g + 4, :], start=False, stop=True),
                                             reads=["idb", "BTo"], writes=[pk])
                                    elif br == 1 and kt == qt - 4:
                                        f.op('pe', lambda en, ps=ps: en.matmul(ps[:, 0:512], lhsT=self.idb[:, :],
                                                                               rhs=BTt[:, :][:, None, :].to_broadcast([128, 4, 128]), start=False, stop=True),
                                             reads=["idb", "BTt"], writes=[pk])
                                    pt_, ptk = PT[kt % 2], "PT%d" % (kt % 2)
                                    f.op('act', lambda en, ps=ps, pt_=pt_: en.activation(out=pt_[:, :], in_=ps[:, 0:512], func=AF.Exp), reads=[pk], writes=[ptk])
                                    for j in range(4):
                                        f.op('pe', lambda en, j=j, pt_=pt_, kt=kt, br=br, kts=kts: en.matmul(
                                            pa[:, j * 65:(j + 1) * 65], lhsT=pt_[:, j * 128:(j + 1) * 128], rhs=VA[:, kt, br, g, :],
                                            start=(kt == kts[0] and j == 0), stop=(kt == kts[-1]), skip_group_check=True), reads=[ptk, "VA"], writes=[pak])
                                pav = pa[:, 0:260].rearrange("p (h d) -> p h d", h=4)
                                f.op('dve', lambda en, pav=pav: en.tensor_scalar(out=sm[:, 32:36], in0=pav[:, :, 64], scalar1=1e-30, scalar2=None, op0=ALU.max),
                                     reads=[pak], writes=["sm"])
                                f.op('dve', lambda en: en.reciprocal(out=sm[:, 32:36], in_=sm[:, 32:36]), reads=["sm"], writes=["sm"])
                                f.op('dve', lambda en, br=br: en.tensor_tensor(out=sm[:, 32:36], in0=sm[:, 32:36], in1=G[:, r, 16 * (br + 1) + 4 * g:16 * (br + 1) + 4 * g + 4],
                                                                               op=ALU.mult), reads=["sm", "G"], writes=["sm"])
                                f.op('dve', lambda en, pav=pav: en.tensor_tensor(out=tmpo[:, :, :], in0=pav[:, :, 0:64],
                                                                                 in1=sm[:, 32:36][:, :, None].to_broadcast([128, 4, 64]), op=ALU.mult),
                                     reads=[pak, "sm"], writes=["tmpo"])
                                f.op('pool', lambda en: en.tensor_tensor(out=Oall[:, 4 * g:4 * g + 4, :], in0=Oall[:, 4 * g:4 * g + 4, :], in1=tmpo[:, :, :], op=ALU.add),
                                     reads=["tmpo", ("Oall", g)], writes=[("Oall", g)])
                        O2 = Oall[:, :, :].rearrange("p h d -> p (h d)")
                        for c0 in range(0, 8, 4):
                            ps, pk = f.pget()
                            for j in range(4):
                                f.op('pe', lambda en, j=j, c0=c0, ps=ps: en.transpose(ps[:, j * 128:(j + 1) * 128], O2[:, (c0 + j) * 128:(c0 + j + 1) * 128], self.ids[:, :]),
                                     reads=[("Oall", gg) for gg in range(4)] + ["ids"], writes=[pk])
                            self.copy(self.ev_engine(), OT[:, c0:c0 + 4, qs], ps[:, 0:512].rearrange("p (c t) -> p c t", c=4), [pk],
                                      [("xn", cc) for cc in range(c0, c0 + 4)])
                    def cons_o(m, outs, ti=ti, t0=t0):
                        for (ps, pk, c0, cn) in outs:
                            f.op('dve', lambda en, ps=ps: en.tensor_tensor(out=X[:, m, t0 + c0:t0 + c0 + cn], in0=ps[:, 0:cn],
                                                                           in1=X[:, m, t0 + c0:t0 + c0 + cn], op=ALU.add),
                                 reads=[pk, ("X", m, ti)], writes=[("X", m, ti)])
                    self.linear(I["w_out"], 8, D, lambda k: OT[:, k, :], [("xn", kk) for kk in range(8)], 512, cons_o)


    def gather_KT_V(self, pool, IDX, b, page0, npages, CT, ci, VAs=None, plain=None):
        f = self.f
        for r0 in range(0, npages, 4):
            i = self.si % 2
            self.si += 1
            st, sk = self.stg[i], "stg%d" % i
            for j in range(4):
                pg_ = page0 + r0 + j
                if plain is not None:
                    f.dma(st[:, j * 256:(j + 1) * 256], plain[(r0 + j) * 128:(r0 + j + 1) * 128, :], writes=[sk])
                else:
                    f.dma(None, None, reads=["IDX"], writes=[sk], q='pool',
                          fn=lambda en, j=j, pg_=pg_, st=st: en.indirect_dma_start(
                              out=st[:, j * 256:(j + 1) * 256], out_offset=None, in_=pool[:, :],
                              in_offset=bass.IndirectOffsetOnAxis(ap=IDX[:, b, pg_:pg_ + 1], axis=0)))
            if VAs is not None:
                f.op('act', lambda en, st=st, r0=r0: en.activation(
                    out=VAs[:, r0:r0 + 4, :, 0:64], in_=st[:, 0:1024].rearrange("p (r g d) -> p r g d", r=4, g=4), func=AF.Identity),
                    reads=[sk], writes=["VAs"])
                continue
            for gp in range(2):
                ps, pk = f.pget()
                for j in range(4):
                    f.op('pe', lambda en, j=j, gp=gp, ps=ps, st=st: en.transpose(
                        ps[:, j * 128:(j + 1) * 128], st[:, j * 256 + gp * 128:j * 256 + (gp + 1) * 128], self.ids[:, :]),
                        reads=[sk, "ids"], writes=[pk])
                self.copy(self.ev_engine(), CT[:, ci, gp, r0 * 128:(r0 + 4) * 128], ps[:, 0:512], [pk], [("CT", ci)])

    def nsa_samples(self):
        f = self.f
        I = self.I
        X = self.X
        VT = self.VT
        win = I["w_in"]
        with self.scope():
            CT = f.sb("CTs", [128, 2, 2, T], BF16)
            CTx = f.sb("CTx", [128, 2, 2, 7, 32], BF16)
            W1r = f.sb("W1rs", [128, 2, 32, 128], BF16)
            W2k = f.sb("W2ks", [128, 128], BF16)
            W2v = f.sb("W2vs", [128, 64], BF16)
            VTb = f.sb("VTbs", [128, 32], BF16)
            hb = f.sb("hbs", [128, 2])
            shb = f.sb("shbs", [128, 128], BF16)
            KCTs = f.sb("KCTs", [128, 2, 1024], BF16)
            VCMs = f.sb("VCMs", [128, 8, 4, 64], BF16)
            VAs = f.sb("VAs", [128, 16, 4, 65], BF16)
            PTi = f.sb("PTi", [128, NS, 128], I32)
            IDX = f.sb("IDX", [128, NS, 128], I32)
            iop = f.sb("iop", [128, 1], I32)
            iopf = f.sb("iopf", [128, 1])
            IDXf = self.sc[0][:, 0:NS * 128] if False else None
            QZS = f.sb("QZS", [128, NS, 16], BF16)
            KTn = f.sb("KTn", [128, 2, 2, NS], BF16)
            VN = f.sb("VN", [1, NS, 2, 4, 65], BF16)
            gT = f.sb("gT", [48, NS])
            Jm = f.sb("Jm", [48, 4])
            Rm = f.sb("Rm", [48, 12])
            Lg = f.sb("Lg", [48, 4])
            G4 = f.sb("G4", [4, 12])
            TabAug = f.sb("TabAugs", [34, 16])
            OHs = f.sb("OHs", [34, 1024])
            OH127 = f.sb("OH127", [34, 128])
            B127 = f.sb("B127", [128, 16], BF16)
            trow = f.sb("trow", [1, 32])
            C31s = f.sb("C31s", [1, 16], BF16)
            T0s = f.sb("T0s", [1, 16], BF16)
            onesb = f.sb("onesbs", [1, 128], BF16)
            E2 = f.sb("E2", [33, 128], BF16)
            NS2 = f.sb("NS2", [33, 4, 128], BF16)
            FVs = f.sb("FVs", [1, 256])
            scs = f.sb("scs", [4, 1024])
            ees = f.sb("ees", [4, 1024])
            bss = f.sb("bss", [4, 1024])
            sms = f.sb("sms", [4, 16])
            PGs = f.sb("PGs", [1, 1032])
            sco = f.sb("scos", [1, 784])
            NSr = f.sb("NSr", [1, 384], BF16)
            pTs = f.sb("pTs", [128, 8, 4], BF16)
            Ps = f.sb("Ps", [128, 64], BF16)
            Pn = f.sb("Pn", [1, 4], BF16)
            Ob = f.sb("Ob", [4, 3, 4, 64])
            Osum = f.sb("Osum", [4, 4, 128])
            tmpo = f.sb("tmpos", [4, 4, 64])
            shx = f.sb("shx", [128, 8], BF16)
            vx = f.sb("vx", [8, 64], BF16)
            OTs = f.sb("OTs", [128, 8, NS], BF16)
            for ci, nm in enumerate(["ck_w1", "cv_w1"]):
                src = I[nm].rearrange("(t h) m -> h t m", h=64)
                f.dma(W1r[0:64, ci], src, writes=["W1r"], q='pool')
                f.dma(W1r[64:128, ci], src, writes=["W1r"], q='pool')
            f.dma(W2k[:, 0:64], I["ck_w2"], writes=["W2k"], q='pool')
            f.dma(W2k[:, 64:128], I["ck_w2"], writes=["W2k"], q='pool')
            f.dma(W2v[:, :], I["cv_w2"], writes=["W2v"], q='pool')
            f.op('dve', lambda en: en.tensor_copy(out=VTb[:, :], in_=VT[:, PEK:PEK + 32]), reads=["VT"], writes=["VTb"])
            for ci in range(2):
                pss = [f.pget(), f.pget()]
                for hf in range(2):
                    ps, pk = pss[hf]
                    for r in range(16):
                        t = 2 * r + hf
                        f.op('pe', lambda en, t=t, r=r, hf=hf, ps=ps, ci=ci: en.matmul(
                            ps[:, 0:1], lhsT=W1r[hf * 64:(hf + 1) * 64, ci, t, :], rhs=VTb[hf * 64:(hf + 1) * 64, ci * 16 + r:ci * 16 + r + 1],
                            start=(r == 0), stop=(r == 15)), reads=["W1r", "VTb"], writes=[pk])
                self.copy('dve', hb[:, ci:ci + 1], pss[0][0][:, 0:1], [pss[0][1]], ["hb"])
                f.op('dve', lambda en, ci=ci: en.tensor_tensor(out=hb[:, ci:ci + 1], in0=pss[1][0][:, 0:1], in1=hb[:, ci:ci + 1], op=ALU.add),
                     reads=[pss[1][1], "hb"], writes=["hb"])
            f.dma(TabAug[0:32, :], I["rel_table"], writes=["TabAug"])
            f.op('pool', lambda en: en.memset(TabAug[32:33, :], NEG), writes=["TabAug"])
            f.dma(TabAug[33:34, :], I["rel_table"][31:32, :], writes=["TabAug"])
            f.dma(OHs[:, :], I["ohs"], writes=["OHs"])
            f.dma(OH127[:, :], I["oh127"], writes=["OH127"])
            f.dma(trow[0:1, 0:16], I["rel_table"][31:32, :], writes=["trow"])
            f.dma(trow[0:1, 16:32], I["rel_table"][0:1, :], writes=["trow"])
            f.op('dve', lambda en: en.tensor_copy(out=C31s[0:1, :], in_=trow[0:1, 0:16]), reads=["trow"], writes=["C31s"])
            f.op('dve', lambda en: en.tensor_copy(out=T0s[0:1, :], in_=trow[0:1, 16:32]), reads=["trow"], writes=["T0s"])
            f.op('pool', lambda en: en.memset(onesb[0:1, :], 1.0), writes=["onesb"])
            f.dma(E2[:, :], I["e2"], writes=["E2"], q='pool')
            f.dma(FVs[:, :], I["fvs"], writes=["FVs"])
            f.dma(Jm[:, :], I["jm"], writes=["Jm"])
            f.dma(Rm[:, :], I["rm"], writes=["Rm"])
            f.op('pool', lambda en: en.memset(NS2[:, :, :], 0.0), writes=["NS2"])
            f.op('pool', lambda en: en.memset(PGs[:, :], 0.0), writes=["PGs"])
            f.op('pool', lambda en: en.memset(VAs[:, :, :, :], 1.0), writes=["VAs"])
            f.op('pool', lambda en: en.memset(VN[:, :, :, :, :], 1.0), writes=["VN"])
            f.op('pool', lambda en: en.memset(QZS[:, :, :], 0.0), writes=["QZS"])
            f.op('pool', lambda en: en.memset(Osum[:, :, :], 0.0), writes=["Osum"])
            ps, pk = f.pget()
            f.op('pe', lambda en, ps=ps: en.matmul(ps[:, 0:16], lhsT=OH127[0:34, :], rhs=TabAug[0:34, :], start=True, stop=True),
                 reads=["OH127", "TabAug"], writes=[pk])
            self.copy('dve', B127[:, :], ps[:, 0:16], [pk], ["B127"])
            f.dma(PTi[:, :, :].rearrange("p a b -> p (a b)"), I["ptab"].partition_broadcast(128), writes=["PTi"])
            f.op('pool', lambda en: en.iota(iop[:, 0:1], pattern=[[0, 1]], base=0, channel_multiplier=1), writes=["iop"])
            IDXf, _ = self.scr()
            IDXf = IDXf[:, 0:NS * 128]
            f.op('dve', lambda en: en.tensor_copy(out=iopf[:, 0:1], in_=iop[:, 0:1]), reads=["iop"], writes=["iopf"])
            f.op('dve', lambda en: en.tensor_copy(out=IDXf[:, :], in_=PTi[:, :, :].rearrange("p a b -> p (a b)")), reads=["PTi"], writes=["sc0", "sc1", "sc2"])
            f.op('dve', lambda en: en.tensor_scalar(out=IDXf[:, :], in0=IDXf[:, :], scalar1=128.0, scalar2=iopf[:, 0:1], op0=ALU.mult, op1=ALU.add),
                 reads=["iopf", "sc0", "sc1", "sc2"], writes=["sc0", "sc1", "sc2"])
            f.op('dve', lambda en: en.tensor_copy(out=IDX[:, :, :].rearrange("p a b -> p (a b)"), in_=IDXf[:, :]), reads=["sc0", "sc1", "sc2"], writes=["IDX"])
            self.rmsnorm(3, 1536, 516, NM1)
            xk = [("xn", kk) for kk in range(8)]
            xs_ = lambda k: self.xn[:, k, 512:516]
            for half in range(2):
                wi_ = self.wi % 3
                self.wi += 1
                wb_, wqk = self.wbuf[wi_], "wb%d" % wi_
                wq5 = wb_[:, 0:4096].rearrange("p (k s u i) -> p k s u i", k=8, s=4, u=2)
                src5 = win[:, half * 512:(half + 1) * 512].rearrange("(k p) (u s i) -> p k u s i", p=128, u=2, s=4)
                for u in range(2):
                    for s4 in range(4):
                        f.dma(wq5[:, :, s4, u, :], src5[:, :, u, s4, :], writes=[wqk], q='pool')
                wq = wb_[:, 0:4096].rearrange("p (k m) -> p k m", k=8)
                for A in range(4):
                    ps, pk = f.pget()
                    for k in range(8):
                        f.op('pe', lambda en, k=k, A=A, ps=ps: en.matmul(ps[:, 0:NS], lhsT=wq[:, k, 128 * A:128 * A + 128], rhs=xs_(k),
                                                                         start=(k == 0), stop=(k == 7)), reads=[wqk] + xk, writes=[pk])
                    gA = 2 * half
                    f.op('act', lambda en, ps=ps, gA=gA, A=A: en.activation(out=QZS[0:64, :, 4 * gA + A], in_=ps[0:64, 0:NS], func=AF.Identity, scale=0.125),
                         reads=[pk], writes=["QZS"])
                    f.op('act', lambda en, ps=ps, gA=gA, A=A: en.activation(out=QZS[64:128, :, 4 * (gA + 1) + A], in_=ps[64:128, 0:NS], func=AF.Identity, scale=0.125),
                         reads=[pk], writes=["QZS"])
            wg, wgk = self.wload(win[:, 2560:2608], 8, 48)
            ps, pk = f.pget()
            for k in range(8):
                f.op('pe', lambda en, k=k, ps=ps: en.matmul(ps[0:48, 0:NS], lhsT=wg[:, k, 0:48], rhs=xs_(k), start=(k == 0), stop=(k == 7)),
                     reads=[wgk] + xk, writes=[pk])
            f.op('act', lambda en, ps=ps: en.activation(out=gT[:, :], in_=ps[0:48, 0:NS], func=AF.Sigmoid), reads=[pk], writes=["gT"])
            for ty, base in enumerate([1536, 2048]):
                wv, wk = self.wload(win[:, base:base + 512], 8, 512)
                for m in range(2):
                    ps, pk = f.pget()
                    for k in range(8):
                        f.op('pe', lambda en, k=k, m=m, ps=ps: en.matmul(ps[:, 0:NS], lhsT=wv[:, k, m * 128:(m + 1) * 128], rhs=xs_(k),
                                                                         start=(k == 0), stop=(k == 7)), reads=[wk] + xk, writes=[pk])
                    self.copy('dve', KTn[:, ty, m, :], ps[:, 0:NS], [pk], ["KTn"])
                for b in range(NS):
                    ps, pk = f.pget()
                    for k in range(8):
                        f.op('pe', lambda en, k=k, b=b, ps=ps: en.matmul(ps[0:1, 0:256], lhsT=self.xn[:, k, 512 + b:513 + b], rhs=wv[:, k, 256:512],
                                                                         start=(k == 0), stop=(k == 7)), reads=[wk] + xk, writes=[pk])
                    self.copy('dve', VN[0:1, b, ty, :, 0:64], ps[0:1, 0:256].rearrange("p (g d) -> p g d", g=4), [pk], ["VN"])

            def compress(ci, nblk, rhs_fn, kdst_fn, vdst_fn, sh):
                for g in range(4):
                    hf, sl = g % 2, g // 2
                    P0 = hf * 64
                    ps, pk = f.pget()
                    for t in range(32):
                        f.op('pe', lambda en, t=t, ps=ps, P0=P0, sl=sl: en.matmul(
                            ps[:, 0:nblk], lhsT=W1r[P0:P0 + 64, ci, t, :], rhs=rhs_fn(P0, sl, t), start=(t == 0), stop=(t == 31)),
                            reads=["W1r", ("CT", ci), "CTx"], writes=[pk])
                    f.op('act', lambda en, ps=ps: en.activation(out=sh[:, 0:nblk], in_=ps[:, 0:nblk], func=AF.Silu, bias=hb[:, ci:ci + 1]),
                         reads=[pk, "hb"], writes=["sh"])
                    ps2, pk2 = f.pget()
                    if ci == 0:
                        f.op('pe', lambda en, ps2=ps2: en.matmul(ps2[:, 0:nblk], lhsT=W2k[:, :], rhs=sh[:, 0:nblk], start=True, stop=True),
                             reads=["W2k", "sh"], writes=[pk2])
                        kdst_fn(g, P0, sl, ps2, pk2)
                    else:
                        f.op('pe', lambda en, ps2=ps2: en.matmul(ps2[0:nblk, 0:64], lhsT=sh[:, 0:nblk], rhs=W2v[:, :], start=True, stop=True),
                             reads=["W2v", "sh"], writes=[pk2])
                        vdst_fn(g, ps2, pk2)

            for b in range(NS):
                f.op('pool', lambda en: en.memset(KCTs[:, :, :], 0.0), writes=["KCTs"])
                f.op('pool', lambda en: en.memset(VCMs[:, :, :, :], 0.0), writes=["VCMs"])
                for s_ in range(8):
                    for ci, pool in enumerate([I["cck"], I["ccv"]]):
                        self.gather_KT_V(pool, IDX, b, 16 * s_, 16, CT, ci)
                        for sl in range(2):
                            if s_ < 7:
                                f.op('pool', lambda en, ci=ci, sl=sl, s_=s_: en.tensor_copy(out=CTx[:, ci, sl, s_, 0:16], in_=CT[:, ci, sl, 2032:2048]),
                                     reads=[("CT", ci)], writes=["CTx"])
                            if s_ > 0:
                                f.op('pool', lambda en, ci=ci, sl=sl, s_=s_: en.tensor_copy(out=CTx[:, ci, sl, s_ - 1, 16:32], in_=CT[:, ci, sl, 0:16]),
                                     reads=[("CT", ci)], writes=["CTx"])

                        def kd(g, P0, sl, ps2, pk2, s_=s_):
                            self.copy('dve', KCTs[P0:P0 + 64, sl, s_ * 128:s_ * 128 + 127], ps2[P0:P0 + 64, 0:127], [pk2], ["KCTs"])

                        def vd(g, ps2, pk2, s_=s_):
                            self.copy('dve', VCMs[0:127, s_, g, :], ps2[0:127, 0:64], [pk2], ["VCMs"])
                        compress(ci, 127, lambda P0, sl, t, ci=ci: CT[P0:P0 + 64, ci, sl, t:t + 16 * 126 + 1:16], kd, vd, shb)
                for ci in range(2):
                    def kdx(g, P0, sl, ps2, pk2):
                        self.copy('dve', KCTs[P0:P0 + 64, sl, 127:127 + 128 * 6 + 1:128], ps2[P0:P0 + 64, 0:7], [pk2], ["KCTs"])

                    def vdx(g, ps2, pk2):
                        self.copy('dve', vx[0:7, :], ps2[0:7, 0:64], [pk2], ["vx"])
                        for s_ in range(7):
                            f.dma(VCMs[127:128, s_, g, :], vx[s_:s_ + 1, :], reads=["vx"], writes=["VCMs"])
                    compress(ci, 7, lambda P0, sl, t, ci=ci: CTx[P0:P0 + 64, ci, sl, 0:7, t], kdx, vdx, shx)
                for g in range(4):
                    sl = g // 2
                    for c0 in (0, 512):
                        ps, pk = f.pget()
                        f.op('pe', lambda en, ps=ps, c0=c0: en.matmul(ps[0:4, 0:512], lhsT=QZS[:, b, 4 * g:4 * g + 4], rhs=KCTs[:, sl, c0:c0 + 512],
                                                                      start=True, stop=True), reads=["QZS", "KCTs"], writes=[pk])
                        pb, pbk = f.pget()
                        f.op('pe', lambda en, pb=pb, c0=c0: en.matmul(pb[0:4, 0:512], lhsT=TabAug[0:34, 4 * g:4 * g + 4], rhs=OHs[0:34, c0:c0 + 512],
                                                                      start=True, stop=True), reads=["TabAug", "OHs"], writes=[pbk])
                        self.copy('act', bss[0:4, c0:c0 + 512], pb[0:4, 0:512], [pbk], ["bss"])
                        f.op('dve', lambda en, ps=ps, c0=c0: en.tensor_tensor(out=scs[0:4, c0:c0 + 512], in0=ps[0:4, 0:512], in1=bss[0:4, c0:c0 + 512], op=ALU.add),
                             reads=[pk, "bss"], writes=["scs"])
                    f.op('dve', lambda en: en.tensor_reduce(out=sms[0:4, 0:1], in_=scs[0:4, :], axis=AX.X, op=ALU.max), reads=["scs"], writes=["sms"])
                    f.op('dve', lambda en: en.tensor_scalar(out=sms[0:4, 1:2], in0=sms[0:4, 0:1], scalar1=-10000.0, scalar2=-1.0, op0=ALU.max, op1=ALU.mult),
                         reads=["sms"], writes=["sms"])
                    f.op('dve', lambda en: en.memset(sms[0:4, 2:3], 0.0), reads=["sms"], writes=["sms"])
                    f.op('act', lambda en: en.activation(out=ees[0:4, :], in_=scs[0:4, :], func=AF.Exp, bias=sms[0:4, 1:2], accum_out=sms[0:4, 2:3]),
                         reads=["scs", "sms"], writes=["ees", "sms"])
                    f.op('dve', lambda en: en.tensor_scalar(out=sms[0:4, 3:4], in0=sms[0:4, 2:3], scalar1=1e-30, scalar2=None, op0=ALU.max), reads=["sms"], writes=["sms"])
                    f.op('dve', lambda en: en.reciprocal(out=sms[0:4, 3:4], in_=sms[0:4, 3:4]), reads=["sms"], writes=["sms"])
                    f.op('dve', lambda en: en.tensor_scalar(out=ees[0:4, :], in0=ees[0:4, :], scalar1=sms[0:4, 3:4], scalar2=None, op0=ALU.mult),
                         reads=["ees", "sms"], writes=["ees"])
                    for c0 in (0, 512):
                        ps, pk = f.pget()
                        f.op('pe', lambda en, ps=ps, c0=c0: en.matmul(ps[0:1, 0:512], lhsT=self.ones[0:4, 0:1], rhs=ees[0:4, c0:c0 + 512], start=True, stop=True),
                             reads=["ones", "ees"], writes=[pk])
                        self.copy('dve', PGs[0:1, 1 + c0:1 + c0 + 512], ps[0:1, 0:512], [pk], ["PGs"])
                    S = sco
                    f.op('dve', lambda en: en.tensor_tensor(out=S[0:1, 0:256], in0=PGs[0:1, 1:1022:4], in1=PGs[0:1, 2:1023:4], op=ALU.add), reads=["PGs"], writes=["sco"])
                    f.op('dve', lambda en: en.tensor_tensor(out=S[0:1, 0:256], in0=S[0:1, 0:256], in1=PGs[0:1, 3:1024:4], op=ALU.add), reads=["PGs", "sco"], writes=["sco"])
                    f.op('dve', lambda en: en.scalar_tensor_tensor(out=S[0:1, 0:256], in0=S[0:1, 0:256], scalar=2.0, in1=PGs[0:1, 0:1021:4], op0=ALU.mult, op1=ALU.add),
                         reads=["PGs", "sco"], writes=["sco"])
                    f.op('dve', lambda en: en.tensor_tensor(out=S[0:1, 0:256], in0=S[0:1, 0:256], in1=PGs[0:1, 4:1025:4], op=ALU.add), reads=["PGs", "sco"], writes=["sco"])
                    f.op('dve', lambda en: en.tensor_tensor(out=S[0:1, 0:256], in0=S[0:1, 0:256], in1=FVs[0:1, :], op=ALU.add), reads=["FVs", "sco"], writes=["sco"])
                    f.op('dve', lambda en: en.max(out=S[0:1, 768:776], in_=S[0:1, 0:256]), reads=["sco"], writes=["sco"])
                    f.op('dve', lambda en: en.match_replace(out=S[0:1, 256:512], in_to_replace=S[0:1, 768:776], in_values=S[0:1, 0:256], imm_value=-1e30),
                         reads=["sco"], writes=["sco"])
                    f.op('dve', lambda en: en.max(out=S[0:1, 776:784], in_=S[0:1, 256:512]), reads=["sco"], writes=["sco"])
                    f.op('dve', lambda en: en.tensor_scalar(out=S[0:1, 512:768], in0=S[0:1, 0:256], scalar1=S[0:1, 782:783], scalar2=None, op0=ALU.is_ge),
                         reads=["sco"], writes=["sco"])
                    f.op('dve', lambda en: en.tensor_scalar(out=NSr[0:1, 0:256], in0=S[0:1, 512:768], scalar1=-1.0, scalar2=-NEG, op0=ALU.add, op1=ALU.mult),
                         reads=["sco"], writes=["NSr"])
                    f.op('dve', lambda en, g=g: en.tensor_copy(out=NS2[0:1, g, :], in_=NSr[0:1, 0:256:2]), reads=["NSr"], writes=["NS2"])
                    f.op('dve', lambda en: en.tensor_copy(out=NSr[0:1, 256:384], in_=NSr[0:1, 1:256:2]), reads=["NSr"], writes=["NSr"])
                    f.dma(NS2[32:33, g, :], NSr[0:1, 256:384], reads=["NSr"], writes=["NS2"])
                    ps, pk = f.pget()
                    for tl in range(8):
                        f.op('pe', lambda en, tl=tl, ps=ps: en.transpose(ps[:, tl * 4:(tl + 1) * 4], ees[0:4, tl * 128:(tl + 1) * 128], self.ids[0:4, 0:4]),
                             reads=["ees", "ids"], writes=[pk])
                    self.copy('act', pTs[:, :, :], ps[:, 0:32].rearrange("p (t h) -> p t h", t=8), [pk], ["pTs"])
                    ps, pk = f.pget()
                    for tl in range(8):
                        f.op('pe', lambda en, tl=tl, ps=ps: en.matmul(ps[0:4, 0:64], lhsT=pTs[:, tl, :], rhs=VCMs[:, tl, g, :], start=(tl == 0), stop=(tl == 7)),
                             reads=["pTs", "VCMs"], writes=[pk])
                    self.copy('dve', Ob[0:4, 0, g, :], ps[0:4, 0:64], [pk], ["Ob"])
                for br in range(2):
                    pa, pak = f.pacc(br)
                    first = [True]
                    nseg = 8 if br == 0 else 1
                    for s_ in range(nseg):
                        ntile = 16 if br == 0 else 4
                        if br == 0:
                            self.gather_KT_V(I["csk"], IDX, b, 16 * s_, 16, CT, 0)
                            self.gather_KT_V(I["csv"], IDX, b, 16 * s_, 16, CT, 0, VAs=VAs)
                        else:
                            self.gather_KT_V(None, None, b, 0, 4, CT, 0, plain=I["swk"][b * 512:(b + 1) * 512, :])
                            self.gather_KT_V(None, None, b, 0, 4, CT, 0, VAs=VAs, plain=I["swv"][b * 512:(b + 1) * 512, :])
                        for g in range(4):
                            sl = g // 2
                            ps, pk = f.pget()
                            for rt in range(ntile):
                                last = (s_ == nseg - 1 and rt == ntile - 1)
                                f.op('pe', lambda en, rt=rt, ps=ps: en.matmul(ps[:, rt * 4:(rt + 1) * 4], lhsT=CT[:, 0, sl, rt * 128:(rt + 1) * 128],
                                                                              rhs=QZS[:, b, 4 * g:4 * g + 4], start=True, stop=False),
                                     reads=[("CT", 0), "QZS"], writes=[pk])
                                if br == 0:
                                    f.op('pe', lambda en, rt=rt, ps=ps, s_=s_: en.matmul(
                                        ps[:, rt * 4:(rt + 1) * 4], lhsT=E2[0:33, :],
                                        rhs=NS2[0:33, g, 16 * s_ + rt:16 * s_ + rt + 1].to_broadcast([33, 4]), start=False, stop=False),
                                        reads=["E2", "NS2"], writes=[pk])
                                f.op('pe', lambda en, rt=rt, ps=ps, last=last: en.matmul(ps[:, rt * 4:(rt + 1) * 4], lhsT=onesb[0:1, :], rhs=C31s[0:1, 4 * g:4 * g + 4],
                                                                                       start=False, stop=(not last)), reads=["onesb", "C31s"], writes=[pk])
                                if last:
                                    f.op('pe', lambda en, rt=rt, ps=ps: en.matmul(ps[:, rt * 4:(rt + 1) * 4], lhsT=self.idb[:, :], rhs=B127[:, 4 * g:4 * g + 4],
                                                                                  start=False, stop=True), reads=["idb", "B127"], writes=[pk])
                            f.op('act', lambda en, ps=ps, ntile=ntile: en.activation(out=Ps[:, 0:ntile * 4], in_=ps[:, 0:ntile * 4], func=AF.Exp), reads=[pk], writes=["Ps"])
                            for rt in range(ntile):
                                f.op('pe', lambda en, rt=rt, fs=first[0]: en.matmul(pa[0:4, g * 65:(g + 1) * 65], lhsT=Ps[:, rt * 4:(rt + 1) * 4], rhs=VAs[:, rt, g, :],
                                                                                  start=fs, stop=False, skip_group_check=True), reads=["Ps", "VAs"], writes=[pak])
                                first[0] = False
                    for g in range(4):
                        sl = g // 2
                        ps, pk = f.pget()
                        f.op('pe', lambda en, ps=ps: en.matmul(ps[0:1, 0:4], lhsT=KTn[:, br, sl, b:b + 1], rhs=QZS[:, b, 4 * g:4 * g + 4], start=True, stop=False),
                             reads=["KTn", "QZS"], writes=[pk])
                        f.op('pe', lambda en, ps=ps: en.matmul(ps[0:1, 0:4], lhsT=onesb[0:1, 0:1], rhs=T0s[0:1, 4 * g:4 * g + 4], start=False, stop=True),
                             reads=["onesb", "T0s"], writes=[pk])
                        f.op('act', lambda en, ps=ps: en.activation(out=Pn[0:1, :], in_=ps[0:1, 0:4], func=AF.Exp), reads=[pk], writes=["Pn"])
                        f.op('pe', lambda en: en.matmul(pa[0:4, g * 65:(g + 1) * 65], lhsT=Pn[0:1, 0:4], rhs=VN[0:1, b, br, g, :], start=False, stop=True,
                                                        skip_group_check=True), reads=["Pn", "VN"], writes=[pak])
                    pav = pa[0:4, 0:260].rearrange("p (g d) -> p g d", g=4)
                    f.op('dve', lambda en, pav=pav: en.tensor_scalar(out=sms[0:4, 8:12], in0=pav[:, :, 64], scalar1=1e-30, scalar2=None, op0=ALU.max),
                         reads=[pak], writes=["sms"])
                    f.op('dve', lambda en: en.reciprocal(out=sms[0:4, 8:12], in_=sms[0:4, 8:12]), reads=["sms"], writes=["sms"])
                    f.op('dve', lambda en, pav=pav, br=br: en.tensor_tensor(out=Ob[0:4, 1 + br, :, :], in0=pav[:, :, 0:64],
                                                                            in1=sms[0:4, 8:12][:, :, None].to_broadcast([4, 4, 64]), op=ALU.mult),
                         reads=[pak, "sms"], writes=["Ob"])
                f.op('dve', lambda en: en.tensor_scalar(out=Lg[:, :], in0=Jm[:, :], scalar1=gT[:, b:b + 1], scalar2=None, op0=ALU.mult),
                     reads=["Jm", "gT"], writes=["Lg"])
                ps, pk = f.pget()
                f.op('pe', lambda en, ps=ps: en.matmul(ps[0:4, 0:12], lhsT=Lg[:, :], rhs=Rm[:, :], start=True, stop=True), reads=["Lg", "Rm"], writes=[pk])
                self.copy('dve', G4[:, :], ps[0:4, 0:12], [pk], ["G4"])
                for br in range(3):
                    f.op('dve', lambda en, br=br: en.tensor_tensor(out=tmpo[:, :, :], in0=Ob[0:4, br, :, :],
                                                                   in1=G4[:, 4 * br:4 * br + 4][:, :, None].to_broadcast([4, 4, 64]), op=ALU.mult),
                         reads=["Ob", "G4"], writes=["tmpo"])
                    if br == 0:
                        f.op('dve', lambda en: en.tensor_copy(out=Osum[:, :, 0:64], in_=tmpo[:, :, :]), reads=["tmpo"], writes=["Osum"])
                    else:
                        f.op('dve', lambda en: en.tensor_tensor(out=Osum[:, :, 0:64], in0=Osum[:, :, 0:64], in1=tmpo[:, :, :], op=ALU.add),
                             reads=["tmpo", "Osum"], writes=["Osum"])
                f.op('dve', lambda en: en.tensor_copy(out=Osum[:, :, 64:128], in_=Osum[:, :, 0:64]), reads=["Osum"], writes=["Osum"])
                for g in range(4):
                    ps, pk = f.pget()
                    f.op('pe', lambda en, ps=ps: en.transpose(ps[:, 0:4], Osum[0:4, g, :], self.ids[0:4, 0:4]), reads=["Osum", "ids"], writes=[pk])
                    self.copy('dve', OTs[0:64, 2 * g:2 * g + 2, b], ps[0:64, 0:4:2], [pk], ["OTs"])
                    self.copy('dve', OTs[64:128, 2 * g:2 * g + 2, b], ps[64:128, 1:4:2], [pk], ["OTs"])
            def cons_o(m, outs):
                for (ps, pk, c0, cn) in outs:
                    f.op('dve', lambda en, ps=ps: en.tensor_tensor(out=X[:, m, T:NT], in0=ps[:, 0:NS], in1=X[:, m, T:NT], op=ALU.add),
                         reads=[pk, ("X", m, 4)], writes=[("X", m, 4)])
            self.linear(I["w_out"], 8, D, lambda k: OTs[:, k, :], ["OTs"], NS, cons_o)

    def final(self, raw=False):
        f = self.f
        Yo = self.hT
        for ti, (t0, n) in enumerate(TILES):
            if not raw:
                for (c0, cn) in segs(n):
                    ps, pk = f.pget()
                    for c in range(8):
                        s, sk = self.scr()
                        f.op('act', lambda en, c=c, s=s: en.activation(out=s[:, 0:cn], in_=self.X[:, c, t0 + c0:t0 + c0 + cn], func=AF.Square),
                             reads=xkeys(ti, [c]), writes=[sk])
                        f.op('pe', lambda en, c=c, s=s: en.matmul(ps[:, 0:cn], lhsT=self.ones[:, :], rhs=s[:, 0:cn],
                                                                  start=(c == 0), stop=(c == 7)), reads=[sk, "ones"], writes=[pk])
                    f.op('act', lambda en: en.activation(out=self.rstd[:, c0:c0 + cn], in_=ps[:, 0:cn], func=AF.Sqrt, bias=1e-6, scale=1.0 / D),
                         reads=[pk], writes=["rstd"])
                    f.op('dve', lambda en: en.reciprocal(out=self.rstd[:, c0:c0 + cn], in_=self.rstd[:, c0:c0 + cn]),
                         reads=["rstd"], writes=["rstd"])
                for c in range(8):
                    f.op('dve', lambda en, c=c: en.scalar_tensor_tensor(
                        out=self.X[:, c, t0:t0 + n], in0=self.X[:, c, t0:t0 + n], scalar=self.VT[:, NFIN + c:NFIN + c + 1],
                        in1=self.rstd[:, 0:n], op0=ALU.mult, op1=ALU.mult),
                        reads=xkeys(ti, [c]) + ["rstd", "VT"], writes=xkeys(ti, [c]))
            for r in range(4):
                self.store_T(self.O["y_p"][t0 + r * 128:t0 + (r + 1) * 128, :],
                             lambda c, r=r, t0=t0: self.X[:, c, t0 + r * 128:t0 + (r + 1) * 128], 128, xkeys(ti))
        self.store_T(self.O["y_s"][:, :], lambda c: self.X[:, c, T:NT], NS, xkeys(3))


def build(stage=10, dbg=None, nit=8):
    nc = bass.Bass("TRN2", target_bir_lowering=False)
    es = ExitStack()
    with es:
        k = K(nc, es, stage=stage, dbg=dbg)
        for it in range(nit):
            if it:
                k.f.new_sems()
            k.bind(it)
            k.setup(first=(it == 0))
            with k.scope():
                k.hT = k.f.sb("hT", [128, 32, 516], BF16)
                k.conv_mixer()
            k.nsa()
            k.nsa_samples()
            with k.scope():
                k.hT = k.f.sb("hT1", [128, 32, 516], BF16)
                for ti, (t0, n) in enumerate(TILES):
                    k.ffn_ple(1, ti, t0, n)
            k.final()
            k.f.fence()
        k.f.finish()
        print("instructions:", k.f.n_inst, {e: k.f.cnt[e] for e in k.f.cnt}, "dmas", k.f.dcnt)
    return nc


def host_vecs(inp):
    rows = []
    for name in ["norm_mix", "norm_ffn", "norm_ple"]:
        rows.append(np.asarray(inp[name]).reshape(16, 128))
    rows.append(np.asarray(inp["norm_final"]).reshape(8, 128))
    rows.append(np.asarray(inp["conv_b1"]).reshape(16, 128))
    for name in ["conv_dwb", "conv_ln_g", "conv_ln_b", "conv_b2"]:
        rows.append(np.asarray(inp[name]).reshape(8, 128))
    rows.append(np.asarray(inp["conv_dw"]).reshape(31 * 8, 128))
    rows.append(np.asarray(inp["cmpk_pe"]).reshape(16, 128))
    rows.append(np.asarray(inp["cmpv_pe"]).reshape(16, 128))
    v = np.ascontiguousarray(np.concatenate(rows, axis=0).astype(np.float32))
    assert v.shape == (NVEC, 128)
    return v


def rel_bucket_np(dist):
    n = np.maximum(dist, 0)
    nf = np.maximum(n, 1).astype(np.float32)
    large = 16 + (np.log(nf / np.float32(16)) / np.float32(math.log(8.0)) * np.float32(16)).astype(np.int32)
    large = np.minimum(large, 31)
    return np.where(n < 16, n, large)


def onehot_table(dist, sub31):
    d = dist.reshape(-1)
    oh = np.zeros((34, d.size), np.float32)
    ok = d >= 0
    b = rel_bucket_np(d)
    idx = np.nonzero(ok)[0]
    oh[b[idx], idx] = 1.0
    oh[32, ~ok] = 1.0
    if sub31:
        oh[33, ok] = -1.0
    return oh


_CONST = {}


def host_consts():
    if _CONST:
        return _CONST
    key = np.arange(128)[:, None]
    q = np.arange(128)[None, :]
    _CONST["ohd"] = onehot_table(q - key, True)
    _CONST["oho"] = onehot_table(128 + q - key, True)
    ql = np.arange(128)[:, None]
    m = np.arange(254)[None, :]
    _CONST["ohg"] = onehot_table(ql - 16 * (m - 127) - 31, False)
    _CONST["tailm"] = np.where(q <= key, 0.0, NEG).astype(np.float32)
    qpos = (128 * np.arange(16)[None, :, None] + np.arange(128)[:, None, None])
    j = np.arange(32)[None, None, :]
    cur = qpos // 64
    valid = j * 64 <= qpos
    forced = (j == 0) | (j == cur) | (j == cur - 1)
    fv = np.where(forced, 1000.0, np.where(valid, 0.0, -1000.0)).astype(np.float32)
    _CONST["fvtab"] = np.ascontiguousarray(fv.reshape(128, 512))
    b = np.arange(32)[:, None, None]
    kt = np.arange(16)[None, :, None]
    k = np.arange(128)[None, None, :]
    _CONST["exm"] = np.ascontiguousarray((b == 2 * kt + k // 64).astype(np.float32).reshape(32, 2048))
    _CONST["ident"] = np.eye(128, dtype=np.float32)
    c = np.arange(1024)
    dist = 16384 - 16 * c - 31
    dist[1023] = -1
    _CONST["ohs"] = onehot_table(dist, False)
    fvs = np.zeros((1, 256), np.float32)
    fvs[0, 0] = 1000.0
    fvs[0, 255] = 1000.0
    _CONST["fvs"] = fvs
    e2 = np.zeros((33, 128), np.float32)
    e2[0, 0:64] = 1.0
    e2[32, 64:128] = 1.0
    _CONST["e2"] = e2
    row = np.arange(48)
    _CONST["jm"] = (row[:, None] % 4 == np.arange(4)[None, :]).astype(np.float32)
    _CONST["rm"] = (row[:, None] // 4 == np.arange(12)[None, :]).astype(np.float32)
    _CONST["oh127"] = onehot_table(128 - np.arange(128), True)
    return _CONST


def full_inputs(inp):
    g = lambda n: np.asarray(inp[n])
    c = host_consts()
    m = {
        "xp": np.ascontiguousarray(g("x_prompt").reshape(8 * T, D)),
        "xs": np.ascontiguousarray(g("x_sample").reshape(32, D)),
        "pp": np.ascontiguousarray(g("p_prompt").reshape(2, 8 * T, 256)),
        "psm": np.ascontiguousarray(g("p_sample").reshape(2, 32, 256)),
        "sconv": np.ascontiguousarray(g("state_conv").reshape(32 * 30, D)),
        "vecs": host_vecs(inp),
        "conv_w1": g("conv_w1")[0], "conv_w2": g("conv_w2")[0],
        "mlp_up": g("mlp_up"), "mlp_down": g("mlp_down"),
        "ple_proj": g("ple_proj"), "ple_gate": g("ple_gate"),
        "w_in": g("attn_w_in")[0], "w_out": g("attn_w_out")[0],
        "ck_w1": g("cmpk_w1")[0], "ck_w2": g("cmpk_w2")[0], "cv_w1": g("cmpv_w1")[0], "cv_w2": g("cmpv_w2")[0],
        "rel_table": g("rel_table"),
        "swk": np.ascontiguousarray(g("state_win_k").reshape(32 * 512, 256)),
        "swv": np.ascontiguousarray(g("state_win_v").reshape(32 * 512, 256)),
        "cck": g("cache_cmp_k").reshape(655360, 256), "ccv": g("cache_cmp_v").reshape(655360, 256),
        "csk": g("cache_slc_k").reshape(655360, 256), "csv": g("cache_slc_v").reshape(655360, 256),
        "ptab": np.ascontiguousarray(g("page_table").reshape(1, 32 * 128).astype(np.int32)),
    }
    for k in ["ident", "ohd", "oho", "ohg", "tailm", "fvtab", "exm", "ohs", "fvs", "e2", "jm", "rm", "oh127"]:
        m[k] = c[k]
    return {k: np.ascontiguousarray(v) for k, v in m.items()}


_NC_CACHE = {}


def kernel(**inp):
    m = full_inputs(inp)
    if "nc" not in _NC_CACHE:
        _NC_CACHE["nc"] = build(stage=10, nit=8)
    nc = _NC_CACHE["nc"]
    res = run_bass_kernel_spmd(nc, [m], core_ids=[0])
    R = res.results[0]
    g = lambda nm: np.asarray(R[nm]).astype(np.float32)
    outs = [g("y_p").reshape(8, T, D), g("y_s").reshape(32, 1, D)]
    for nm in ["cmp_k_p", "cmp_v_p", "slc_k_p", "slc_v_p"]:
        outs.append(g(nm).reshape(1, 8, T, 4, 64))
    for nm in ["win_k_p", "win_v_p"]:
        outs.append(g(nm).reshape(1, 8, 512, 4, 64))
    outs.append(g("conv_p").reshape(1, 8, 30, D))
    for nm in ["cmp_k_s", "cmp_v_s", "slc_k_s", "slc_v_s"]:
        outs.append(g(nm).reshape(1, 32, 1, 4, 64))
    for nm in ["win_k_s", "win_v_s"]:
        outs.append(g(nm).reshape(1, 32, 512, 4, 64))
    outs.append(g("conv_s").reshape(1, 32, 30, D))
    return tuple(outs)
```

```python
import math
import numpy as np
from contextlib import ExitStack
import concourse.bass as bass
import concourse.mybir as mybir
from concourse.bass_utils import run_bass_kernel_spmd

F32 = mybir.dt.float32
BF16 = mybir.dt.bfloat16
I32 = mybir.dt.int32
ALU = mybir.AluOpType
AF = mybir.ActivationFunctionType
AX = mybir.AxisListType

NCORES = 8
T = 2048
NS = 4
NT = T + NS
D = 1024
NEG = -30000.0

NM0, NM1, NF0, NF1, NP0, NP1, NFIN, B1, DWB, LNG, LNB, B2, DW, PEK, PEV = \
    0, 8, 16, 24, 32, 40, 48, 56, 72, 80, 88, 96, 104, 352, 368
NVEC = 384


class FW:
    NDMA = 40

    def __init__(self, nc, es):
        self.nc = nc
        self.es = es
        self.eng = {'pe': nc.tensor, 'act': nc.scalar, 'dve': nc.vector, 'pool': nc.gpsimd, 'sp': nc.sync}
        self.n_inst = 0
        self.psl = []
        self.psi = 0
        self.cur_es = None
        self.sfx = ""
        self.nset = 0
        self.cnt = {e: 0 for e in self.eng}
        self.dsem = [es.enter_context(nc.semaphore("d%d" % j)) for j in range(self.NDMA)]
        self.dcnt = 0
        self.dslot_tok = [None] * self.NDMA
        self.waited = {e: {} for e in self.eng}
        self.new_sems()

    def new_sems(self):
        if self.nset:
            self.fence()
        i = self.nset
        self.nset += 1
        self.sem = {e: self.es.enter_context(self.nc.semaphore("s%d_%s" % (i, e))) for e in self.eng}
        self.cnt = {e: 0 for e in self.eng}
        self.waited = {e: {k: v for k, v in self.waited[e].items() if isinstance(k, tuple)} for e in self.eng}
        self.lastw = {}
        self.readers = {}

    def sb(self, name, shape, dt=F32, es=None):
        return (es or self.cur_es or self.es).enter_context(self.nc.sbuf_tensor(name + self.sfx, list(shape), dt))

    def fence(self):
        toks = [(e, self.cnt[e]) for e in self.eng if self.cnt[e]] + [t for t in self.dslot_tok if t is not None]
        for e in self.eng:
            for t in toks:
                if t[0] != e:
                    self._wait(e, t)

    def mkpsum(self):
        for i in range(8):
            self.psl.append((self.es.enter_context(self.nc.psum_tensor("ps%d" % i, [128, 512], F32)), "ps%d" % i))

    def pget(self):
        r = self.psl[self.psi % 6]
        self.psi += 1
        return r

    def pacc(self, i):
        return self.psl[6 + i]

    def _semof(self, tok):
        return self.sem[tok[0]] if tok[0] != 'd' else self.dsem[tok[1]]

    def _wait(self, e, tok):
        key = tok[0] if tok[0] != 'd' else ('d', tok[1])
        val = tok[-1]
        if self.waited[e].get(key, 0) >= val:
            return
        self.waited[e][key] = val
        self.eng[e].wait_ge(self._semof(tok), val)

    def _deps(self, e, reads, writes):
        toks = []
        for k in list(reads) + list(writes):
            t = self.lastw.get(k)
            if t is not None:
                toks.append(t)
        for k in writes:
            toks.extend(self.readers.get(k, ()))
        for t in toks:
            if e == 'pe' and t[0] == 'pe':
                continue
            self._wait(e, t)

    def _record(self, tok, reads, writes):
        for k in reads:
            self.readers.setdefault(k, []).append(tok)
        for k in writes:
            self.lastw[k] = tok
            self.readers[k] = []

    def op(self, e, fn, reads=(), writes=(), sig=True):
        px = [k for k in reads if isinstance(k, str) and k.startswith("ps")]
        if px:
            writes = list(writes) + px
        self._deps(e, reads, writes)
        ins = fn(self.eng[e])
        if e == 'pe' and not sig:
            self._record((e, self.cnt[e] + 1), reads, writes)
            self.n_inst += 1
            return ins
        self.cnt[e] += 1
        ins.then_inc(self.sem[e], 1)
        self._record((e, self.cnt[e]), reads, writes)
        self.n_inst += 1
        return ins

    def dma(self, out, in_, reads=(), writes=(), q='sp', fn=None, **kw):
        s = self.dcnt % self.NDMA
        v = 16 * (self.dcnt // self.NDMA + 1)
        self.dcnt += 1
        prev = self.dslot_tok[s]
        if prev is not None:
            self._wait(q, prev)
        self._deps(q, reads, writes)
        if fn is not None:
            ins = fn(self.eng[q])
        else:
            ins = self.eng[q].dma_start(out=out, in_=in_, **kw)
        ins.then_inc(self.dsem[s], 16)
        tok = ('d', s, v)
        self.dslot_tok[s] = tok
        self._record(tok, reads, writes)
        self.n_inst += 1
        return tok

    def finish(self):
        for s in range(self.NDMA):
            if self.dslot_tok[s] is not None:
                self._wait('sp', self.dslot_tok[s])
        for e in self.eng:
            if self.cnt[e]:
                self._wait('sp', (e, self.cnt[e]))


def segs(n):
    return [(0, n)] if n <= 512 else [(0, 512), (512, n - 512)]


TILES = [(0, 512), (512, 512), (1024, 512), (1536, 516)]


def xkeys(ti, cs=range(8)):
    ks = [("X", c, ti) for c in cs]
    if ti == 3:
        ks += [("X", c, 4) for c in cs]
    return ks


class K:
    def __init__(self, nc, es, dbg=None, stage=9):
        self.nc = nc
        self.f = FW(nc, es)
        self.dbg = dbg
        self.stage = stage
        self.rr = 0
        f = self.f
        dt = lambda name, shape, d=F32, kind="ExternalInput": nc.dram_tensor(name, list(shape), d, kind=kind).ap()
        NSQ, NSM = 8, 32
        FI = {}
        for name, shape in [("xp", [NSQ * T, D]), ("xs", [NSM, D]), ("pp", [2, NSQ * T, 256]), ("psm", [2, NSM, 256]),
                            ("sconv", [NSM * 30, D]), ("vecs", [NVEC, 128]), ("ident", [128, 128]),
                            ("conv_w1", [D, 2 * D]), ("conv_w2", [D, D]),
                            ("mlp_up", [2, D, 4 * D]), ("mlp_down", [2, 4 * D, D]),
                            ("ple_proj", [2, 256, D]), ("ple_gate", [2, D, D]),
                            ("w_in", [D, 2608]), ("w_out", [D, D]), ("ck_w1", [2048, 128]), ("ck_w2", [128, 64]),
                            ("cv_w1", [2048, 128]), ("cv_w2", [128, 64]), ("rel_table", [32, 16]),
                            ("ohd", [34, 16384]), ("oho", [34, 16384]), ("ohg", [34, 32512]), ("tailm", [128, 128]),
                            ("fvtab", [128, 512]), ("exm", [32, 2048]),
                            ("swk", [NSM * 512, 256]), ("swv", [NSM * 512, 256]),
                            ("cck", [655360, 256]), ("ccv", [655360, 256]), ("csk", [655360, 256]), ("csv", [655360, 256]),
                            ("ohs", [34, 1024]), ("fvs", [1, 256]), ("e2", [33, 128]), ("jm", [48, 4]), ("rm", [48, 12]),
                            ("oh127", [34, 128])]:
            FI[name] = dt(name, shape)
        FI["ptab"] = dt("ptab", [1, NSM * 128], I32)
        FO = {}
        FO["y_p"] = dt("y_p", [NSQ * T, D], kind="ExternalOutput")
        FO["y_s"] = dt("y_s", [NSM, D], kind="ExternalOutput")
        FO["conv_p"] = dt("conv_p", [NSQ * 30, D], kind="ExternalOutput")
        FO["conv_s"] = dt("conv_s", [NSM * 30, D], kind="ExternalOutput")
        for nm in ["cmp_k_p", "cmp_v_p", "slc_k_p", "slc_v_p"]:
            FO[nm] = dt(nm, [NSQ * T, 256], kind="ExternalOutput")
        for nm in ["win_k_p", "win_v_p"]:
            FO[nm] = dt(nm, [NSQ * 512, 256], kind="ExternalOutput")
        for nm in ["cmp_k_s", "cmp_v_s", "slc_k_s", "slc_v_s"]:
            FO[nm] = dt(nm, [NSM, 256], kind="ExternalOutput")
        for nm in ["win_k_s", "win_v_s"]:
            FO[nm] = dt(nm, [NSM * 512, 256], kind="ExternalOutput")
        self.bsc = dt("bsc", [16, 32768], kind="Internal")
        self.FI, self.FO = FI, FO
        self.bind(0)
        self.X = f.sb("X", [128, 8, NT])
        self.xn = f.sb("xn", [128, 8, 516], BF16)
        self.hT = None
        self.VT = f.sb("VT", [128, NVEC])
        self.ids = f.sb("ids", [128, 128])
        self.idb = f.sb("idb", [128, 128], BF16)
        self.ones = f.sb("ones", [128, 128])
        self.wbuf = [f.sb("wb%d" % i, [128, 4096], BF16) for i in range(3)]
        self.wi = 0
        self.stg = [f.sb("stg%d" % i, [128, 1024]) for i in range(2)]
        self.si = 0
        self.sc = [f.sb("sc%d" % i, [128, 516]) for i in range(3)]
        self.sci = 0
        self.rstd = f.sb("rstd", [128, 516])
        f.mkpsum()

    def bind(self, it):
        FI, FO = self.FI, self.FO
        I = dict(FI)
        I["xp"] = FI["xp"][it * T:(it + 1) * T, :]
        I["xs"] = FI["xs"][it * NS:(it + 1) * NS, :]
        I["pp"] = FI["pp"][:, it * T:(it + 1) * T, :]
        I["psm"] = FI["psm"][:, it * NS:(it + 1) * NS, :]
        I["sconv"] = FI["sconv"][it * NS * 30:(it + 1) * NS * 30, :]
        I["swk"] = FI["swk"][it * NS * 512:(it + 1) * NS * 512, :]
        I["swv"] = FI["swv"][it * NS * 512:(it + 1) * NS * 512, :]
        I["ptab"] = FI["ptab"][:, it * NS * 128:(it + 1) * NS * 128]
        O = {}
        O["y_p"] = FO["y_p"][it * T:(it + 1) * T, :]
        O["y_s"] = FO["y_s"][it * NS:(it + 1) * NS, :]
        O["conv_p"] = FO["conv_p"][it * 30:(it + 1) * 30, :]
        O["conv_s"] = FO["conv_s"][it * NS * 30:(it + 1) * NS * 30, :]
        for nm in ["cmp_k_p", "cmp_v_p", "slc_k_p", "slc_v_p"]:
            O[nm] = FO[nm][it * T:(it + 1) * T, :]
        for nm in ["win_k_p", "win_v_p"]:
            O[nm] = FO[nm][it * 512:(it + 1) * 512, :]
        for nm in ["cmp_k_s", "cmp_v_s", "slc_k_s", "slc_v_s"]:
            O[nm] = FO[nm][it * NS:(it + 1) * NS, :]
        for nm in ["win_k_s", "win_v_s"]:
            O[nm] = FO[nm][it * NS * 512:(it + 1) * NS * 512, :]
        self.I, self.O = I, O
        self.f.sfx = "_i%d" % it

    def scope(self):
        k = self

        class _S:
            def __enter__(s2):
                s2.prev = k.f.cur_es
                s2.es = ExitStack()
                s2.es.__enter__()
                k.f.cur_es = s2.es
                return s2

            def __exit__(s2, *a):
                k.f.fence()
                k.f.cur_es = s2.prev
                s2.es.__exit__(None, None, None)
                return False
        return _S()

    def scr(self):
        i = self.sci % 3
        self.sci += 1
        return self.sc[i], "sc%d" % i

    def ev_engine(self):
        self.rr += 1
        return 'act' if self.rr % 2 else 'dve'

    def copy(self, e, out, in_, reads, writes):
        if e == 'act':
            self.f.op('act', lambda en: en.activation(out=out, in_=in_, func=AF.Identity), reads, writes)
        else:
            self.f.op(e, lambda en: en.tensor_copy(out=out, in_=in_), reads, writes)

    def load_rows_T(self, src, nrows, ncols, sink):
        f = self.f
        i = self.si % 2
        self.si += 1
        st, sk = self.stg[i], "stg%d" % i
        f.dma(st[0:nrows, 0:ncols], src, writes=[sk])
        nch = ncols // 128
        per = max(1, 512 // max(nrows, 1))
        per = min(per, 4)
        c = 0
        while c < nch:
            g = min(per, nch - c)
            ps, pk = f.pget()
            for j in range(g):
                f.op('pe', lambda en, j=j, c=c: en.transpose(ps[:, j * nrows:(j + 1) * nrows],
                                                              st[0:nrows, (c + j) * 128:(c + j + 1) * 128],
                                                              self.ids[0:nrows, 0:nrows]),
                     reads=[sk, "ids"], writes=[pk])
            sink(c, g, ps, pk)
            c += g

    def store_T(self, dst, src_fn, ntok, src_keys, ncols=1024, q='sp'):
        f = self.f
        i = self.si % 2
        self.si += 1
        st, sk = self.stg[i], "stg%d" % i
        nch = ncols // 128
        for c0 in range(0, nch, 4):
            ps, pk = f.pget()
            g = min(4, nch - c0)
            for j in range(g):
                f.op('pe', lambda en, j=j, c0=c0: en.transpose(ps[0:ntok, j * 128:(j + 1) * 128], src_fn(c0 + j), self.ids[:, :]),
                     reads=list(src_keys) + ["ids"], writes=[pk])
            self.copy(self.ev_engine(), st[0:ntok, c0 * 128:(c0 + g) * 128], ps[0:ntok, 0:g * 128], [pk], [sk])
        f.dma(dst, st[0:ntok, 0:ncols], reads=[sk], q=q)

    def wload(self, src, kc, width):
        i = self.wi % 3
        self.wi += 1
        wb, wk = self.wbuf[i], "wb%d" % i
        view = wb[:, 0:kc * width].rearrange("p (k m) -> p k m", k=kc)
        self.f.dma(view, src.rearrange("(k p) m -> p k m", p=128), writes=[wk], q='pool')
        return view, wk

    def linear(self, wsrc, kc, nout, xin, xkeys_, n, consume, bw=None):
        f = self.f
        if bw is None:
            bw = 512 if kc <= 8 else 128
        bw = min(bw, nout)
        blocks = list(range(0, nout, bw))
        nxt = self.wload(wsrc[:, blocks[0]:blocks[0] + bw], kc, bw)
        for bi, b0 in enumerate(blocks):
            wv, wk = nxt
            if bi + 1 < len(blocks):
                nb = blocks[bi + 1]
                nxt = self.wload(wsrc[:, nb:nb + bw], kc, bw)
            for mm in range(bw // 128):
                outs = []
                for (c0, cn) in segs(n):
                    ps, pk = f.pget()
                    for k in range(kc):
                        f.op('pe', lambda en, k=k, mm=mm, c0=c0, cn=cn, ps=ps, wv=wv: en.matmul(
                            ps[:, 0:cn], lhsT=wv[:, k, mm * 128:(mm + 1) * 128], rhs=xin(k)[:, c0:c0 + cn],
                            start=(k == 0), stop=(k == kc - 1)), reads=[wk] + list(xkeys_), writes=[pk], sig=(k == kc - 1))
                    outs.append((ps, pk, c0, cn))
                consume(b0 // 128 + mm, outs)

    def rmsnorm(self, ti, t0, n, gcol):
        f = self.f
        X = self.X
        for (c0, cn) in segs(n):
            ps, pk = f.pget()
            for c in range(8):
                s, sk = self.scr()
                f.op('act', lambda en, c=c, s=s: en.activation(out=s[:, 0:cn], in_=X[:, c, t0 + c0:t0 + c0 + cn], func=AF.Square),
                     reads=xkeys(ti, [c]), writes=[sk])
                f.op('pe', lambda en, c=c, s=s: en.matmul(ps[:, 0:cn], lhsT=self.ones[:, :], rhs=s[:, 0:cn],
                                                           start=(c == 0), stop=(c == 7)), reads=[sk, "ones"], writes=[pk])
            f.op('act', lambda en: en.activation(out=self.rstd[:, c0:c0 + cn], in_=ps[:, 0:cn], func=AF.Sqrt,
                                                 bias=1e-6, scale=1.0 / D), reads=[pk], writes=["rstd"])
            f.op('dve', lambda en: en.reciprocal(out=self.rstd[:, c0:c0 + cn], in_=self.rstd[:, c0:c0 + cn]),
                 reads=["rstd"], writes=["rstd"])
        for c in range(8):
            f.op('dve', lambda en, c=c: en.scalar_tensor_tensor(
                out=self.xn[:, c, 0:n], in0=X[:, c, t0:t0 + n], scalar=self.VT[:, gcol + c:gcol + c + 1],
                in1=self.rstd[:, 0:n], op0=ALU.mult, op1=ALU.mult),
                reads=xkeys(ti, [c]) + ["rstd", "VT"], writes=[("xn", c)])

    def setup(self, first=True):
        f = self.f
        I = self.I
        if first:
            f.dma(self.ids[:], I["ident"], writes=["ids"])
            f.op('dve', lambda en: en.tensor_copy(out=self.idb[:], in_=self.ids[:]), reads=["ids"], writes=["idb"])
            f.op('pool', lambda en: en.memset(self.ones[:], 1.0), writes=["ones"])
            for r in range(NVEC // 128):
                def sink(c, g, ps, pk, r=r):
                    self.copy('dve', self.VT[:, r * 128:(r + 1) * 128], ps[:, 0:128], [pk], ["VT"])
                self.load_rows_T(I["vecs"][r * 128:(r + 1) * 128, :], 128, 128, sink)
        for rt in range(16):
            ti = rt // 4

            def sink(c, g, ps, pk, rt=rt, ti=ti):
                self.copy(self.ev_engine(), self.X[:, c:c + g, rt * 128:(rt + 1) * 128],
                          ps[:, 0:g * 128].rearrange("p (c t) -> p c t", c=g), [pk], [("X", cc, ti) for cc in range(c, c + g)])
            self.load_rows_T(I["xp"][rt * 128:(rt + 1) * 128, :], 128, D, sink)

        def sink_s(c, g, ps, pk):
            self.copy('dve', self.X[:, c:c + g, T:NT], ps[:, 0:g * NS].rearrange("p (c t) -> p c t", c=g),
                      [pk], [("X", cc, 4) for cc in range(c, c + g)])
        self.load_rows_T(I["xs"][:, :], NS, D, sink_s)

    def conv_mixer(self):
        f = self.f
        I = self.I
        X = self.X
        VT = self.VT
        Ub = f.sb("Ub", [128, 8, 30 + 512], BF16)
        Ulast = f.sb("Ulast", [128, 8, 30])
        CS = f.sb("CS", [128, 8, NS, 31])
        Y = f.sb("Y", [128, 8, 516])
        Dg = [f.sb("Dg%d" % i, [128, 31, 128], BF16) for i in range(2)]
        tmpc = f.sb("tmpc", [128, 8, NS, 31])
        mu = f.sb("mu", [128, 516])
        f.op('pool', lambda en: en.memset(Ub[:, :, 0:30], 0.0), writes=[("Ub", c) for c in range(8)])
        def sink_cs(c, g, ps, pk):
            self.copy('dve', CS[:, c:c + g, :, 0:30], ps[:, 0:g * 120].rearrange("p (c b k) -> p c b k", c=g, b=NS), [pk], ["CS"])
        self.load_rows_T(I["sconv"][:, :], NS * 30, D, sink_cs)
        for b in range(NS):
            f.dma(self.O["conv_s"][b * 30:b * 30 + 29, :], I["sconv"][b * 30 + 1:b * 30 + 30, :])

        for ti, (t0, n) in enumerate(TILES):
            self.rmsnorm(ti, t0, n, NM0)
            for mb in range(2):
                wa, wak = self.wload(I["conv_w1"][:, mb * 512:(mb + 1) * 512], 8, 512)
                wg, wgk = self.wload(I["conv_w1"][:, D + mb * 512:D + (mb + 1) * 512], 8, 512)
                for j in range(4):
                    m = mb * 4 + j
                    for (c0, cn) in segs(n):
                        pa, pak = f.pget()
                        pg, pgk = f.pget()
                        for k in range(8):
                            f.op('pe', lambda en, k=k, j=j, pa=pa: en.matmul(pa[:, 0:cn], lhsT=wa[:, k, j * 128:(j + 1) * 128],
                                                                              rhs=self.xn[:, k, c0:c0 + cn], start=(k == 0), stop=(k == 7)),
                                 reads=[wak] + [("xn", kk) for kk in range(8)], writes=[pak])
                        for k in range(8):
                            f.op('pe', lambda en, k=k, j=j, pg=pg: en.matmul(pg[:, 0:cn], lhsT=wg[:, k, j * 128:(j + 1) * 128],
                                                                              rhs=self.xn[:, k, c0:c0 + cn], start=(k == 0), stop=(k == 7)),
                                 reads=[wgk] + [("xn", kk) for kk in range(8)], writes=[pgk])
                        s, sk = self.scr()
                        f.op('act', lambda en, s=s, pg=pg, m=m: en.activation(out=s[:, 0:cn], in_=pg[:, 0:cn], func=AF.Sigmoid,
                                                                              bias=VT[:, B1 + 8 + m:B1 + 9 + m]),
                             reads=[pgk, "VT"], writes=[sk])
                        u, uk = self.scr()
                        f.op('dve', lambda en, s=s, u=u, pa=pa, m=m: en.scalar_tensor_tensor(
                            out=u[:, 0:cn], in0=pa[:, 0:cn], scalar=VT[:, B1 + m:B1 + m + 1], in1=s[:, 0:cn],
                            op0=ALU.add, op1=ALU.mult), reads=[pak, sk, "VT"], writes=[uk])
                        if c0 == 0:
                            f.op('act', lambda en, u=u, m=m: en.activation(out=Ub[:, m, 30:30 + 512], in_=u[:, 0:512], func=AF.Identity),
                                 reads=[uk], writes=[("Ub", m)])
                            if ti == 3:
                                f.op('pool', lambda en, u=u, m=m: en.tensor_copy(out=Ulast[:, m, :], in_=u[:, 482:512]),
                                     reads=[uk], writes=["Ulast"])
                        else:
                            f.op('pool', lambda en, u=u, m=m: en.tensor_copy(out=CS[:, m, :, 30], in_=u[:, 0:NS]),
                                 reads=[uk], writes=["CS"])
            for c in range(8):
                dg, dgk = Dg[c % 2], "Dg%d" % (c % 2)
                f.op('dve', lambda en, c=c, dg=dg: en.tensor_tensor(
                    out=dg[:, :, :], in0=self.ids[:, None, :].to_broadcast([128, 31, 128]),
                    in1=VT[:, DW + c:DW + c + 248:8][:, :, None].to_broadcast([128, 31, 128]), op=ALU.mult),
                    reads=["ids", "VT"], writes=[dgk])
                ps, pk = f.pget()
                for k in range(31):
                    f.op('pe', lambda en, k=k, c=c, dg=dg, ps=ps: en.matmul(ps[:, 0:512], lhsT=dg[:, k, :], rhs=Ub[:, c, k:k + 512],
                                                                           start=(k == 0), stop=(k == 30)),
                         reads=[dgk, ("Ub", c)], writes=[pk], sig=(k == 30))
                f.op('act', lambda en, c=c, ps=ps: en.activation(out=Y[:, c, 0:512], in_=ps[:, 0:512], func=AF.Identity,
                                                                 bias=VT[:, DWB + c:DWB + c + 1]), reads=[pk, "VT"], writes=[("Y", c)])
                f.op('pool', lambda en, c=c: en.tensor_copy(out=Ub[:, c, 0:30], in_=Ub[:, c, 512:542]),
                     reads=[("Ub", c)], writes=[("Ub", c)])
            if ti == 3:
                dwv = VT[:, DW:DW + 248].rearrange("p (k c) -> p c k", c=8)
                f.op('dve', lambda en: en.tensor_tensor(out=tmpc[:, :, :, :], in0=CS[:, :, :, :],
                                                        in1=dwv[:, :, None, :].to_broadcast([128, 8, NS, 31]), op=ALU.mult),
                     reads=["CS", "VT"], writes=["tmpc"])
                f.op('dve', lambda en: en.tensor_reduce(out=Y[:, :, 512:516], in_=tmpc[:, :, :, :], axis=AX.X, op=ALU.add),
                     reads=["tmpc"], writes=[("Y", c) for c in range(8)])
                f.op('dve', lambda en: en.tensor_tensor(out=Y[:, :, 512:516], in0=Y[:, :, 512:516],
                                                        in1=VT[:, DWB:DWB + 8][:, :, None].to_broadcast([128, 8, NS]), op=ALU.add),
                     reads=[("Y", c) for c in range(8)] + ["VT"], writes=[("Y", c) for c in range(8)])
            for (c0, cn) in segs(n):
                p1, p1k = f.pget()
                p2, p2k = f.pget()
                for c in range(8):
                    f.op('pe', lambda en, c=c: en.matmul(p1[:, 0:cn], lhsT=self.ones[:, :], rhs=Y[:, c, c0:c0 + cn],
                                                         start=(c == 0), stop=(c == 7)), reads=[("Y", c), "ones"], writes=[p1k])
                for c in range(8):
                    s, sk = self.scr()
                    f.op('act', lambda en, c=c, s=s: en.activation(out=s[:, 0:cn], in_=Y[:, c, c0:c0 + cn], func=AF.Square),
                         reads=[("Y", c)], writes=[sk])
                    f.op('pe', lambda en, c=c, s=s: en.matmul(p2[:, 0:cn], lhsT=self.ones[:, :], rhs=s[:, 0:cn],
                                                              start=(c == 0), stop=(c == 7)), reads=[sk, "ones"], writes=[p2k])
                f.op('act', lambda en: en.activation(out=mu[:, c0:c0 + cn], in_=p1[:, 0:cn], func=AF.Identity, scale=1.0 / D),
                     reads=[p1k], writes=["mu"])
                s, sk = self.scr()
                f.op('dve', lambda en, s=s: en.tensor_tensor(out=s[:, 0:cn], in0=mu[:, c0:c0 + cn], in1=mu[:, c0:c0 + cn], op=ALU.mult),
                     reads=["mu"], writes=[sk])
                f.op('dve', lambda en, s=s: en.scalar_tensor_tensor(out=s[:, 0:cn], in0=p2[:, 0:cn], scalar=1.0 / D, in1=s[:, 0:cn],
                                                                    op0=ALU.mult, op1=ALU.subtract), reads=[p2k, sk], writes=[sk])
                f.op('act', lambda en, s=s: en.activation(out=self.rstd[:, c0:c0 + cn], in_=s[:, 0:cn], func=AF.Sqrt, bias=1e-6, scale=1.0),
                     reads=[sk], writes=["rstd"])
                f.op('dve', lambda en: en.reciprocal(out=self.rstd[:, c0:c0 + cn], in_=self.rstd[:, c0:c0 + cn]),
                     reads=["rstd"], writes=["rstd"])
            for c in range(8):
                s, sk = self.scr()
                f.op('dve', lambda en, c=c, s=s: en.tensor_tensor(out=s[:, 0:n], in0=Y[:, c, 0:n], in1=mu[:, 0:n], op=ALU.subtract),
                     reads=[("Y", c), "mu"], writes=[sk])
                f.op('dve', lambda en, c=c, s=s: en.tensor_tensor(out=s[:, 0:n], in0=s[:, 0:n], in1=self.rstd[:, 0:n], op=ALU.mult),
                     reads=[sk, "rstd"], writes=[sk])
                f.op('act', lambda en, c=c, s=s: en.activation(out=self.xn[:, c, 0:n], in_=s[:, 0:n], func=AF.Silu,
                                                               bias=VT[:, LNB + c:LNB + c + 1], scale=VT[:, LNG + c:LNG + c + 1]),
                     reads=[sk, "VT"], writes=[("xn", c)])
            def cons(m, outs, ti=ti, t0=t0):
                for (ps, pk, c0, cn) in outs:
                    f.op('dve', lambda en, ps=ps: en.scalar_tensor_tensor(
                        out=X[:, m, t0 + c0:t0 + c0 + cn], in0=ps[:, 0:cn], scalar=VT[:, B2 + m:B2 + m + 1],
                        in1=X[:, m, t0 + c0:t0 + c0 + cn], op0=ALU.add, op1=ALU.add),
                        reads=[pk, "VT"] + xkeys(ti, [m]), writes=xkeys(ti, [m]))
            self.linear(I["conv_w2"], 8, D, lambda k: self.xn[:, k, :], [("xn", kk) for kk in range(8)], n, cons)
            self.ffn_ple(0, ti, t0, n)
        self.store_T(self.O["conv_p"][:, :], lambda c: Ulast[:, c, :], 30, ["Ulast"])
        Us = f.sb("Us", [128, 8, NS])
        f.op('dve', lambda en: en.tensor_copy(out=Us[:, :, :], in_=CS[:, :, :, 30]), reads=["CS"], writes=["Us"])
        i = self.si % 2
        self.si += 1
        st, sk = self.stg[i], "stg%d" % i
        for c0 in range(0, 8, 4):
            ps, pk = f.pget()
            for j in range(4):
                f.op('pe', lambda en, j=j, c0=c0: en.transpose(ps[0:NS, j * 128:(j + 1) * 128], Us[:, c0 + j, :], self.ids[:, :]),
                     reads=["Us", "ids"], writes=[pk])
            self.copy('dve', st[0:NS, c0 * 128:(c0 + 4) * 128], ps[0:NS, 0:512], [pk], [sk])
        for b in range(NS):
            f.dma(self.O["conv_s"][b * 30 + 29:b * 30 + 30, :], st[b:b + 1, 0:D], reads=[sk])

    def ffn_ple(self, L, ti, t0, n):
        f = self.f
        I = self.I
        X = self.X
        self.rmsnorm(ti, t0, n, NF0 if L == 0 else NF1)

        def cons_up(m, outs):
            for (ps, pk, c0, cn) in outs:
                s, sk = self.scr()
                f.op('act', lambda en, s=s, ps=ps: en.activation(out=s[:, 0:cn], in_=ps[:, 0:cn], func=AF.Relu), reads=[pk], writes=[sk])
                f.op('dve', lambda en, s=s: en.tensor_tensor(out=self.hT[:, m, c0:c0 + cn], in0=s[:, 0:cn], in1=s[:, 0:cn], op=ALU.mult),
                     reads=[sk], writes=[("hT", m)])
        self.linear(I["mlp_up"][L], 8, 4 * D, lambda k: self.xn[:, k, :], [("xn", kk) for kk in range(8)], n, cons_up)

        def cons_dn(m, outs):
            for (ps, pk, c0, cn) in outs:
                f.op('dve', lambda en, ps=ps: en.tensor_tensor(out=X[:, m, t0 + c0:t0 + c0 + cn], in0=ps[:, 0:cn],
                                                               in1=X[:, m, t0 + c0:t0 + c0 + cn], op=ALU.add),
                     reads=[pk] + xkeys(ti, [m]), writes=xkeys(ti, [m]))
        self.linear(I["mlp_down"][L], 32, D, lambda k: self.hT[:, k, :], [("hT", kk) for kk in range(32)], n, cons_dn, bw=128)
        self.rmsnorm(ti, t0, n, NP0 if L == 0 else NP1)
        pT = self.hT
        for r in range(4):
            def sink(c, g, ps, pk, r=r):
                self.copy(self.ev_engine(), pT[:, c:c + g, r * 128:(r + 1) * 128], ps[:, 0:g * 128].rearrange("p (c t) -> p c t", c=g),
                          [pk], [("hT", cc) for cc in range(c, c + g)])
            self.load_rows_T(I["pp"][L, t0 + r * 128:t0 + (r + 1) * 128, :], 128, 256, sink)
        if ti == 3:
            def sink2(c, g, ps, pk):
                self.copy('dve', pT[:, c:c + g, 512:516], ps[:, 0:g * NS].rearrange("p (c t) -> p c t", c=g),
                          [pk], [("hT", cc) for cc in range(c, c + g)])
            self.load_rows_T(I["psm"][L, :, :], NS, 256, sink2)
        wp, wpk = None, None
        for mb in range(2):
            wgt, wgk = self.wload(I["ple_gate"][L][:, mb * 512:(mb + 1) * 512], 8, 512)
            wp, wpk = self.wload(I["ple_proj"][L][:, mb * 512:(mb + 1) * 512], 2, 512)
            for j in range(4):
                m = mb * 4 + j
                for (c0, cn) in segs(n):
                    pa, pak = f.pget()
                    pb, pbk = f.pget()
                    for k in range(8):
                        f.op('pe', lambda en, k=k, j=j, pa=pa: en.matmul(pa[:, 0:cn], lhsT=wgt[:, k, j * 128:(j + 1) * 128],
                                                                          rhs=self.xn[:, k, c0:c0 + cn], start=(k == 0), stop=(k == 7)),
                             reads=[wgk] + [("xn", kk) for kk in range(8)], writes=[pak])
                    for k in range(2):
                        f.op('pe', lambda en, k=k, j=j, pb=pb: en.matmul(pb[:, 0:cn], lhsT=wp[:, k, j * 128:(j + 1) * 128],
                                                                          rhs=pT[:, k, c0:c0 + cn], start=(k == 0), stop=(k == 1)),
                             reads=[wpk, ("hT", 0), ("hT", 1)], writes=[pbk])
                    s, sk = self.scr()
                    f.op('act', lambda en, s=s, pa=pa: en.activation(out=s[:, 0:cn], in_=pa[:, 0:cn], func=AF.Sigmoid), reads=[pak], writes=[sk])
                    f.op('dve', lambda en, s=s, pb=pb: en.tensor_tensor(out=s[:, 0:cn], in0=pb[:, 0:cn], in1=s[:, 0:cn], op=ALU.mult),
                         reads=[pbk, sk], writes=[sk])
                    f.op('dve', lambda en, s=s, m=m: en.tensor_tensor(out=X[:, m, t0 + c0:t0 + c0 + cn], in0=s[:, 0:cn],
                                                                      in1=X[:, m, t0 + c0:t0 + c0 + cn], op=ALU.add),
                         reads=[sk] + xkeys(ti, [m]), writes=xkeys(ti, [m]))


    def gen_table(self, ohsrc, ncols, TabAug, dst_view_fn):
        f = self.f
        for h0 in range(0, ncols, 1024):
            hw = min(1024, ncols - h0)
            i = self.si % 2
            self.si += 1
            st, sk = self.stg[i], "stg%d" % i
            f.dma(st[0:34, 0:hw], ohsrc[:, h0:h0 + hw], writes=[sk])
            for c0 in range(0, hw, 512):
                cw = min(512, hw - c0)
                ps, pk = f.pget()
                f.op('pe', lambda en, c0=c0, cw=cw, ps=ps, st=st: en.matmul(ps[0:16, 0:cw], lhsT=TabAug[0:34, 0:16], rhs=st[0:34, c0:c0 + cw],
                                                                            start=True, stop=True), reads=[sk, "TabAug"], writes=[pk])
                o, ok = self.scr()
                self.copy('dve', o[0:16, 0:cw], ps[0:16, 0:cw], [pk], [ok])
                f.dma(self.bsc[:, h0 + c0:h0 + c0 + cw], o[0:16, 0:cw], reads=[ok], writes=["bsc"])

    def nsa(self):
        f = self.f
        I = self.I
        X = self.X
        VT = self.VT
        win = I["w_in"]
        with self.scope():
            KT = f.sb("KT", [128, 2, 2, T], BF16)
            VA = f.sb("VA", [128, 16, 2, 4, 65], BF16)
            KCT = f.sb("KCT", [128, 2, 128], BF16)
            VCM = f.sb("VCM", [128, 4, 64], BF16)
            f.op('pool', lambda en: en.memset(VA[:, :, :, :, :], 1.0), writes=["VA"])
            with self.scope():
                CT = f.sb("CT", [128, 2, 2, T], BF16)
                W1r = f.sb("W1r", [128, 2, 32, 128], BF16)
                W2k = f.sb("W2k", [128, 128], BF16)
                W2v = f.sb("W2v", [128, 64], BF16)
                VTb = f.sb("VTb", [128, 32], BF16)
                hb = f.sb("hb", [128, 2])
                shb = f.sb("shb", [128, 128], BF16)
                for ci, nm in enumerate(["ck_w1", "cv_w1"]):
                    src = I[nm].rearrange("(t h) m -> h t m", h=64)
                    f.dma(W1r[0:64, ci], src, writes=["W1r"], q='pool')
                    f.dma(W1r[64:128, ci], src, writes=["W1r"], q='pool')
                f.dma(W2k[:, 0:64], I["ck_w2"], writes=["W2k"], q='pool')
                f.dma(W2k[:, 64:128], I["ck_w2"], writes=["W2k"], q='pool')
                f.dma(W2v[:, :], I["cv_w2"], writes=["W2v"], q='pool')
                f.op('dve', lambda en: en.tensor_copy(out=VTb[:, :], in_=VT[:, PEK:PEK + 32]), reads=["VT"], writes=["VTb"])
                kvn = ["cmp_k_p", "cmp_v_p", "slc_k_p", "slc_v_p", "win_k_p", "win_v_p"]
                for ti, (t0, n) in enumerate(TILES):
                    self.rmsnorm(ti, t0, n, NM1)
                    xk = [("xn", kk) for kk in range(8)]
                    for cb in range(3):
                        wv, wk = self.wload(win[:, 1024 + cb * 512:1024 + (cb + 1) * 512], 8, 512)
                        for r in range(4):
                            rt = ti * 4 + r
                            ps, pk = f.pget()
                            for k in range(8):
                                f.op('pe', lambda en, k=k, r=r, ps=ps: en.matmul(ps[:, 0:512], lhsT=self.xn[:, k, r * 128:(r + 1) * 128],
                                                                                  rhs=wv[:, k, :], start=(k == 0), stop=(k == 7)),
                                     reads=[wk] + xk, writes=[pk])
                            if cb < 2 or ti == 3:
                                s_, sk_ = self.scr()
                                self.copy('act', s_[:, 0:512], ps[:, 0:512], [pk], [sk_])
                                for hh in range(2):
                                    nm = kvn[2 * cb + hh]
                                    row0 = (t0 + r * 128) if cb < 2 else r * 128
                                    f.dma(self.O[nm][row0:row0 + 128, :], s_[:, hh * 256:(hh + 1) * 256], reads=[sk_])
                            if ti == 3 and r == 0:
                                ps4, pk4 = f.pget()
                                for k in range(8):
                                    f.op('pe', lambda en, k=k, ps4=ps4: en.matmul(ps4[0:NS, 0:512], lhsT=self.xn[:, k, 512:516],
                                                                                  rhs=wv[:, k, :], start=(k == 0), stop=(k == 7)),
                                         reads=[wk] + xk, writes=[pk4])
                                s4, sk4 = self.scr()
                                self.copy('dve', s4[0:NS, 0:512], ps4[0:NS, 0:512], [pk4], [sk4])
                                if cb < 2:
                                    for hh in range(2):
                                        f.dma(self.O[["cmp_k_s", "cmp_v_s", "slc_k_s", "slc_v_s"][2 * cb + hh]][:, :],
                                              s4[0:NS, hh * 256:(hh + 1) * 256], reads=[sk4])
                                else:
                                    for hh, (onm, inm) in enumerate([("win_k_s", "swk"), ("win_v_s", "swv")]):
                                        for b in range(NS):
                                            f.dma(self.O[onm][b * 512 + 511:b * 512 + 512, :], s4[b:b + 1, hh * 256:(hh + 1) * 256], reads=[sk4])
                                            f.dma(self.O[onm][b * 512:b * 512 + 511, :], I[inm][b * 512 + 1:b * 512 + 512, :])
                            if cb >= 1:
                                f.op('dve', lambda en, ps=ps, rt=rt, cb=cb: en.tensor_copy(
                                    out=VA[:, rt, cb - 1, :, 0:64], in_=ps[:, 256:512].rearrange("p (g d) -> p g d", g=4)),
                                    reads=[pk], writes=["VA"])
                    for ty, base in enumerate([1024, 1280, 1536, 2048]):
                        if self.stage < 2.2:
                            break
                        def cons(m, outs, ty=ty, t0=t0):
                            for (ps, pk, c0, cn) in outs:
                                if c0 != 0:
                                    continue
                                dst = CT[:, ty, m, t0:t0 + 512] if ty < 2 else KT[:, ty - 2, m, t0:t0 + 512]
                                key = ("CT", ty) if ty < 2 else ("KT", ty - 2)
                                self.copy(self.ev_engine(), dst, ps[:, 0:512], [pk], [key])
                        self.linear(win[:, base:base + 256], 8, 256, lambda k: self.xn[:, k, :], xk, 512, cons, bw=256)
                for ci in range(2):
                    if self.stage < 2.3:
                        break
                    pss = [f.pget(), f.pget()]
                    for hf in range(2):
                        ps, pk = pss[hf]
                        for r in range(16):
                            t = 2 * r + hf
                            f.op('pe', lambda en, t=t, r=r, hf=hf, ps=ps, ci=ci: en.matmul(
                                ps[:, 0:1], lhsT=W1r[hf * 64:(hf + 1) * 64, ci, t, :], rhs=VTb[hf * 64:(hf + 1) * 64, ci * 16 + r:ci * 16 + r + 1],
                                start=(r == 0), stop=(r == 15)), reads=["W1r", "VTb"], writes=[pk])
                    self.copy('dve', hb[:, ci:ci + 1], pss[0][0][:, 0:1], [pss[0][1]], ["hb"])
                    f.op('dve', lambda en, ci=ci: en.tensor_tensor(out=hb[:, ci:ci + 1], in0=pss[1][0][:, 0:1], in1=hb[:, ci:ci + 1], op=ALU.add),
                         reads=[pss[1][1], "hb"], writes=["hb"])
                for ci in range(2):
                    if self.stage < 2.4:
                        break
                    for g in range(4):
                        hf, sl = g % 2, g // 2
                        P0 = hf * 64
                        ps, pk = f.pget()
                        for t in range(32):
                            f.op('pe', lambda en, t=t, ps=ps, ci=ci, P0=P0, sl=sl: en.matmul(
                                ps[:, 0:127], lhsT=W1r[P0:P0 + 64, ci, t, :], rhs=CT[P0:P0 + 64, ci, sl, t:t + 16 * 126 + 1:16],
                                start=(t == 0), stop=(t == 31)), reads=["W1r", ("CT", ci)], writes=[pk])
                        f.op('act', lambda en, ps=ps, ci=ci: en.activation(out=shb[:, 0:127], in_=ps[:, 0:127], func=AF.Silu, bias=hb[:, ci:ci + 1]),
                             reads=[pk, "hb"], writes=["shb"])
                        ps2, pk2 = f.pget()
                        if ci == 0:
                            f.op('pe', lambda en, ps2=ps2: en.matmul(ps2[:, 0:127], lhsT=W2k[:, :], rhs=shb[:, 0:127], start=True, stop=True),
                                 reads=["W2k", "shb"], writes=[pk2])
                            self.copy('dve', KCT[P0:P0 + 64, sl, 0:127], ps2[P0:P0 + 64, 0:127], [pk2], ["KCT"])
                        else:
                            f.op('pe', lambda en, ps2=ps2: en.matmul(ps2[0:127, 0:64], lhsT=shb[:, 0:127], rhs=W2v[:, :], start=True, stop=True),
                                 reads=["W2v", "shb"], writes=[pk2])
                            self.copy('dve', VCM[0:127, g, :], ps2[0:127, 0:64], [pk2], ["VCM"])
            if self.stage < 3:
                return
            with self.scope():
                TabAug = f.sb("TabAug", [34, 16])
                BTd = f.sb("BTd", [128, 16, 128], BF16)
                BTo = f.sb("BTo", [128, 16, 128], BF16)
                BTt = f.sb("BTt", [128, 128], BF16)
                Gtab = f.sb("Gtab", [128, 16, 254], BF16)
                FV = f.sb("FV", [128, 16, 32])
                Ex = f.sb("Ex", [32, 16, 128], BF16)
                t31 = f.sb("t31", [1, 16])
                C31 = f.sb("C31", [1, 16, 128], BF16)
                onesb = f.sb("onesb", [1, 128], BF16)
                qT = f.sb("qT", [128, 16, 512], BF16)
                G = f.sb("G", [128, 4, 48])
                Oall = f.sb("Oall", [128, 16, 64])
                scm = f.sb("scm", [128, 4, 127])
                ee = f.sb("ee", [128, 4, 127])
                sm = f.sb("sm", [128, 64])
                PG = f.sb("PG", [128, 132])
                sco = f.sb("sco", [128, 96])
                NST = f.sb("NST", [32, 128], BF16)
                pT = f.sb("pT", [128, 512], BF16)
                PT = [f.sb("PT%d" % i, [128, 512], BF16) for i in range(2)]
                tmpo = f.sb("tmpo", [128, 4, 64])
                f.dma(TabAug[0:32, :], I["rel_table"], writes=["TabAug"])
                f.op('pool', lambda en: en.memset(TabAug[32:33, :], NEG), writes=["TabAug"])
                f.dma(TabAug[33:34, :], I["rel_table"][31:32, :], writes=["TabAug"])
                f.dma(t31[0:1, :], I["rel_table"][31:32, :], writes=["t31"])
                f.op('dve', lambda en: en.tensor_copy(out=C31[0:1, :, :], in_=t31[0:1, :][:, :, None].to_broadcast([1, 16, 128])),
                     reads=["t31"], writes=["C31"])
                f.op('pool', lambda en: en.memset(onesb[0:1, :], 1.0), writes=["onesb"])
                f.op('pool', lambda en: en.memset(PG[:, :], 0.0), writes=["PG"])
                f.dma(FV[:, :, :], I["fvtab"].rearrange("p (a b) -> p a b", a=16), writes=["FV"])
                f.dma(Ex[:, :, :], I["exm"].rearrange("p (a b) -> p a b", a=16), writes=["Ex"], q='pool')
                f.dma(BTt[:, :], I["tailm"], writes=["BTt"], q='pool')
                self.gen_table(I["ohd"], 16384, TabAug, None)
                f.dma(BTd[:, :, :], self.bsc[:, 0:16384].rearrange("h (k q) -> k h q", k=128), reads=["bsc"], writes=["BTd"], q='pool')
                self.gen_table(I["oho"], 16384, TabAug, None)
                f.dma(BTo[:, :, :], self.bsc[:, 0:16384].rearrange("h (k q) -> k h q", k=128), reads=["bsc"], writes=["BTo"], q='pool')
                self.gen_table(I["ohg"], 32512, TabAug, None)
                f.dma(Gtab[:, :, :], self.bsc[:, 0:32512].rearrange("h (q m) -> q h m", q=128), reads=["bsc"], writes=["Gtab"], q='pool')
                f.op('pool', lambda en: en.memset(qT[:, :, :], 0.0), writes=["qT"])

                for ti, (t0, n) in enumerate(TILES):
                    if self.stage < 4:
                        break
                    self.rmsnorm(ti, t0, n, NM1)
                    xk = [("xn", kk) for kk in range(8)]
                    for half in range(2):
                        wi_ = self.wi % 3
                        self.wi += 1
                        wb_, wqk = self.wbuf[wi_], "wb%d" % wi_
                        wq5 = wb_[:, 0:4096].rearrange("p (k s u i) -> p k s u i", k=8, s=4, u=2)
                        src5 = win[:, half * 512:(half + 1) * 512].rearrange("(k p) (u s i) -> p k u s i", p=128, u=2, s=4)
                        for u in range(2):
                            for s4 in range(4):
                                f.dma(wq5[:, :, s4, u, :], src5[:, :, u, s4, :], writes=[wqk], q='pool')
                        wq = wb_[:, 0:4096].rearrange("p (k m) -> p k m", k=8)
                        for sl4 in range(4):
                            s_ = half * 4 + sl4
                            A = sl4
                            ps, pk = f.pget()
                            for k in range(8):
                                f.op('pe', lambda en, k=k, A=A, ps=ps: en.matmul(
                                    ps[:, 0:512],
                                    lhsT=wq[:, k, 128 * A:128 * A + 128],
                                    rhs=self.xn[:, k, 0:512], start=(k == 0), stop=(k == 7)), reads=[wqk] + xk, writes=[pk])
                            gA = 2 * half
                            f.op('act', lambda en, ps=ps, gA=gA, A=A: en.activation(out=qT[0:64, 4 * gA + A, :], in_=ps[0:64, 0:512], func=AF.Identity, scale=0.125),
                                 reads=[pk], writes=["qT"])
                            f.op('act', lambda en, ps=ps, gA=gA, A=A: en.activation(out=qT[64:128, 4 * (gA + 1) + A, :], in_=ps[64:128, 0:512], func=AF.Identity, scale=0.125),
                                 reads=[pk], writes=["qT"])
                    wg, wgk = self.wload(win[:, 2560:2608], 8, 48)
                    for r in range(4):
                        ps, pk = f.pget()
                        for k in range(8):
                            f.op('pe', lambda en, k=k, r=r, ps=ps: en.matmul(ps[:, 0:48], lhsT=self.xn[:, k, r * 128:(r + 1) * 128], rhs=wg[:, k, :],
                                                                              start=(k == 0), stop=(k == 7)), reads=[wgk] + xk, writes=[pk])
                        f.op('act', lambda en, ps=ps, r=r: en.activation(out=G[:, r, :], in_=ps[:, 0:48], func=AF.Sigmoid), reads=[pk], writes=["G"])
                    OT = self.xn
                    for r in range(4):
                        qt = ti * 4 + r
                        qs = slice(r * 128, (r + 1) * 128)
                        for g in range(4):
                            hf, sl = g % 2, g // 2
                            P0 = hf * 64
                            s0 = 4 * sl
                            ps, pk = f.pget()
                            for j in range(4):
                                f.op('pe', lambda en, j=j, ps=ps: en.matmul(ps[:, j * 127:(j + 1) * 127], lhsT=qT[:, 4 * g + j, qs],
                                                                            rhs=KCT[:, sl, 0:127], start=True, stop=True),
                                     reads=["qT", "KCT"], writes=[pk])
                            f.op('dve', lambda en, ps=ps: en.tensor_tensor(out=scm[:, :, :], in0=ps[:, 0:508].rearrange("p (h j) -> p h j", h=4),
                                                                           in1=Gtab[:, 4 * g:4 * g + 4, 127 - 8 * qt:254 - 8 * qt], op=ALU.add),
                                 reads=[pk, "Gtab"], writes=["scm"])
                            f.op('dve', lambda en: en.tensor_reduce(out=sm[:, 0:4], in_=scm[:, :, :], axis=AX.X, op=ALU.max), reads=["scm"], writes=["sm"])
                            f.op('dve', lambda en: en.tensor_scalar(out=sm[:, 4:8], in0=sm[:, 0:4], scalar1=-10000.0, scalar2=-1.0, op0=ALU.max, op1=ALU.mult),
                                 reads=["sm"], writes=["sm"])
                            f.op('dve', lambda en: en.memset(sm[:, 8:12], 0.0), reads=["sm"], writes=["sm"])
                            for j in range(4):
                                f.op('act', lambda en, j=j: en.activation(out=ee[:, j, :], in_=scm[:, j, :], func=AF.Exp, bias=sm[:, 4 + j:5 + j],
                                                                         accum_out=sm[:, 8 + j:9 + j]), reads=["scm", "sm"], writes=["ee", "sm"])
                            f.op('dve', lambda en: en.tensor_scalar(out=sm[:, 12:16], in0=sm[:, 8:12], scalar1=1e-30, scalar2=None, op0=ALU.max),
                                 reads=["sm"], writes=["sm"])
                            f.op('dve', lambda en: en.reciprocal(out=sm[:, 12:16], in_=sm[:, 12:16]), reads=["sm"], writes=["sm"])
                            f.op('dve', lambda en: en.tensor_tensor(out=ee[:, :, :], in0=ee[:, :, :], in1=sm[:, 12:16][:, :, None].to_broadcast([128, 4, 127]),
                                                                    op=ALU.mult), reads=["ee", "sm"], writes=["ee"])
                            f.op('dve', lambda en: en.tensor_reduce(out=PG[:, 1:128], in_=ee[:, :, :].rearrange("p h j -> p j h"), axis=AX.X, op=ALU.add),
                                 reads=["ee"], writes=["PG"])
                            f.op('dve', lambda en: en.tensor_tensor(out=sco[:, 0:32], in0=PG[:, 1:129:4], in1=PG[:, 2:130:4], op=ALU.add), reads=["PG"], writes=["sco"])
                            f.op('dve', lambda en: en.tensor_tensor(out=sco[:, 0:32], in0=sco[:, 0:32], in1=PG[:, 3:131:4], op=ALU.add), reads=["PG", "sco"], writes=["sco"])
                            f.op('dve', lambda en: en.scalar_tensor_tensor(out=sco[:, 0:32], in0=sco[:, 0:32], scalar=2.0, in1=PG[:, 0:128:4],
                                                                           op0=ALU.mult, op1=ALU.add), reads=["PG", "sco"], writes=["sco"])
                            f.op('dve', lambda en: en.tensor_tensor(out=sco[:, 0:32], in0=sco[:, 0:32], in1=PG[:, 4:132:4], op=ALU.add), reads=["PG", "sco"], writes=["sco"])
                            f.op('dve', lambda en: en.tensor_tensor(out=sco[:, 0:32], in0=sco[:, 0:32], in1=FV[:, qt, :], op=ALU.add), reads=["FV", "sco"], writes=["sco"])
                            f.op('dve', lambda en: en.max(out=sm[:, 16:24], in_=sco[:, 0:32]), reads=["sco"], writes=["sm"])
                            f.op('dve', lambda en: en.match_replace(out=sco[:, 32:64], in_to_replace=sm[:, 16:24], in_values=sco[:, 0:32], imm_value=-1e30),
                                 reads=["sco", "sm"], writes=["sco"])
                            f.op('dve', lambda en: en.max(out=sm[:, 24:32], in_=sco[:, 32:64]), reads=["sco"], writes=["sm"])
                            f.op('dve', lambda en: en.tensor_scalar(out=sco[:, 64:96], in0=sco[:, 0:32], scalar1=sm[:, 31:32], scalar2=None, op0=ALU.is_ge),
                                 reads=["sco", "sm"], writes=["sco"])
                            f.op('dve', lambda en: en.tensor_scalar(out=sco[:, 64:96], in0=sco[:, 64:96], scalar1=-1.0, scalar2=-NEG, op0=ALU.add, op1=ALU.mult),
                                 reads=["sco"], writes=["sco"])
                            ps, pk = f.pget()
                            f.op('pe', lambda en, ps=ps: en.transpose(ps[0:32, 0:128], sco[:, 64:96], self.ids[:, :]), reads=["sco", "ids"], writes=[pk])
                            self.copy('act', NST[0:32, :], ps[0:32, 0:128], [pk], ["NST"])
                            ps, pk = f.pget()
                            for j in range(4):
                                f.op('pe', lambda en, j=j, ps=ps: en.transpose(ps[0:127, j * 128:(j + 1) * 128], ee[:, j, :], self.ids[:, :]),
                                     reads=["ee", "ids"], writes=[pk])
                            self.copy('act', pT[0:127, :], ps[0:127, 0:512], [pk], ["pT"])
                            ps, pk = f.pget()
                            for j in range(4):
                                f.op('pe', lambda en, j=j, ps=ps: en.matmul(ps[:, j * 64:(j + 1) * 64], lhsT=pT[0:127, j * 128:(j + 1) * 128],
                                                                            rhs=VCM[0:127, g, :], start=True, stop=True), reads=["pT", "VCM"], writes=[pk])
                            f.op('dve', lambda en, ps=ps: en.tensor_tensor(out=Oall[:, 4 * g:4 * g + 4, :], in0=ps[:, 0:256].rearrange("p (h d) -> p h d", h=4),
                                                                           in1=G[:, r, 4 * g:4 * g + 4][:, :, None].to_broadcast([128, 4, 64]), op=ALU.mult),
                                 reads=[pk, "G"], writes=[("Oall", g)])
                            for br in range(2):
                                if self.stage < 5 + br:
                                    break
                                pa, pak = f.pacc(br)
                                kts = list(range(0, qt + 1)) if br == 0 else list(range(max(0, qt - 4), qt + 1))
                                for kt in kts:
                                    ps, pk = f.pget()
                                    special = (kt == qt) or (kt == qt - 1) or (br == 1 and kt == qt - 4)
                                    rd = [("KT", br), "qT", "C31", "onesb"]
                                    f.op('pe', lambda en, ps=ps, kt=kt, br=br: en.matmul(ps[:, 0:512], lhsT=KT[:, br, sl, kt * 128:(kt + 1) * 128],
                                                                                       rhs=qT[:, 4 * g:4 * g + 4, qs], start=True, stop=False),
                                         reads=rd, writes=[pk])
                                    if br == 0:
                                        f.op('pe', lambda en, ps=ps, kt=kt: en.matmul(ps[:, 0:512], lhsT=Ex[0:32, kt, :],
                                                                                    rhs=NST[0:32, :][:, None, :].to_broadcast([32, 4, 128]), start=False, stop=False),
                                             reads=["Ex", "NST"], writes=[pk])
                                    f.op('pe', lambda en, ps=ps, sp_=special: en.matmul(ps[:, 0:512], lhsT=onesb[0:1, :], rhs=C31[0:1, 4 * g:4 * g + 4, :],
                                                                                       start=False, stop=(not sp_)), reads=rd, writes=[pk])
                                    if kt == qt:
                                        f.op('pe', lambda en, ps=ps: en.matmul(ps[:, 0:512], lhsT=self.idb[:, :], rhs=BTd[:, 4 * g:4 * g + 4, :], start=False, stop=True),
                                             reads=["idb", "BTd"], writes=[pk])
                                    elif kt == qt - 1:
                                        f.op('pe', lambda en, ps=ps: en.matmul(ps[:, 0:512], lhsT=self.idb[:, :], rhs=BTo[:, 4 * g:4 * g + 4, :], start=False, stop=True),
                                             reads=["idb", "BTo"], writes=[pk])
                                    elif br == 1 and kt == qt - 4:
                                        f.op('pe', lambda en, ps=ps: en.matmul(ps[:, 0:512], lhsT=self.idb[:, :],
                                                                               rhs=BTt[:, :][:, None, :].to_broadcast([128, 4, 128]), start=False, stop=True),
                                             reads=["idb", "BTt"], writes=[pk])
                                    pt_, ptk = PT[kt % 2], "PT%d" % (kt % 2)
                                    f.op('act', lambda en, ps=ps, pt_=pt_: en.activation(out=pt_[:, :], in_=ps[:, 0:512], func=AF.Exp), reads=[pk], writes=[ptk])
                                    for j in range(4):
                                        f.op('pe', lambda en, j=j, pt_=pt_, kt=kt, br=br, kts=kts: en.matmul(
                                            pa[:, j * 65:(j + 1) * 65], lhsT=pt_[:, j * 128:(j + 1) * 128], rhs=VA[:, kt, br, g, :],
                                            start=(kt == kts[0] and j == 0), stop=(kt == kts[-1]), skip_group_check=True), reads=[ptk, "VA"], writes=[pak])
                                pav = pa[:, 0:260].rearrange("p (h d) -> p h d", h=4)
                                f.op('dve', lambda en, pav=pav: en.tensor_scalar(out=sm[:, 32:36], in0=pav[:, :, 64], scalar1=1e-30, scalar2=None, op0=ALU.max),
                                     reads=[pak], writes=["sm"])
                                f.op('dve', lambda en: en.reciprocal(out=sm[:, 32:36], in_=sm[:, 32:36]), reads=["sm"], writes=["sm"])
                                f.op('dve', lambda en, br=br: en.tensor_tensor(out=sm[:, 32:36], in0=sm[:, 32:36], in1=G[:, r, 16 * (br + 1) + 4 * g:16 * (br + 1) + 4 * g + 4],
                                                                               op=ALU.mult), reads=["sm", "G"], writes=["sm"])
                                f.op('dve', lambda en, pav=pav: en.tensor_tensor(out=tmpo[:, :, :], in0=pav[:, :, 0:64],
                                                                                 in1=sm[:, 32:36][:, :, None].to_broadcast([128, 4, 64]), op=ALU.mult),
                                     reads=[pak, "sm"], writes=["tmpo"])
                                f.op('pool', lambda en: en.tensor_tensor(out=Oall[:, 4 * g:4 * g + 4, :], in0=Oall[:, 4 * g:4 * g + 4, :], in1=tmpo[:, :, :], op=ALU.add),
                                     reads=["tmpo", ("Oall", g)], writes=[("Oall", g)])
                        O2 = Oall[:, :, :].rearrange("p h d -> p (h d)")
                        for c0 in range(0, 8, 4):
                            ps, pk = f.pget()
                            for j in range(4):
                                f.op('pe', lambda en, j=j, c0=c0, ps=ps: en.transpose(ps[:, j * 128:(j + 1) * 128], O2[:, (c0 + j) * 128:(c0 + j + 1) * 128], self.ids[:, :]),
                                     reads=[("Oall", gg) for gg in range(4)] + ["ids"], writes=[pk])
                            self.copy(self.ev_engine(), OT[:, c0:c0 + 4, qs], ps[:, 0:512].rearrange("p (c t) -> p c t", c=4), [pk],
                                      [("xn", cc) for cc in range(c0, c0 + 4)])
                    def cons_o(m, outs, ti=ti, t0=t0):
                        for (ps, pk, c0, cn) in outs:
                            f.op('dve', lambda en, ps=ps: en.tensor_tensor(out=X[:, m, t0 + c0:t0 + c0 + cn], in0=ps[:, 0:cn],
                                                                           in1=X[:, m, t0 + c0:t0 + c0 + cn], op=ALU.add),
                                 reads=[pk, ("X", m, ti)], writes=[("X", m, ti)])
                    self.linear(I["w_out"], 8, D, lambda k: OT[:, k, :], [("xn", kk) for kk in range(8)], 512, cons_o)


    def gather_KT_V(self, pool, IDX, b, page0, npages, CT, ci, VAs=None, plain=None):
        f = self.f
        for r0 in range(0, npages, 4):
            i = self.si % 2
            self.si += 1
            st, sk = self.stg[i], "stg%d" % i
            for j in range(4):
                pg_ = page0 + r0 + j
                if plain is not None:
                    f.dma(st[:, j * 256:(j + 1) * 256], plain[(r0 + j) * 128:(r0 + j + 1) * 128, :], writes=[sk])
                else:
                    f.dma(None, None, reads=["IDX"], writes=[sk], q='pool',
                          fn=lambda en, j=j, pg_=pg_, st=st: en.indirect_dma_start(
                              out=st[:, j * 256:(j + 1) * 256], out_offset=None, in_=pool[:, :],
                              in_offset=bass.IndirectOffsetOnAxis(ap=IDX[:, b, pg_:pg_ + 1], axis=0)))
            if VAs is not None:
                f.op('act', lambda en, st=st, r0=r0: en.activation(
                    out=VAs[:, r0:r0 + 4, :, 0:64], in_=st[:, 0:1024].rearrange("p (r g d) -> p r g d", r=4, g=4), func=AF.Identity),
                    reads=[sk], writes=["VAs"])
                continue
            for gp in range(2):
                ps, pk = f.pget()
                for j in range(4):
                    f.op('pe', lambda en, j=j, gp=gp, ps=ps, st=st: en.transpose(
                        ps[:, j * 128:(j + 1) * 128], st[:, j * 256 + gp * 128:j * 256 + (gp + 1) * 128], self.ids[:, :]),
                        reads=[sk, "ids"], writes=[pk])
                self.copy(self.ev_engine(), CT[:, ci, gp, r0 * 128:(r0 + 4) * 128], ps[:, 0:512], [pk], [("CT", ci)])

    def nsa_samples(self):
        f = self.f
        I = self.I
        X = self.X
        VT = self.VT
        win = I["w_in"]
        with self.scope():
            CT = f.sb("CTs", [128, 2, 2, T], BF16)
            CTx = f.sb("CTx", [128, 2, 2, 7, 32], BF16)
            W1r = f.sb("W1rs", [128, 2, 32, 128], BF16)
            W2k = f.sb("W2ks", [128, 128], BF16)
            W2v = f.sb("W2vs", [128, 64], BF16)
            VTb = f.sb("VTbs", [128, 32], BF16)
            hb = f.sb("hbs", [128, 2])
            shb = f.sb("shbs", [128, 128], BF16)
            KCTs = f.sb("KCTs", [128, 2, 1024], BF16)
            VCMs = f.sb("VCMs", [128, 8, 4, 64], BF16)
            VAs = f.sb("VAs", [128, 16, 4, 65], BF16)
            PTi = f.sb("PTi", [128, NS, 128], I32)
            IDX = f.sb("IDX", [128, NS, 128], I32)
            iop = f.sb("iop", [128, 1], I32)
            iopf = f.sb("iopf", [128, 1])
            IDXf = self.sc[0][:, 0:NS * 128] if False else None
            QZS = f.sb("QZS", [128, NS, 16], BF16)
            KTn = f.sb("KTn", [128, 2, 2, NS], BF16)
            VN = f.sb("VN", [1, NS, 2, 4, 65], BF16)
            gT = f.sb("gT", [48, NS])
            Jm = f.sb("Jm", [48, 4])
            Rm = f.sb("Rm", [48, 12])
            Lg = f.sb("Lg", [48, 4])
            G4 = f.sb("G4", [4, 12])
            TabAug = f.sb("TabAugs", [34, 16])
            OHs = f.sb("OHs", [34, 1024])
            OH127 = f.sb("OH127", [34, 128])
            B127 = f.sb("B127", [128, 16], BF16)
            trow = f.sb("trow", [1, 32])
            C31s = f.sb("C31s", [1, 16], BF16)
            T0s = f.sb("T0s", [1, 16], BF16)
            onesb = f.sb("onesbs", [1, 128], BF16)
            E2 = f.sb("E2", [33, 128], BF16)
            NS2 = f.sb("NS2", [33, 4, 128], BF16)
            FVs = f.sb("FVs", [1, 256])
            scs = f.sb("scs", [4, 1024])
            ees = f.sb("ees", [4, 1024])
            bss = f.sb("bss", [4, 1024])
            sms = f.sb("sms", [4, 16])
            PGs = f.sb("PGs", [1, 1032])
            sco = f.sb("scos", [1, 784])
            NSr = f.sb("NSr", [1, 384], BF16)
            pTs = f.sb("pTs", [128, 8, 4], BF16)
            Ps = f.sb("Ps", [128, 64], BF16)
            Pn = f.sb("Pn", [1, 4], BF16)
            Ob = f.sb("Ob", [4, 3, 4, 64])
            Osum = f.sb("Osum", [4, 4, 128])
            tmpo = f.sb("tmpos", [4, 4, 64])
            shx = f.sb("shx", [128, 8], BF16)
            vx = f.sb("vx", [8, 64], BF16)
            OTs = f.sb("OTs", [128, 8, NS], BF16)
            for ci, nm in enumerate(["ck_w1", "cv_w1"]):
                src = I[nm].rearrange("(t h) m -> h t m", h=64)
                f.dma(W1r[0:64, ci], src, writes=["W1r"], q='pool')
                f.dma(W1r[64:128, ci], src, writes=["W1r"], q='pool')
            f.dma(W2k[:, 0:64], I["ck_w2"], writes=["W2k"], q='pool')
            f.dma(W2k[:, 64:128], I["ck_w2"], writes=["W2k"], q='pool')
            f.dma(W2v[:, :], I["cv_w2"], writes=["W2v"], q='pool')
            f.op('dve', lambda en: en.tensor_copy(out=VTb[:, :], in_=VT[:, PEK:PEK + 32]), reads=["VT"], writes=["VTb"])
            for ci in range(2):
                pss = [f.pget(), f.pget()]
                for hf in range(2):
                    ps, pk = pss[hf]
                    for r in range(16):
                        t = 2 * r + hf
                        f.op('pe', lambda en, t=t, r=r, hf=hf, ps=ps, ci=ci: en.matmul(
                            ps[:, 0:1], lhsT=W1r[hf * 64:(hf + 1) * 64, ci, t, :], rhs=VTb[hf * 64:(hf + 1) * 64, ci * 16 + r:ci * 16 + r + 1],
                            start=(r == 0), stop=(r == 15)), reads=["W1r", "VTb"], writes=[pk])
                self.copy('dve', hb[:, ci:ci + 1], pss[0][0][:, 0:1], [pss[0][1]], ["hb"])
                f.op('dve', lambda en, ci=ci: en.tensor_tensor(out=hb[:, ci:ci + 1], in0=pss[1][0][:, 0:1], in1=hb[:, ci:ci + 1], op=ALU.add),
                     reads=[pss[1][1], "hb"], writes=["hb"])
            f.dma(TabAug[0:32, :], I["rel_table"], writes=["TabAug"])
            f.op('pool', lambda en: en.memset(TabAug[32:33, :], NEG), writes=["TabAug"])
            f.dma(TabAug[33:34, :], I["rel_table"][31:32, :], writes=["TabAug"])
            f.dma(OHs[:, :], I["ohs"], writes=["OHs"])
            f.dma(OH127[:, :], I["oh127"], writes=["OH127"])
            f.dma(trow[0:1, 0:16], I["rel_table"][31:32, :], writes=["trow"])
            f.dma(trow[0:1, 16:32], I["rel_table"][0:1, :], writes=["trow"])
            f.op('dve', lambda en: en.tensor_copy(out=C31s[0:1, :], in_=trow[0:1, 0:16]), reads=["trow"], writes=["C31s"])
            f.op('dve', lambda en: en.tensor_copy(out=T0s[0:1, :], in_=trow[0:1, 16:32]), reads=["trow"], writes=["T0s"])
            f.op('pool', lambda en: en.memset(onesb[0:1, :], 1.0), writes=["onesb"])
            f.dma(E2[:, :], I["e2"], writes=["E2"], q='pool')
            f.dma(FVs[:, :], I["fvs"], writes=["FVs"])
            f.dma(Jm[:, :], I["jm"], writes=["Jm"])
            f.dma(Rm[:, :], I["rm"], writes=["Rm"])
            f.op('pool', lambda en: en.memset(NS2[:, :, :], 0.0), writes=["NS2"])
            f.op('pool', lambda en: en.memset(PGs[:, :], 0.0), writes=["PGs"])
            f.op('pool', lambda en: en.memset(VAs[:, :, :, :], 1.0), writes=["VAs"])
            f.op('pool', lambda en: en.memset(VN[:, :, :, :, :], 1.0), writes=["VN"])
            f.op('pool', lambda en: en.memset(QZS[:, :, :], 0.0), writes=["QZS"])
            f.op('pool', lambda en: en.memset(Osum[:, :, :], 0.0), writes=["Osum"])
            ps, pk = f.pget()
            f.op('pe', lambda en, ps=ps: en.matmul(ps[:, 0:16], lhsT=OH127[0:34, :], rhs=TabAug[0:34, :], start=True, stop=True),
                 reads=["OH127", "TabAug"], writes=[pk])
            self.copy('dve', B127[:, :], ps[:, 0:16], [pk], ["B127"])
            f.dma(PTi[:, :, :].rearrange("p a b -> p (a b)"), I["ptab"].partition_broadcast(128), writes=["PTi"])
            f.op('pool', lambda en: en.iota(iop[:, 0:1], pattern=[[0, 1]], base=0, channel_multiplier=1), writes=["iop"])
            IDXf, _ = self.scr()
            IDXf = IDXf[:, 0:NS * 128]
            f.op('dve', lambda en: en.tensor_copy(out=iopf[:, 0:1], in_=iop[:, 0:1]), reads=["iop"], writes=["iopf"])
            f.op('dve', lambda en: en.tensor_copy(out=IDXf[:, :], in_=PTi[:, :, :].rearrange("p a b -> p (a b)")), reads=["PTi"], writes=["sc0", "sc1", "sc2"])
            f.op('dve', lambda en: en.tensor_scalar(out=IDXf[:, :], in0=IDXf[:, :], scalar1=128.0, scalar2=iopf[:, 0:1], op0=ALU.mult, op1=ALU.add),
                 reads=["iopf", "sc0", "sc1", "sc2"], writes=["sc0", "sc1", "sc2"])
            f.op('dve', lambda en: en.tensor_copy(out=IDX[:, :, :].rearrange("p a b -> p (a b)"), in_=IDXf[:, :]), reads=["sc0", "sc1", "sc2"], writes=["IDX"])
            self.rmsnorm(3, 1536, 516, NM1)
            xk = [("xn", kk) for kk in range(8)]
            xs_ = lambda k: self.xn[:, k, 512:516]
            for half in range(2):
                wi_ = self.wi % 3
                self.wi += 1
                wb_, wqk = self.wbuf[wi_], "wb%d" % wi_
                wq5 = wb_[:, 0:4096].rearrange("p (k s u i) -> p k s u i", k=8, s=4, u=2)
                src5 = win[:, half * 512:(half + 1) * 512].rearrange("(k p) (u s i) -> p k u s i", p=128, u=2, s=4)
                for u in range(2):
                    for s4 in range(4):
                        f.dma(wq5[:, :, s4, u, :], src5[:, :, u, s4, :], writes=[wqk], q='pool')
                wq = wb_[:, 0:4096].rearrange("p (k m) -> p k m", k=8)
                for A in range(4):
                    ps, pk = f.pget()
                    for k in range(8):
                        f.op('pe', lambda en, k=k, A=A, ps=ps: en.matmul(ps[:, 0:NS], lhsT=wq[:, k, 128 * A:128 * A + 128], rhs=xs_(k),
                                                                         start=(k == 0), stop=(k == 7)), reads=[wqk] + xk, writes=[pk])
                    gA = 2 * half
                    f.op('act', lambda en, ps=ps, gA=gA, A=A: en.activation(out=QZS[0:64, :, 4 * gA + A], in_=ps[0:64, 0:NS], func=AF.Identity, scale=0.125),
                         reads=[pk], writes=["QZS"])
                    f.op('act', lambda en, ps=ps, gA=gA, A=A: en.activation(out=QZS[64:128, :, 4 * (gA + 1) + A], in_=ps[64:128, 0:NS], func=AF.Identity, scale=0.125),
                         reads=[pk], writes=["QZS"])
            wg, wgk = self.wload(win[:, 2560:2608], 8, 48)
            ps, pk = f.pget()
            for k in range(8):
                f.op('pe', lambda en, k=k, ps=ps: en.matmul(ps[0:48, 0:NS], lhsT=wg[:, k, 0:48], rhs=xs_(k), start=(k == 0), stop=(k == 7)),
                     reads=[wgk] + xk, writes=[pk])
            f.op('act', lambda en, ps=ps: en.activation(out=gT[:, :], in_=ps[0:48, 0:NS], func=AF.Sigmoid), reads=[pk], writes=["gT"])
            for ty, base in enumerate([1536, 2048]):
                wv, wk = self.wload(win[:, base:base + 512], 8, 512)
                for m in range(2):
                    ps, pk = f.pget()
                    for k in range(8):
                        f.op('pe', lambda en, k=k, m=m, ps=ps: en.matmul(ps[:, 0:NS], lhsT=wv[:, k, m * 128:(m + 1) * 128], rhs=xs_(k),
                                                                         start=(k == 0), stop=(k == 7)), reads=[wk] + xk, writes=[pk])
                    self.copy('dve', KTn[:, ty, m, :], ps[:, 0:NS], [pk], ["KTn"])
                for b in range(NS):
                    ps, pk = f.pget()
                    for k in range(8):
                        f.op('pe', lambda en, k=k, b=b, ps=ps: en.matmul(ps[0:1, 0:256], lhsT=self.xn[:, k, 512 + b:513 + b], rhs=wv[:, k, 256:512],
                                                                         start=(k == 0), stop=(k == 7)), reads=[wk] + xk, writes=[pk])
                    self.copy('dve', VN[0:1, b, ty, :, 0:64], ps[0:1, 0:256].rearrange("p (g d) -> p g d", g=4), [pk], ["VN"])

            def compress(ci, nblk, rhs_fn, kdst_fn, vdst_fn, sh):
                for g in range(4):
                    hf, sl = g % 2, g // 2
                    P0 = hf * 64
                    ps, pk = f.pget()
                    for t in range(32):
                        f.op('pe', lambda en, t=t, ps=ps, P0=P0, sl=sl: en.matmul(
                            ps[:, 0:nblk], lhsT=W1r[P0:P0 + 64, ci, t, :], rhs=rhs_fn(P0, sl, t), start=(t == 0), stop=(t == 31)),
                            reads=["W1r", ("CT", ci), "CTx"], writes=[pk])
                    f.op('act', lambda en, ps=ps: en.activation(out=sh[:, 0:nblk], in_=ps[:, 0:nblk], func=AF.Silu, bias=hb[:, ci:ci + 1]),
                         reads=[pk, "hb"], writes=["sh"])
                    ps2, pk2 = f.pget()
                    if ci == 0:
                        f.op('pe', lambda en, ps2=ps2: en.matmul(ps2[:, 0:nblk], lhsT=W2k[:, :], rhs=sh[:, 0:nblk], start=True, stop=True),
                             reads=["W2k", "sh"], writes=[pk2])
                        kdst_fn(g, P0, sl, ps2, pk2)
                    else:
                        f.op('pe', lambda en, ps2=ps2: en.matmul(ps2[0:nblk, 0:64], lhsT=sh[:, 0:nblk], rhs=W2v[:, :], start=True, stop=True),
                             reads=["W2v", "sh"], writes=[pk2])
                        vdst_fn(g, ps2, pk2)

            for b in range(NS):
                f.op('pool', lambda en: en.memset(KCTs[:, :, :], 0.0), writes=["KCTs"])
                f.op('pool', lambda en: en.memset(VCMs[:, :, :, :], 0.0), writes=["VCMs"])
                for s_ in range(8):
                    for ci, pool in enumerate([I["cck"], I["ccv"]]):
                        self.gather_KT_V(pool, IDX, b, 16 * s_, 16, CT, ci)
                        for sl in range(2):
                            if s_ < 7:
                                f.op('pool', lambda en, ci=ci, sl=sl, s_=s_: en.tensor_copy(out=CTx[:, ci, sl, s_, 0:16], in_=CT[:, ci, sl, 2032:2048]),
                                     reads=[("CT", ci)], writes=["CTx"])
                            if s_ > 0:
                                f.op('pool', lambda en, ci=ci, sl=sl, s_=s_: en.tensor_copy(out=CTx[:, ci, sl, s_ - 1, 16:32], in_=CT[:, ci, sl, 0:16]),
                                     reads=[("CT", ci)], writes=["CTx"])

                        def kd(g, P0, sl, ps2, pk2, s_=s_):
                            self.copy('dve', KCTs[P0:P0 + 64, sl, s_ * 128:s_ * 128 + 127], ps2[P0:P0 + 64, 0:127], [pk2], ["KCTs"])

                        def vd(g, ps2, pk2, s_=s_):
                            self.copy('dve', VCMs[0:127, s_, g, :], ps2[0:127, 0:64], [pk2], ["VCMs"])
                        compress(ci, 127, lambda P0, sl, t, ci=ci: CT[P0:P0 + 64, ci, sl, t:t + 16 * 126 + 1:16], kd, vd, shb)
                for ci in range(2):
                    def kdx(g, P0, sl, ps2, pk2):
                        self.copy('dve', KCTs[P0:P0 + 64, sl, 127:127 + 128 * 6 + 1:128], ps2[P0:P0 + 64, 0:7], [pk2], ["KCTs"])

                    def vdx(g, ps2, pk2):
                        self.copy('dve', vx[0:7, :], ps2[0:7, 0:64], [pk2], ["vx"])
                        for s_ in range(7):
                            f.dma(VCMs[127:128, s_, g, :], vx[s_:s_ + 1, :], reads=["vx"], writes=["VCMs"])
                    compress(ci, 7, lambda P0, sl, t, ci=ci: CTx[P0:P0 + 64, ci, sl, 0:7, t], kdx, vdx, shx)
                for g in range(4):
                    sl = g // 2
                    for c0 in (0, 512):
                        ps, pk = f.pget()
                        f.op('pe', lambda en, ps=ps, c0=c0: en.matmul(ps[0:4, 0:512], lhsT=QZS[:, b, 4 * g:4 * g + 4], rhs=KCTs[:, sl, c0:c0 + 512],
                                                                      start=True, stop=True), reads=["QZS", "KCTs"], writes=[pk])
                        pb, pbk = f.pget()
                        f.op('pe', lambda en, pb=pb, c0=c0: en.matmul(pb[0:4, 0:512], lhsT=TabAug[0:34, 4 * g:4 * g + 4], rhs=OHs[0:34, c0:c0 + 512],
                                                                      start=True, stop=True), reads=["TabAug", "OHs"], writes=[pbk])
                        self.copy('act', bss[0:4, c0:c0 + 512], pb[0:4, 0:512], [pbk], ["bss"])
                        f.op('dve', lambda en, ps=ps, c0=c0: en.tensor_tensor(out=scs[0:4, c0:c0 + 512], in0=ps[0:4, 0:512], in1=bss[0:4, c0:c0 + 512], op=ALU.add),
                             reads=[pk, "bss"], writes=["scs"])
                    f.op('dve', lambda en: en.tensor_reduce(out=sms[0:4, 0:1], in_=scs[0:4, :], axis=AX.X, op=ALU.max), reads=["scs"], writes=["sms"])
                    f.op('dve', lambda en: en.tensor_scalar(out=sms[0:4, 1:2], in0=sms[0:4, 0:1], scalar1=-10000.0, scalar2=-1.0, op0=ALU.max, op1=ALU.mult),
                         reads=["sms"], writes=["sms"])
                    f.op('dve', lambda en: en.memset(sms[0:4, 2:3], 0.0), reads=["sms"], writes=["sms"])
                    f.op('act', lambda en: en.activation(out=ees[0:4, :], in_=scs[0:4, :], func=AF.Exp, bias=sms[0:4, 1:2], accum_out=sms[0:4, 2:3]),
                         reads=["scs", "sms"], writes=["ees", "sms"])
                    f.op('dve', lambda en: en.tensor_scalar(out=sms[0:4, 3:4], in0=sms[0:4, 2:3], scalar1=1e-30, scalar2=None, op0=ALU.max), reads=["sms"], writes=["sms"])
                    f.op('dve', lambda en: en.reciprocal(out=sms[0:4, 3:4], in_=sms[0:4, 3:4]), reads=["sms"], writes=["sms"])
                    f.op('dve', lambda en: en.tensor_scalar(out=ees[0:4, :], in0=ees[0:4, :], scalar1=sms[0:4, 3:4], scalar2=None, op0=ALU.mult),
                         reads=["ees", "sms"], writes=["ees"])
                    for c0 in (0, 512):
                        ps, pk = f.pget()
                        f.op('pe', lambda en, ps=ps, c0=c0: en.matmul(ps[0:1, 0:512], lhsT=self.ones[0:4, 0:1], rhs=ees[0:4, c0:c0 + 512], start=True, stop=True),
                             reads=["ones", "ees"], writes=[pk])
                        self.copy('dve', PGs[0:1, 1 + c0:1 + c0 + 512], ps[0:1, 0:512], [pk], ["PGs"])
                    S = sco
                    f.op('dve', lambda en: en.tensor_tensor(out=S[0:1, 0:256], in0=PGs[0:1, 1:1022:4], in1=PGs[0:1, 2:1023:4], op=ALU.add), reads=["PGs"], writes=["sco"])
                    f.op('dve', lambda en: en.tensor_tensor(out=S[0:1, 0:256], in0=S[0:1, 0:256], in1=PGs[0:1, 3:1024:4], op=ALU.add), reads=["PGs", "sco"], writes=["sco"])
                    f.op('dve', lambda en: en.scalar_tensor_tensor(out=S[0:1, 0:256], in0=S[0:1, 0:256], scalar=2.0, in1=PGs[0:1, 0:1021:4], op0=ALU.mult, op1=ALU.add),
                         reads=["PGs", "sco"], writes=["sco"])
                    f.op('dve', lambda en: en.tensor_tensor(out=S[0:1, 0:256], in0=S[0:1, 0:256], in1=PGs[0:1, 4:1025:4], op=ALU.add), reads=["PGs", "sco"], writes=["sco"])
                    f.op('dve', lambda en: en.tensor_tensor(out=S[0:1, 0:256], in0=S[0:1, 0:256], in1=FVs[0:1, :], op=ALU.add), reads=["FVs", "sco"], writes=["sco"])
                    f.op('dve', lambda en: en.max(out=S[0:1, 768:776], in_=S[0:1, 0:256]), reads=["sco"], writes=["sco"])
                    f.op('dve', lambda en: en.match_replace(out=S[0:1, 256:512], in_to_replace=S[0:1, 768:776], in_values=S[0:1, 0:256], imm_value=-1e30),
                         reads=["sco"], writes=["sco"])
                    f.op('dve', lambda en: en.max(out=S[0:1, 776:784], in_=S[0:1, 256:512]), reads=["sco"], writes=["sco"])
                    f.op('dve', lambda en: en.tensor_scalar(out=S[0:1, 512:768], in0=S[0:1, 0:256], scalar1=S[0:1, 782:783], scalar2=None, op0=ALU.is_ge),
                         reads=["sco"], writes=["sco"])
                    f.op('dve', lambda en: en.tensor_scalar(out=NSr[0:1, 0:256], in0=S[0:1, 512:768], scalar1=-1.0, scalar2=-NEG, op0=ALU.add, op1=ALU.mult),
                         reads=["sco"], writes=["NSr"])
                    f.op('dve', lambda en, g=g: en.tensor_copy(out=NS2[0:1, g, :], in_=NSr[0:1, 0:256:2]), reads=["NSr"], writes=["NS2"])
                    f.op('dve', lambda en: en.tensor_copy(out=NSr[0:1, 256:384], in_=NSr[0:1, 1:256:2]), reads=["NSr"], writes=["NSr"])
                    f.dma(NS2[32:33, g, :], NSr[0:1, 256:384], reads=["NSr"], writes=["NS2"])
                    ps, pk = f.pget()
                    for tl in range(8):
                        f.op('pe', lambda en, tl=tl, ps=ps: en.transpose(ps[:, tl * 4:(tl + 1) * 4], ees[0:4, tl * 128:(tl + 1) * 128], self.ids[0:4, 0:4]),
                             reads=["ees", "ids"], writes=[pk])
                    self.copy('act', pTs[:, :, :], ps[:, 0:32].rearrange("p (t h) -> p t h", t=8), [pk], ["pTs"])
                    ps, pk = f.pget()
                    for tl in range(8):
                        f.op('pe', lambda en, tl=tl, ps=ps: en.matmul(ps[0:4, 0:64], lhsT=pTs[:, tl, :], rhs=VCMs[:, tl, g, :], start=(tl == 0), stop=(tl == 7)),
                             reads=["pTs", "VCMs"], writes=[pk])
                    self.copy('dve', Ob[0:4, 0, g, :], ps[0:4, 0:64], [pk], ["Ob"])
                for br in range(2):
                    pa, pak = f.pacc(br)
                    first = [True]
                    nseg = 8 if br == 0 else 1
                    for s_ in range(nseg):
                        ntile = 16 if br == 0 else 4
                        if br == 0:
                            self.gather_KT_V(I["csk"], IDX, b, 16 * s_, 16, CT, 0)
                            self.gather_KT_V(I["csv"], IDX, b, 16 * s_, 16, CT, 0, VAs=VAs)
                        else:
                            self.gather_KT_V(None, None, b, 0, 4, CT, 0, plain=I["swk"][b * 512:(b + 1) * 512, :])
                            self.gather_KT_V(None, None, b, 0, 4, CT, 0, VAs=VAs, plain=I["swv"][b * 512:(b + 1) * 512, :])
                        for g in range(4):
                            sl = g // 2
                            ps, pk = f.pget()
                            for rt in range(ntile):
                                last = (s_ == nseg - 1 and rt == ntile - 1)
                                f.op('pe', lambda en, rt=rt, ps=ps: en.matmul(ps[:, rt * 4:(rt + 1) * 4], lhsT=CT[:, 0, sl, rt * 128:(rt + 1) * 128],
                                                                              rhs=QZS[:, b, 4 * g:4 * g + 4], start=True, stop=False),
                                     reads=[("CT", 0), "QZS"], writes=[pk])
                                if br == 0:
                                    f.op('pe', lambda en, rt=rt, ps=ps, s_=s_: en.matmul(
                                        ps[:, rt * 4:(rt + 1) * 4], lhsT=E2[0:33, :],
                                        rhs=NS2[0:33, g, 16 * s_ + rt:16 * s_ + rt + 1].to_broadcast([33, 4]), start=False, stop=False),
                                        reads=["E2", "NS2"], writes=[pk])
                                f.op('pe', lambda en, rt=rt, ps=ps, last=last: en.matmul(ps[:, rt * 4:(rt + 1) * 4], lhsT=onesb[0:1, :], rhs=C31s[0:1, 4 * g:4 * g + 4],
                                                                                       start=False, stop=(not last)), reads=["onesb", "C31s"], writes=[pk])
                                if last:
                                    f.op('pe', lambda en, rt=rt, ps=ps: en.matmul(ps[:, rt * 4:(rt + 1) * 4], lhsT=self.idb[:, :], rhs=B127[:, 4 * g:4 * g + 4],
                                                                                  start=False, stop=True), reads=["idb", "B127"], writes=[pk])
                            f.op('act', lambda en, ps=ps, ntile=ntile: en.activation(out=Ps[:, 0:ntile * 4], in_=ps[:, 0:ntile * 4], func=AF.Exp), reads=[pk], writes=["Ps"])
                            for rt in range(ntile):
                                f.op('pe', lambda en, rt=rt, fs=first[0]: en.matmul(pa[0:4, g * 65:(g + 1) * 65], lhsT=Ps[:, rt * 4:(rt + 1) * 4], rhs=VAs[:, rt, g, :],
                                                                                  start=fs, stop=False, skip_group_check=True), reads=["Ps", "VAs"], writes=[pak])
                                first[0] = False
                    for g in range(4):
                        sl = g // 2
                        ps, pk = f.pget()
                        f.op('pe', lambda en, ps=ps: en.matmul(ps[0:1, 0:4], lhsT=KTn[:, br, sl, b:b + 1], rhs=QZS[:, b, 4 * g:4 * g + 4], start=True, stop=False),
                             reads=["KTn", "QZS"], writes=[pk])
                        f.op('pe', lambda en, ps=ps: en.matmul(ps[0:1, 0:4], lhsT=onesb[0:1, 0:1], rhs=T0s[0:1, 4 * g:4 * g + 4], start=False, stop=True),
                             reads=["onesb", "T0s"], writes=[pk])
                        f.op('act', lambda en, ps=ps: en.activation(out=Pn[0:1, :], in_=ps[0:1, 0:4], func=AF.Exp), reads=[pk], writes=["Pn"])
                        f.op('pe', lambda en: en.matmul(pa[0:4, g * 65:(g + 1) * 65], lhsT=Pn[0:1, 0:4], rhs=VN[0:1, b, br, g, :], start=False, stop=True,
                                                        skip_group_check=True), reads=["Pn", "VN"], writes=[pak])
                    pav = pa[0:4, 0:260].rearrange("p (g d) -> p g d", g=4)
                    f.op('dve', lambda en, pav=pav: en.tensor_scalar(out=sms[0:4, 8:12], in0=pav[:, :, 64], scalar1=1e-30, scalar2=None, op0=ALU.max),
                         reads=[pak], writes=["sms"])
                    f.op('dve', lambda en: en.reciprocal(out=sms[0:4, 8:12], in_=sms[0:4, 8:12]), reads=["sms"], writes=["sms"])
                    f.op('dve', lambda en, pav=pav, br=br: en.tensor_tensor(out=Ob[0:4, 1 + br, :, :], in0=pav[:, :, 0:64],
                                                                            in1=sms[0:4, 8:12][:, :, None].to_broadcast([4, 4, 64]), op=ALU.mult),
                         reads=[pak, "sms"], writes=["Ob"])
                f.op('dve', lambda en: en.tensor_scalar(out=Lg[:, :], in0=Jm[:, :], scalar1=gT[:, b:b + 1], scalar2=None, op0=ALU.mult),
                     reads=["Jm", "gT"], writes=["Lg"])
                ps, pk = f.pget()
                f.op('pe', lambda en, ps=ps: en.matmul(ps[0:4, 0:12], lhsT=Lg[:, :], rhs=Rm[:, :], start=True, stop=True), reads=["Lg", "Rm"], writes=[pk])
                self.copy('dve', G4[:, :], ps[0:4, 0:12], [pk], ["G4"])
                for br in range(3):
                    f.op('dve', lambda en, br=br: en.tensor_tensor(out=tmpo[:, :, :], in0=Ob[0:4, br, :, :],
                                                                   in1=G4[:, 4 * br:4 * br + 4][:, :, None].to_broadcast([4, 4, 64]), op=ALU.mult),
                         reads=["Ob", "G4"], writes=["tmpo"])
                    if br == 0:
                        f.op('dve', lambda en: en.tensor_copy(out=Osum[:, :, 0:64], in_=tmpo[:, :, :]), reads=["tmpo"], writes=["Osum"])
                    else:
                        f.op('dve', lambda en: en.tensor_tensor(out=Osum[:, :, 0:64], in0=Osum[:, :, 0:64], in1=tmpo[:, :, :], op=ALU.add),
                             reads=["tmpo", "Osum"], writes=["Osum"])
                f.op('dve', lambda en: en.tensor_copy(out=Osum[:, :, 64:128], in_=Osum[:, :, 0:64]), reads=["Osum"], writes=["Osum"])
                for g in range(4):
                    ps, pk = f.pget()
                    f.op('pe', lambda en, ps=ps: en.transpose(ps[:, 0:4], Osum[0:4, g, :], self.ids[0:4, 0:4]), reads=["Osum", "ids"], writes=[pk])
                    self.copy('dve', OTs[0:64, 2 * g:2 * g + 2, b], ps[0:64, 0:4:2], [pk], ["OTs"])
                    self.copy('dve', OTs[64:128, 2 * g:2 * g + 2, b], ps[64:128, 1:4:2], [pk], ["OTs"])
            def cons_o(m, outs):
                for (ps, pk, c0, cn) in outs:
                    f.op('dve', lambda en, ps=ps: en.tensor_tensor(out=X[:, m, T:NT], in0=ps[:, 0:NS], in1=X[:, m, T:NT], op=ALU.add),
                         reads=[pk, ("X", m, 4)], writes=[("X", m, 4)])
            self.linear(I["w_out"], 8, D, lambda k: OTs[:, k, :], ["OTs"], NS, cons_o)

    def final(self, raw=False):
        f = self.f
        Yo = self.hT
        for ti, (t0, n) in enumerate(TILES):
            if not raw:
                for (c0, cn) in segs(n):
                    ps, pk = f.pget()
                    for c in range(8):
                        s, sk = self.scr()
                        f.op('act', lambda en, c=c, s=s: en.activation(out=s[:, 0:cn], in_=self.X[:, c, t0 + c0:t0 + c0 + cn], func=AF.Square),
                             reads=xkeys(ti, [c]), writes=[sk])
                        f.op('pe', lambda en, c=c, s=s: en.matmul(ps[:, 0:cn], lhsT=self.ones[:, :], rhs=s[:, 0:cn],
                                                                  start=(c == 0), stop=(c == 7)), reads=[sk, "ones"], writes=[pk])
                    f.op('act', lambda en: en.activation(out=self.rstd[:, c0:c0 + cn], in_=ps[:, 0:cn], func=AF.Sqrt, bias=1e-6, scale=1.0 / D),
                         reads=[pk], writes=["rstd"])
                    f.op('dve', lambda en: en.reciprocal(out=self.rstd[:, c0:c0 + cn], in_=self.rstd[:, c0:c0 + cn]),
                         reads=["rstd"], writes=["rstd"])
                for c in range(8):
                    f.op('dve', lambda en, c=c: en.scalar_tensor_tensor(
                        out=self.X[:, c, t0:t0 + n], in0=self.X[:, c, t0:t0 + n], scalar=self.VT[:, NFIN + c:NFIN + c + 1],
                        in1=self.rstd[:, 0:n], op0=ALU.mult, op1=ALU.mult),
                        reads=xkeys(ti, [c]) + ["rstd", "VT"], writes=xkeys(ti, [c]))
            for r in range(4):
                self.store_T(self.O["y_p"][t0 + r * 128:t0 + (r + 1) * 128, :],
                             lambda c, r=r, t0=t0: self.X[:, c, t0 + r * 128:t0 + (r + 1) * 128], 128, xkeys(ti))
        self.store_T(self.O["y_s"][:, :], lambda c: self.X[:, c, T:NT], NS, xkeys(3))


def build(stage=10, dbg=None, nit=8):
    nc = bass.Bass("TRN2", target_bir_lowering=False)
    es = ExitStack()
    with es:
        k = K(nc, es, stage=stage, dbg=dbg)
        for it in range(nit):
            if it:
                k.f.new_sems()
            k.bind(it)
            k.setup(first=(it == 0))
            with k.scope():
                k.hT = k.f.sb("hT", [128, 32, 516], BF16)
                k.conv_mixer()
            k.nsa()
            k.nsa_samples()
            with k.scope():
                k.hT = k.f.sb("hT1", [128, 32, 516], BF16)
                for ti, (t0, n) in enumerate(TILES):
                    k.ffn_ple(1, ti, t0, n)
            k.final()
            k.f.fence()
        k.f.finish()
        print("instructions:", k.f.n_inst, {e: k.f.cnt[e] for e in k.f.cnt}, "dmas", k.f.dcnt)
    return nc


def host_vecs(inp):
    rows = []
    for name in ["norm_mix", "norm_ffn", "norm_ple"]:
        rows.append(np.asarray(inp[name]).reshape(16, 128))
    rows.append(np.asarray(inp["norm_final"]).reshape(8, 128))
    rows.append(np.asarray(inp["conv_b1"]).reshape(16, 128))
    for name in ["conv_dwb", "conv_ln_g", "conv_ln_b", "conv_b2"]:
        rows.append(np.asarray(inp[name]).reshape(8, 128))
    rows.append(np.asarray(inp["conv_dw"]).reshape(31 * 8, 128))
    rows.append(np.asarray(inp["cmpk_pe"]).reshape(16, 128))
    rows.append(np.asarray(inp["cmpv_pe"]).reshape(16, 128))
    v = np.ascontiguousarray(np.concatenate(rows, axis=0).astype(np.float32))
    assert v.shape == (NVEC, 128)
    return v


def rel_bucket_np(dist):
    n = np.maximum(dist, 0)
    nf = np.maximum(n, 1).astype(np.float32)
    large = 16 + (np.log(nf / np.float32(16)) / np.float32(math.log(8.0)) * np.float32(16)).astype(np.int32)
    large = np.minimum(large, 31)
    return np.where(n < 16, n, large)


def onehot_table(dist, sub31):
    d = dist.reshape(-1)
    oh = np.zeros((34, d.size), np.float32)
    ok = d >= 0
    b = rel_bucket_np(d)
    idx = np.nonzero(ok)[0]
    oh[b[idx], idx] = 1.0
    oh[32, ~ok] = 1.0
    if sub31:
        oh[33, ok] = -1.0
    return oh


_CONST = {}


def host_consts():
    if _CONST:
        return _CONST
    key = np.arange(128)[:, None]
    q = np.arange(128)[None, :]
    _CONST["ohd"] = onehot_table(q - key, True)
    _CONST["oho"] = onehot_table(128 + q - key, True)
    ql = np.arange(128)[:, None]
    m = np.arange(254)[None, :]
    _CONST["ohg"] = onehot_table(ql - 16 * (m - 127) - 31, False)
    _CONST["tailm"] = np.where(q <= key, 0.0, NEG).astype(np.float32)
    qpos = (128 * np.arange(16)[None, :, None] + np.arange(128)[:, None, None])
    j = np.arange(32)[None, None, :]
    cur = qpos // 64
    valid = j * 64 <= qpos
    forced = (j == 0) | (j == cur) | (j == cur - 1)
    fv = np.where(forced, 1000.0, np.where(valid, 0.0, -1000.0)).astype(np.float32)
    _CONST["fvtab"] = np.ascontiguousarray(fv.reshape(128, 512))
    b = np.arange(32)[:, None, None]
    kt = np.arange(16)[None, :, None]
    k = np.arange(128)[None, None, :]
    _CONST["exm"] = np.ascontiguousarray((b == 2 * kt + k // 64).astype(np.float32).reshape(32, 2048))
    _CONST["ident"] = np.eye(128, dtype=np.float32)
    c = np.arange(1024)
    dist = 16384 - 16 * c - 31
    dist[1023] = -1
    _CONST["ohs"] = onehot_table(dist, False)
    fvs = np.zeros((1, 256), np.float32)
    fvs[0, 0] = 1000.0
    fvs[0, 255] = 1000.0
    _CONST["fvs"] = fvs
    e2 = np.zeros((33, 128), np.float32)
    e2[0, 0:64] = 1.0
    e2[32, 64:128] = 1.0
    _CONST["e2"] = e2
    row = np.arange(48)
    _CONST["jm"] = (row[:, None] % 4 == np.arange(4)[None, :]).astype(np.float32)
    _CONST["rm"] = (row[:, None] // 4 == np.arange(12)[None, :]).astype(np.float32)
    _CONST["oh127"] = onehot_table(128 - np.arange(128), True)
    return _CONST


def full_inputs(inp):
    g = lambda n: np.asarray(inp[n])
    c = host_consts()
    m = {
        "xp": np.ascontiguousarray(g("x_prompt").reshape(8 * T, D)),
        "xs": np.ascontiguousarray(g("x_sample").reshape(32, D)),
        "pp": np.ascontiguousarray(g("p_prompt").reshape(2, 8 * T, 256)),
        "psm": np.ascontiguousarray(g("p_sample").reshape(2, 32, 256)),
        "sconv": np.ascontiguousarray(g("state_conv").reshape(32 * 30, D)),
        "vecs": host_vecs(inp),
        "conv_w1": g("conv_w1")[0], "conv_w2": g("conv_w2")[0],
        "mlp_up": g("mlp_up"), "mlp_down": g("mlp_down"),
        "ple_proj": g("ple_proj"), "ple_gate": g("ple_gate"),
        "w_in": g("attn_w_in")[0], "w_out": g("attn_w_out")[0],
        "ck_w1": g("cmpk_w1")[0], "ck_w2": g("cmpk_w2")[0], "cv_w1": g("cmpv_w1")[0], "cv_w2": g("cmpv_w2")[0],
        "rel_table": g("rel_table"),
        "swk": np.ascontiguousarray(g("state_win_k").reshape(32 * 512, 256)),
        "swv": np.ascontiguousarray(g("state_win_v").reshape(32 * 512, 256)),
        "cck": g("cache_cmp_k").reshape(655360, 256), "ccv": g("cache_cmp_v").reshape(655360, 256),
        "csk": g("cache_slc_k").reshape(655360, 256), "csv": g("cache_slc_v").reshape(655360, 256),
        "ptab": np.ascontiguousarray(g("page_table").reshape(1, 32 * 128).astype(np.int32)),
    }
    for k in ["ident", "ohd", "oho", "ohg", "tailm", "fvtab", "exm", "ohs", "fvs", "e2", "jm", "rm", "oh127"]:
        m[k] = c[k]
    return {k: np.ascontiguousarray(v) for k, v in m.items()}


_NC_CACHE = {}


def kernel(**inp):
    m = full_inputs(inp)
    if "nc" not in _NC_CACHE:
        _NC_CACHE["nc"] = build(stage=10, nit=8)
    nc = _NC_CACHE["nc"]
    res = run_bass_kernel_spmd(nc, [m], core_ids=[0])
    R = res.results[0]
    g = lambda nm: np.asarray(R[nm]).astype(np.float32)
    outs = [g("y_p").reshape(8, T, D), g("y_s").reshape(32, 1, D)]
    for nm in ["cmp_k_p", "cmp_v_p", "slc_k_p", "slc_v_p"]:
        outs.append(g(nm).reshape(1, 8, T, 4, 64))
    for nm in ["win_k_p", "win_v_p"]:
        outs.append(g(nm).reshape(1, 8, 512, 4, 64))
    outs.append(g("conv_p").reshape(1, 8, 30, D))
    for nm in ["cmp_k_s", "cmp_v_s", "slc_k_s", "slc_v_s"]:
        outs.append(g(nm).reshape(1, 32, 1, 4, 64))
    for nm in ["win_k_s", "win_v_s"]:
        outs.append(g(nm).reshape(1, 32, 512, 4, 64))
    outs.append(g("conv_s").reshape(1, 32, 30, D))
    return tuple(outs)
```
